# Optimizing a Trainium2 kernel written in Bass

```python
import jax
import jax.numpy as jnp
from jax import lax
import numpy as np

D_MODEL = 2048
BATCH = 4
SEQ = 4096
DEPTH = 2

CTX_LEN = 256
GRID_W = 64
EPS = 1e-6
NEG = -1e30
N_MOD = 3
A_HEADS = 4
A_DK = 128
A_DV = 128
A_WIDTH = A_HEADS * A_DV
B_HEADS = 4
B_DH = 128
B_WIDTH = B_HEADS * B_DH
C_Q_HEADS = 8
C_KV_HEADS = 2
C_DH = 128
C_WIDTH = C_Q_HEADS * C_DH
MIX_WIDTH = A_WIDTH + B_WIDTH + C_WIDTH
CHUNK = 64
CONV_WIDTH = 3
ATTN_BLOCK = 128
WINDOW = 128
ROPE_BASE = 10000.0
ROPE_AXIS_DIM = C_DH // 2
PROJ_LAYOUT = (
    ('a_q', A_HEADS * A_DK), ('a_f_fwd', A_HEADS * A_DK), ('a_f_bwd', A_HEADS * A_DK),
    ('a_i', A_WIDTH), ('a_gate', A_WIDTH),
    ('b_q', B_WIDTH), ('b_k', B_WIDTH), ('b_v', B_WIDTH), ('b_o', B_WIDTH),
    ('b_gates', 4 * B_HEADS), ('b_z', B_WIDTH),
    ('c_q', C_WIDTH), ('c_k', C_KV_HEADS * C_DH), ('c_v', C_KV_HEADS * C_DH), ('c_z', C_WIDTH),
)
PROJ_WIDTH = 3 * A_HEADS * A_DK + 2 * A_WIDTH + 5 * B_WIDTH + 4 * B_HEADS + 2 * C_WIDTH + 2 * C_KV_HEADS * C_DH

kernel_name = 'hybrid_hgrn2_mlstm_swa_prefix_block'


def rms_norm(h, gain=None):
    h32 = h.astype(jnp.float32)
    y = h32 * lax.rsqrt(jnp.mean(h32 * h32, axis=-1, keepdims=True) + EPS)
    if gain is not None:
        y = y * gain.astype(jnp.float32)
    return y.astype(h.dtype)


def modulate(h, gain, shift, scale):
    return rms_norm(h, gain) * (1.0 + scale) + shift


def split_proj(p):
    parts, start = {}, 0
    for name, width in PROJ_LAYOUT:
        parts[name] = p[..., start:start + width]
        start += width
    return parts


def to_heads(a, n_heads):
    return a.reshape(a.shape[:-1] + (n_heads, a.shape[-1] // n_heads))


def join_dir(a_ctx, a_lat, reverse):
    if reverse:
        a_ctx, a_lat = a_ctx[:, ::-1], a_lat[:, ::-1]
    return jnp.concatenate([a_ctx, a_lat], axis=1)


def split_dir(y, n_ctx, reverse):
    y_ctx, y_lat = y[:, :n_ctx], y[:, n_ctx:]
    if reverse:
        y_ctx, y_lat = y_ctx[:, ::-1], y_lat[:, ::-1]
    return y_ctx, y_lat


def to_chunks(a):
    b, t = a.shape[:2]
    a = a.reshape((b, t // CHUNK, CHUNK) + a.shape[2:])
    return jnp.moveaxis(a, (1, 3), (0, 2))


def from_chunks(o):
    o = jnp.moveaxis(o, (0, 2), (1, 3))
    return o.reshape((o.shape[0], o.shape[1] * o.shape[2]) + o.shape[3:])


def hgrn2_scan(q, k, v, log_f):
    b, _, h, dk = q.shape
    dv = v.shape[-1]
    causal = jnp.tril(jnp.ones((CHUNK, CHUNK), bool))

    def step(state, inp):
        qc, kc, vc, fc = inp
        cum = jnp.cumsum(fc, axis=2)
        diff = cum[:, :, :, None, :] - cum[:, :, None, :, :]
        decay = jnp.exp(jnp.where(causal[:, :, None], diff, NEG))
        scores = jnp.einsum('bhtd,bhtsd,bhsd->bhts', qc, decay, kc)
        out = (jnp.einsum('bhtd,bhdv->bhtv', qc * jnp.exp(cum), state)
               + jnp.einsum('bhts,bhsv->bhtv', scores, vc))
        cum_end = cum[:, :, -1]
        state = (jnp.exp(cum_end)[..., None] * state
                 + jnp.einsum('bhsd,bhsv->bhdv', kc * jnp.exp(cum_end[:, :, None] - cum), vc))
        return state, out

    s0 = jnp.zeros((b, h, dk, dv), jnp.float32)
    _, out = lax.scan(step, s0, tuple(to_chunks(a) for a in (q, k, v, log_f)))
    return from_chunks(out)


def mlstm_scan(q, k, v, log_i, log_f):
    b, _, h, dk = q.shape
    dv = v.shape[-1]
    causal = jnp.tril(jnp.ones((CHUNK, CHUNK), bool))

    def step(carry, inp):
        c_mat, n_vec, m = carry
        qc, kc, vc, ic, fc = inp
        cum = jnp.cumsum(fc, axis=-1)
        log_d = jnp.where(causal, cum[..., :, None] - cum[..., None, :] + ic[..., None, :], NEG)
        inter = cum + m[..., None]
        m_t = jnp.maximum(inter, jnp.max(log_d, axis=-1))
        w_intra = jnp.exp(log_d - m_t[..., None])
        w_inter = jnp.exp(inter - m_t)
        s = jnp.einsum('bhtd,bhsd->bhts', qc, kc) * w_intra
        num = (w_inter[..., None] * jnp.einsum('bhtd,bhdv->bhtv', qc, c_mat)
               + jnp.einsum('bhts,bhsv->bhtv', s, vc))
        den = w_inter * jnp.einsum('bhtd,bhd->bht', qc, n_vec) + jnp.sum(s, axis=-1)
        h_out = num / jnp.maximum(jnp.abs(den), jnp.exp(-m_t))[..., None]
        log_w = cum[..., -1:] - cum + ic
        m_new = jnp.maximum(cum[..., -1] + m, jnp.max(log_w, axis=-1))
        w = jnp.exp(log_w - m_new[..., None])
        carry_scale = jnp.exp(cum[..., -1] + m - m_new)
        c_mat = carry_scale[..., None, None] * c_mat + jnp.einsum('bhs,bhsd,bhsv->bhdv', w, kc, vc)
        n_vec = carry_scale[..., None] * n_vec + jnp.einsum('bhs,bhsd->bhd', w, kc)
        return (c_mat, n_vec, m_new), h_out

    init = (jnp.zeros((b, h, dk, dv), jnp.float32), jnp.zeros((b, h, dk), jnp.float32),
            jnp.zeros((b, h), jnp.float32))
    _, out = lax.scan(step, init, tuple(to_chunks(a) for a in (q, k, v, log_i, log_f)))
    return from_chunks(out)


def short_conv(a, w):
    pad = CONV_WIDTH // 2
    t = a.shape[1]
    ap = jnp.pad(a, ((0, 0), (pad, pad), (0, 0)))
    acc = w[0] * ap[:, 0:t]
    for j in range(1, CONV_WIDTH):
        acc = acc + w[j] * ap[:, j:j + t]
    return acc


def rotate_pairs(x, cos, sin):
    x1, x2 = jnp.split(x, 2, axis=-1)
    cos = cos[None, :, None, :]
    sin = sin[None, :, None, :]
    return jnp.concatenate([x1 * cos - x2 * sin, x2 * cos + x1 * sin], axis=-1)


def axial_rope(x, rope):
    cos_r, sin_r, cos_c, sin_c = rope
    x_row, x_col = jnp.split(x.astype(jnp.float32), 2, axis=-1)
    y = jnp.concatenate([rotate_pairs(x_row, cos_r, sin_r), rotate_pairs(x_col, cos_c, sin_c)], axis=-1)
    return y.astype(x.dtype)


def window_attention(q, k, v, k_ctx, v_ctx, sink):
    b, n, hq, dh = q.shape
    hkv = k.shape[2]
    g = hq // hkv
    nb = n // ATTN_BLOCK
    span = 3 * ATTN_BLOCK
    n_ctx = k_ctx.shape[1]
    qb = q.reshape(b, nb, ATTN_BLOCK, hkv, g, dh)

    def band_windows(a):
        ap = jnp.pad(a.reshape(b, nb, ATTN_BLOCK, hkv, dh), ((0, 0), (1, 1), (0, 0), (0, 0), (0, 0)))
        return jnp.concatenate([ap[:, :-2], ap[:, 1:-1], ap[:, 2:]], axis=2)

    kw, vw = band_windows(k), band_windows(v)
    qi = jnp.arange(ATTN_BLOCK)
    kj = jnp.arange(span)
    blk = jnp.arange(nb)
    band = jnp.abs(kj[None, :] - qi[:, None] - ATTN_BLOCK) <= WINDOW
    kpos = (blk[:, None] - 1) * ATTN_BLOCK + kj[None, :]
    mask = band[None] & ((kpos >= 0) & (kpos < n))[:, None, :]
    scale = dh ** -0.5
    s_loc = jnp.einsum('bnqkgd,bnskd->bnkgqs', qb, kw, preferred_element_type=jnp.float32) * scale
    s_loc = jnp.where(mask[None, :, None, None], s_loc, NEG)
    s_ctx = jnp.einsum('bnqkgd,bckd->bnkgqc', qb, k_ctx, preferred_element_type=jnp.float32) * scale
    s_sink = jnp.broadcast_to(sink.astype(jnp.float32).reshape(1, 1, hkv, g, 1, 1), s_loc.shape[:-1] + (1,))
    p = jax.nn.softmax(jnp.concatenate([s_loc, s_ctx, s_sink], axis=-1), axis=-1).astype(v.dtype)
    out = (jnp.einsum('bnkgqs,bnskd->bnqkgd', p[..., :span], vw)
           + jnp.einsum('bnkgqc,bckd->bnqkgd', p[..., span:span + n_ctx], v_ctx))
    return out.reshape(b, n, hq * dh)


def context_attention(q, k, v, sink):
    b, n_ctx, hq, dh = q.shape
    hkv = k.shape[2]
    g = hq // hkv
    qg = q.reshape(b, n_ctx, hkv, g, dh)
    s = jnp.einsum('bqkgd,bskd->bkgqs', qg, k, preferred_element_type=jnp.float32) * dh ** -0.5
    s_sink = jnp.broadcast_to(sink.astype(jnp.float32).reshape(1, hkv, g, 1, 1), s.shape[:-1] + (1,))
    p = jax.nn.softmax(jnp.concatenate([s, s_sink], axis=-1), axis=-1).astype(v.dtype)
    out = jnp.einsum('bkgqs,bskd->bqkgd', p[..., :n_ctx], v)
    return out.reshape(b, n_ctx, hq * dh)


def hgrn2_branch(pc, pl, lower_bound, with_ctx):
    n_ctx = pc['a_q'].shape[1]
    ys_ctx, ys_lat = [], []
    for d, (f_name, reverse) in enumerate((('a_f_fwd', False), ('a_f_bwd', True))):
        q = to_heads(jax.nn.silu(join_dir(pc['a_q'], pl['a_q'], reverse).astype(jnp.float32)), A_HEADS)
        zf = to_heads(join_dir(pc[f_name], pl[f_name], reverse).astype(jnp.float32), A_HEADS)
        i_in = to_heads(join_dir(pc['a_i'], pl['a_i'], reverse).astype(jnp.float32), A_HEADS)
        lb = lower_bound[d]
        log_f = jnp.log(lb + (1.0 - lb) * jax.nn.sigmoid(zf))
        k = (1.0 - lb) * jax.nn.sigmoid(-zf)
        y_ctx, y_lat = split_dir(hgrn2_scan(q, k, i_in, log_f), n_ctx, reverse)
        ys_ctx.append(y_ctx)
        ys_lat.append(y_lat)

    def readout(y, gate):
        y = rms_norm(y).reshape(y.shape[:2] + (A_WIDTH,))
        return (y * jax.nn.silu(gate.astype(jnp.float32))).astype(gate.dtype)

    lat = readout(ys_lat[0] + ys_lat[1], pl['a_gate'])
    ctx = readout(ys_ctx[0] + ys_ctx[1], pc['a_gate']) if with_ctx else None
    return ctx, lat


def mlstm_branch(pc, pl, conv_w, gate_bias, with_ctx):
    n_ctx = pc['b_q'].shape[1]

    def qk_path(p):
        a = jax.nn.silu(short_conv(jnp.concatenate([p['b_q'], p['b_k']], axis=-1), conv_w))
        return a[..., :B_WIDTH], a[..., B_WIDTH:]

    q_c, k_c = qk_path(pc)
    q_l, k_l = qk_path(pl)
    g_c = pc['b_gates'] + gate_bias
    g_l = pl['b_gates'] + gate_bias
    ys_ctx, ys_lat = [], []
    for d, reverse in enumerate((False, True)):
        i_sl = slice((2 * d) * B_HEADS, (2 * d + 1) * B_HEADS)
        f_sl = slice((2 * d + 1) * B_HEADS, (2 * d + 2) * B_HEADS)
        q = to_heads(join_dir(q_c, q_l, reverse).astype(jnp.float32), B_HEADS) * (B_DH ** -0.5)
        k = to_heads(join_dir(k_c, k_l, reverse).astype(jnp.float32), B_HEADS)
        v = to_heads(join_dir(pc['b_v'], pl['b_v'], reverse).astype(jnp.float32), B_HEADS)
        log_i = join_dir(g_c[..., i_sl], g_l[..., i_sl], reverse).astype(jnp.float32)
        log_f = jax.nn.log_sigmoid(join_dir(g_c[..., f_sl], g_l[..., f_sl], reverse).astype(jnp.float32))
        y_ctx, y_lat = split_dir(mlstm_scan(q, k, v, log_i, log_f), n_ctx, reverse)
        ys_ctx.append(y_ctx)
        ys_lat.append(y_lat)

    def readout(y, p):
        h = jax.nn.sigmoid(to_heads(p['b_o'], B_HEADS).astype(jnp.float32)) * y
        h = rms_norm(h).reshape(h.shape[:2] + (B_WIDTH,))
        return (h * jax.nn.silu(p['b_z'].astype(jnp.float32))).astype(p['b_z'].dtype)

    lat = readout(ys_lat[0] + ys_lat[1], pl)
    ctx = readout(ys_ctx[0] + ys_ctx[1], pc) if with_ctx else None
    return ctx, lat


def attention_branch(pc, pl, sink, rope, with_ctx):
    q = axial_rope(to_heads(pl['c_q'], C_Q_HEADS), rope)
    k = axial_rope(to_heads(pl['c_k'], C_KV_HEADS), rope)
    v = to_heads(pl['c_v'], C_KV_HEADS)
    k_ctx = to_heads(pc['c_k'], C_KV_HEADS)
    v_ctx = to_heads(pc['c_v'], C_KV_HEADS)
    lat = window_attention(q, k, v, k_ctx, v_ctx, sink) * jax.nn.silu(pl['c_z'])
    ctx = None
    if with_ctx:
        ctx = context_attention(to_heads(pc['c_q'], C_Q_HEADS), k_ctx, v_ctx, sink) * jax.nn.silu(pc['c_z'])
    return ctx, lat


def setup_inputs(seed: int = 0) -> dict:
    key = jax.random.key(seed)
    ks = jax.random.split(key, 16)
    nrm = jax.random.normal
    x = nrm(ks[0], (BATCH, SEQ, D_MODEL), jnp.float32)
    c = nrm(ks[1], (BATCH, D_MODEL), jnp.float32)
    ctx = nrm(ks[2], (BATCH, CTX_LEN, D_MODEL), jnp.float32)
    c_ctx = nrm(ks[3], (D_MODEL,), jnp.float32)
    w_mod = nrm(ks[4], (DEPTH, D_MODEL, N_MOD * D_MODEL), jnp.float32) * D_MODEL ** -0.5
    b_mod = 0.02 * nrm(ks[5], (DEPTH, N_MOD * D_MODEL), jnp.float32)
    g_pre = 1.0 + 0.05 * nrm(ks[6], (DEPTH, D_MODEL), jnp.float32)
    g_post = 1.0 + 0.05 * nrm(ks[7], (DEPTH, D_MODEL), jnp.float32)
    w_in = nrm(ks[8], (DEPTH, D_MODEL, PROJ_WIDTH), jnp.float32) * D_MODEL ** -0.5
    hgrn_lb_logits = nrm(ks[9], (DEPTH, 2 * A_HEADS * A_DK), jnp.float32)
    mlstm_conv_w = nrm(ks[10], (DEPTH, CONV_WIDTH, 2 * B_WIDTH), jnp.float32) * CONV_WIDTH ** -0.5
    i_bias = 0.1 * nrm(ks[11], (DEPTH, 2, 1, B_HEADS), jnp.float32)
    f_bias = 3.0 + 3.0 * jax.random.uniform(ks[12], (DEPTH, 2, 1, B_HEADS), jnp.float32)
    mlstm_gate_bias = jnp.concatenate([i_bias, f_bias], axis=2).reshape(DEPTH, 4 * B_HEADS)
    attn_sink = nrm(ks[13], (DEPTH, C_Q_HEADS), jnp.float32)
    w_out = nrm(ks[14], (DEPTH, MIX_WIDTH, D_MODEL), jnp.float32) * MIX_WIDTH ** -0.5
    return {'x': x, 'c': c, 'ctx': ctx, 'c_ctx': c_ctx, 'w_mod': w_mod, 'b_mod': b_mod,
            'g_pre': g_pre, 'g_post': g_post, 'w_in': w_in, 'hgrn_lb_logits': hgrn_lb_logits,
            'mlstm_conv_w': mlstm_conv_w, 'mlstm_gate_bias': mlstm_gate_bias,
            'attn_sink': attn_sink, 'w_out': w_out}


def reference(x, c, ctx, c_ctx, w_mod, b_mod, g_pre, g_post, w_in, hgrn_lb_logits,
              mlstm_conv_w, mlstm_gate_bias, attn_sink, w_out):
    n_lat = x.shape[1]
    rows = n_lat // GRID_W
    row = jnp.repeat(jnp.arange(rows, dtype=jnp.float32), GRID_W)
    col = jnp.tile(jnp.arange(GRID_W, dtype=jnp.float32), rows)
    inv_freq = ROPE_BASE ** (-jnp.arange(0, ROPE_AXIS_DIM, 2, dtype=jnp.float32) / ROPE_AXIS_DIM)
    ang_r = row[:, None] * inv_freq[None, :]
    ang_c = col[:, None] * inv_freq[None, :]
    rope = (jnp.cos(ang_r), jnp.sin(ang_r), jnp.cos(ang_c), jnp.sin(ang_c))
    lb_w = jax.nn.softmax(hgrn_lb_logits.astype(jnp.float32), axis=0)
    lower_bounds = jnp.cumsum(lb_w, axis=0) - lb_w[0]
    h_ctx = ctx
    for layer in range(DEPTH):
        with_ctx = layer < DEPTH - 1
        mod_lat = (jax.nn.silu(c) @ w_mod[layer] + b_mod[layer])[:, None, :]
        mod_ctx = (jax.nn.silu(c_ctx) @ w_mod[layer] + b_mod[layer])[None, None, :]
        shift_l, scale_l, gate_l = jnp.split(mod_lat, N_MOD, axis=-1)
        shift_c, scale_c, gate_c = jnp.split(mod_ctx, N_MOD, axis=-1)
        pl = split_proj(modulate(x, g_pre[layer], shift_l, scale_l) @ w_in[layer])
        pc = split_proj(modulate(h_ctx, g_pre[layer], shift_c, scale_c) @ w_in[layer])
        a_c, a_l = hgrn2_branch(pc, pl, lower_bounds[layer].reshape(2, A_HEADS, A_DK), with_ctx)
        b_c, b_l = mlstm_branch(pc, pl, mlstm_conv_w[layer], mlstm_gate_bias[layer], with_ctx)
        c_c, c_l = attention_branch(pc, pl, attn_sink[layer], rope, with_ctx)
        y_lat = jnp.concatenate([a_l, b_l, c_l], axis=-1) @ w_out[layer]
        x = x + gate_l * rms_norm(y_lat, g_post[layer])
        if with_ctx:
            y_ctx = jnp.concatenate([a_c, b_c, c_c], axis=-1) @ w_out[layer]
            h_ctx = h_ctx + gate_c * rms_norm(y_ctx, g_post[layer])
    return x
```

```python
import numpy as np
from contextlib import ExitStack
import concourse.bass as bass
import concourse.mybir as mybir
from concourse.bass_utils import run_bass_kernel_spmd

F32 = mybir.dt.float32
BF16 = mybir.dt.bfloat16
AF = mybir.ActivationFunctionType
ALU = mybir.AluOpType
AX = mybir.AxisListType

D = 2048
TC = 256
TL = 4096
T = TC + TL
NT = T // 128
KC = D // 128
DEPTH = 2
NA, NB, NQ, NKV = 4, 4, 8, 2
EPS = 1e-6
NFM = 31
NTM = 8
TMW = 3840
O_AI, O_AG, O_BV, O_BO, O_BZ, O_CZ, O_CV = 0, 512, 1024, 1536, 2048, 2560, 3584
C_AQ, C_AFF, C_AFB, C_BQ, C_BK, C_CQ, C_CK, C_G = 0, 4, 8, 12, 16, 20, 28, 30
GROUPS = [(0, 256)] + [(256 + 1024 * i, 1024) for i in range(4)]
NEGM = -30000.0


class Prog:
    def __init__(self, nc, es):
        self.nc = nc
        self.es = es
        self.eng = {'pe': nc.tensor, 'act': nc.scalar, 'dve': nc.vector, 'pool': nc.gpsimd, 'sp': nc.sync}
        self.esem = {e: es.enter_context(nc.semaphore("s_" + e)) for e in ('pe', 'act', 'dve', 'pool')}
        self.cnt = {e: 0 for e in self.esem}
        self.waited = {e: {} for e in self.eng}
        self.last_w = {}
        self.readers = {}
        self.dsem = {}
        self.dcnt = {}
        self.sems = {}
        self.nwait = 0
        self.ninst = 0

    def sbuf(self, es, name, shape, dt):
        self.uid = getattr(self, 'uid', 0) + 1
        return es.enter_context(self.nc.sbuf_tensor("%s_%d" % (name, self.uid), shape, dt))

    def psum(self, es, name, shape, dt):
        self.uid = getattr(self, 'uid', 0) + 1
        return es.enter_context(self.nc.psum_tensor("%s_%d" % (name, self.uid), shape, dt))

    def _wait(self, e, toks):
        need = {}
        for t in toks:
            if t is None:
                continue
            s, v = t
            if v > need.get(id(s), (s, 0))[1]:
                need[id(s)] = (s, v)
        for k, (s, v) in need.items():
            if self.waited[e].get(k, 0) < v:
                self.eng[e].wait_ge(s, v)
                self.waited[e][k] = v
                self.nwait += 1

    def _deps(self, e, reads, writes, own=None):
        toks = []
        for r in reads:
            toks.append(self.last_w.get(r))
        for w in writes:
            lw = self.last_w.get(w)
            if lw is not None and not (own is not None and lw[0] is own):
                if not (e == 'pe' and lw[0] is self.esem[e]):
                    toks.append(lw)
            for t in self.readers.get(w, ()):
                if e == 'pe' and t[0] is self.esem[e]:
                    continue
                toks.append(t)
        return toks

    def _commit(self, tok, reads, writes):
        for r in reads:
            self.readers.setdefault(r, []).append(tok)
        for w in writes:
            self.last_w[w] = tok
            self.readers[w] = []

    def op(self, e, fn, reads=(), writes=()):
        self._wait(e, self._deps(e, reads, writes))
        inst = fn(self.eng[e])
        self.cnt[e] += 1
        self.ninst += 1
        inst.then_inc(self.esem[e], 1)
        tok = (self.esem[e], self.cnt[e])
        self._commit(tok, reads, writes)
        return tok

    def dma(self, q, slot, out, in_, reads=(), writes=()):
        if slot not in self.dsem:
            fl = getattr(self, 'free_sems_' + q, None)
            if fl:
                self.dsem[slot] = fl.pop()
                self.dq[slot] = q
            else:
                if not hasattr(self, 'dq'):
                    self.dq = {}
                self.dq[slot] = q
                self.nsem = getattr(self, 'nsem', 0) + 1
                sm = self.es.enter_context(self.nc.semaphore("d_%d" % self.nsem))
                self.dsem[slot] = sm
                self.dcnt[id(sm)] = 0
        s = self.dsem[slot]
        self._wait(q, self._deps(q, reads, writes, own=s))
        self.dcnt[id(s)] += 16
        self.ninst += 1
        self.eng[q].dma_start(out=out, in_=in_).then_inc(s, 16)
        tok = (s, self.dcnt[id(s)])
        self._commit(tok, reads, writes)
        return tok

    def barrier(self):
        toks = [(self.esem[e], self.cnt[e]) for e in self.esem if self.cnt[e] > 0]
        toks += [(self.dsem[s], self.dcnt[id(self.dsem[s])]) for s in self.dsem]
        for e in self.eng:
            self._wait(e, toks)
        self.last_w = {}
        self.readers = {}
        for slot, sm in self.dsem.items():
            q = self.dq[slot]
            if not hasattr(self, 'free_sems_' + q):
                setattr(self, 'free_sems_' + q, [])
            getattr(self, 'free_sems_' + q).append(sm)
        self.dsem = {}
        self.dq = {}
        if not hasattr(self, 'marks'):
            self.marks = []
        self.marks.append(dict(self.cnt))

    def finish(self):
        toks = [(self.esem[e], self.cnt[e]) for e in self.esem if self.cnt[e] > 0]
        toks += [(self.dsem[s], self.dcnt[id(self.dsem[s])]) for s in self.dsem]
        self._wait('sp', toks)


class Rot:
    def __init__(self, name, n):
        self.name, self.n, self.i = name, n, 0

    def next(self):
        k = self.i % self.n
        self.i += 1
        return k, "%s%d" % (self.name, k)


def dram(nc, name, shape, dt, kind="Internal"):
    return nc.dram_tensor(name, list(shape), dt, kind=kind).ap()


def _proj_offsets():
    off, start = {}, 0
    for name, width in (('a_q', 512), ('a_f_fwd', 512), ('a_f_bwd', 512), ('a_i', 512), ('a_gate', 512),
                        ('b_q', 512), ('b_k', 512), ('b_v', 512), ('b_o', 512), ('b_gates', 16), ('b_z', 512),
                        ('c_q', 1024), ('c_k', 256), ('c_v', 256), ('c_z', 1024)):
        off[name] = start
        start += width
    return off


def host_consts():
    c = {}
    c['ident'] = np.eye(128, dtype=np.float32)
    s = np.arange(128)[:, None]
    t = np.arange(128)[None, :]
    c['maskf'] = (s <= t).astype(np.float32)
    c['maskb'] = (s >= t).astype(np.float32)
    qi = np.arange(128)[:, None]
    kj = np.arange(384)[None, :]
    c['band'] = np.where(np.abs(kj - qi - 128) <= 128, 0.0, NEGM).astype(np.float32)
    perm = np.zeros((128, 128), np.float32)
    for dp in range(128):
        blk = dp // 32
        src = dp + 32 if blk % 2 == 0 else dp - 32
        perm[src, dp] = 1.0
    c['perm'] = perm
    inv_freq = (np.float32(10000.0) ** (-np.arange(0, 64, 2, dtype=np.float32) / np.float32(64))).astype(np.float32)
    pos = np.arange(TL)
    row = (pos // 64).astype(np.float32)
    col = (pos % 64).astype(np.float32)
    ang_r = (row[:, None] * inv_freq[None, :]).astype(np.float32)
    ang_c = (col[:, None] * inv_freq[None, :]).astype(np.float32)
    cos = np.zeros((128, TL), np.float32)
    sin = np.zeros((128, TL), np.float32)
    for d in range(128):
        ang = ang_r if d < 64 else ang_c
        j = d % 32
        cos[d] = np.cos(ang[:, j])
        sg = -1.0 if (d // 32) % 2 == 0 else 1.0
        sin[d] = sg * np.sin(ang[:, j])
    c['ropec'] = cos
    c['ropes'] = sin
    rm = np.ones((4, T), np.float32)
    rm[:, ::128] = 0.0
    c['rmask'] = rm
    return c


def host_prep(x, c, ctx, c_ctx, w_mod, b_mod, g_pre, g_post, w_in, hgrn_lb_logits,
              mlstm_conv_w, mlstm_gate_bias, attn_sink, w_out):
    f = np.float32
    off = _proj_offsets()
    shared = {}
    shared['wmod'] = np.ascontiguousarray(
        w_mod.reshape(DEPTH, KC, 128, 12, 512).transpose(0, 3, 2, 1, 4)).astype(f, copy=False)
    shared['bm_fm'] = np.ascontiguousarray(b_mod[:, :4096].reshape(DEPTH, 32, 128).transpose(2, 0, 1))
    shared['bm_g'] = np.ascontiguousarray(np.broadcast_to(b_mod[None, :, 4096:], (128, DEPTH, 2048)))
    shared['gpre'] = np.ascontiguousarray(g_pre.reshape(DEPTH, KC, 128).transpose(2, 0, 1))
    shared['gpost'] = np.ascontiguousarray(np.broadcast_to(g_post[None], (128, DEPTH, 2048)))
    fm_cols = []
    for nm, nh in (('a_q', 4), ('a_f_fwd', 4), ('a_f_bwd', 4), ('b_q', 4), ('b_k', 4), ('c_q', 8), ('c_k', 2)):
        for h in range(nh):
            fm_cols.append(np.arange(off[nm] + h * 128, off[nm] + (h + 1) * 128))
    fm_cols = np.concatenate(fm_cols)
    gcols = np.arange(off['b_gates'], off['b_gates'] + 16)
    wfm = np.zeros((DEPTH, D, NFM * 128), f)
    wfm[:, :, :30 * 128] = w_in[:, :, fm_cols]
    wfm[:, :, 30 * 128:30 * 128 + 16] = w_in[:, :, gcols]
    shared['wfm'] = np.ascontiguousarray(wfm.reshape(DEPTH, KC, 128, NFM, 128).transpose(0, 3, 2, 1, 4)).reshape(DEPTH, NFM, 128, KC * 128)
    tm_cols = np.concatenate([np.arange(off[nm], off[nm] + w) for nm, w in
                              (('a_i', 512), ('a_gate', 512), ('b_v', 512), ('b_o', 512), ('b_z', 512),
                               ('c_z', 1024), ('c_v', 256))])
    wtm = np.zeros((DEPTH, D, NTM * 512), f)
    wtm[:, :, :TMW] = w_in[:, :, tm_cols]
    shared['wtm'] = np.ascontiguousarray(wtm.reshape(DEPTH, KC, 128, NTM, 512).transpose(0, 3, 2, 1, 4)).reshape(DEPTH, NTM, 128, KC * 512)
    shared['wo'] = np.ascontiguousarray(w_out.reshape(DEPTH, KC, 128, 4, 512).transpose(0, 3, 2, 1, 4)).reshape(DEPTH, 4, 128, KC * 512)
    shared['lbl'] = np.ascontiguousarray(hgrn_lb_logits.reshape(DEPTH, 8, 128).transpose(2, 0, 1))
    shared['convw'] = np.ascontiguousarray(mlstm_conv_w.reshape(DEPTH, 3, 8, 128).transpose(3, 0, 1, 2))
    shared['gbias'] = np.ascontiguousarray(mlstm_gate_bias.reshape(DEPTH, 4, 4).transpose(2, 0, 1))
    shared['sink'] = np.ascontiguousarray(np.broadcast_to(attn_sink[None], (128, DEPTH, 8)))
    shared.update(host_consts())
    per = []
    for b in range(x.shape[0]):
        d = dict(shared)
        d['xin'] = np.ascontiguousarray(np.concatenate([ctx[b], x[b]], axis=0))
        cT = np.stack([c[b].reshape(KC, 128).T, c_ctx.reshape(KC, 128).T], axis=-1)
        d['cT'] = np.ascontiguousarray(cT.astype(f))
        per.append(d)
    return per


class Ctx:
    pass


def declare(nc, dbg=False):
    g = Ctx()
    I = lambda n, s, dt=F32: dram(nc, n, s, dt, "ExternalInput")
    g.xin = I("xin", [T, D])
    g.cT = I("cT", [128, KC, 2])
    g.wmod = I("wmod", [DEPTH, 12, 128, KC, 512])
    g.bm_fm = I("bm_fm", [128, DEPTH, 32])
    g.bm_g = I("bm_g", [128, DEPTH, 2048])
    g.gpre = I("gpre", [128, DEPTH, KC])
    g.gpost = I("gpost", [128, DEPTH, 2048])
    g.wfm = I("wfm", [DEPTH, NFM, 128, KC * 128])
    g.wtm = I("wtm", [DEPTH, NTM, 128, KC * 512])
    g.wo = I("wo", [DEPTH, 4, 128, KC * 512])
    g.lbl = I("lbl", [128, DEPTH, 8])
    g.convw = I("convw", [128, DEPTH, 3, 8])
    g.gbias = I("gbias", [4, DEPTH, 4])
    g.sink = I("sink", [128, DEPTH, 8])
    g.ident = I("ident", [128, 128])
    g.maskf = I("maskf", [128, 128])
    g.maskb = I("maskb", [128, 128])
    g.band = I("band", [128, 384])
    g.perm = I("perm", [128, 128])
    g.ropec = I("ropec", [128, TL])
    g.ropes = I("ropes", [128, TL])
    g.rmask = I("rmask", [4, T])
    g.out = dram(nc, "out", [TL, D], F32, "ExternalOutput")
    dbgset = set(dbg) if dbg else set()
    S = lambda n, s, dt: dram(nc, n, s, dt, "ExternalOutput" if n in dbgset else "Internal")
    g.xs = S("xs", [T, D], F32)
    g.wfm_b = S("wfm_b", [DEPTH, NFM, 128, KC * 128], BF16)
    g.wtm_b = S("wtm_b", [DEPTH, NTM, 128, KC * 512], BF16)
    g.wo_b = S("wo_b", [DEPTH, 4, 128, KC * 512], BF16)
    g.PT = S("PT", [30 * 128, T], BF16)
    g.PG = S("PG", [4, 4, T], F32)
    g.PM = S("PM", [T, TMW], BF16)
    g.mix = S("mix", [T, 2048], BF16)
    g.yA = S("yA", [2, T, 512], F32)
    g.yB = S("yB", [2, T, 512], F32)
    g.ggd = S("ggd", [DEPTH, 2, 128, 2048], F32)
    g.gsd = S("gsd", [DEPTH, 2, 128, KC, 2], F32)
    g.carry = S("carry", [1, 512], F32)
    return g


def wcast_jobs(g, layers):
    jobs = []
    for l in layers:
        for c in range(NFM):
            jobs.append((g.wfm[l, c, :, :], g.wfm_b[l, c, :, :]))
        for c in range(NTM):
            for q in range(4):
                jobs.append((g.wtm[l, c, :, q * 2048:(q + 1) * 2048], g.wtm_b[l, c, :, q * 2048:(q + 1) * 2048]))
        for c in range(4):
            for q in range(4):
                jobs.append((g.wo[l, c, :, q * 2048:(q + 1) * 2048], g.wo_b[l, c, :, q * 2048:(q + 1) * 2048]))
    return jobs


class PoolCaster:
    def __init__(self, P, g, es, layers):
        self.P = P
        self.jobs = wcast_jobs(g, layers)
        self.tin = P.sbuf(es, "pc_in", [128, 2, 2048], F32)
        self.tout = P.sbuf(es, "pc_out", [128, 2, 2048], BF16)
        self.rin, self.rout = Rot("pci", 2), Rot("pco", 2)

    def emit(self, n):
        P = self.P
        for _ in range(n):
            if not self.jobs:
                return
            src, dst = self.jobs.pop(0)
            ki, ni = self.rin.next()
            ko, no = self.rout.next()
            P.dma('pool', ni, self.tin[:, ki, :], src, writes=[ni])
            P.op('pool', lambda en: en.tensor_copy(out=self.tout[:, ko, :], in_=self.tin[:, ki, :]), reads=[ni],
                 writes=[no])
            P.dma('pool', no, dst, self.tout[:, ko, :], reads=[no], writes=[])


def phase_wcast(P, g, layers=(0, 1)):
    with ExitStack() as es:
        tin = P.sbuf(es, "wc_in", [128, 3, 2048], F32)
        tout = P.sbuf(es, "wc_out", [128, 3, 2048], BF16)
        rin, rout = Rot("wci", 3), Rot("wco", 3)
        engs = ['pool', 'act', 'dve']
        jobs = []
        for l in layers:
            for c in range(NFM):
                jobs.append((g.wfm[l, c, :, :], g.wfm_b[l, c, :, :]))
            for c in range(NTM):
                for q in range(4):
                    jobs.append((g.wtm[l, c, :, q * 2048:(q + 1) * 2048], g.wtm_b[l, c, :, q * 2048:(q + 1) * 2048]))
            for c in range(4):
                for q in range(4):
                    jobs.append((g.wo[l, c, :, q * 2048:(q + 1) * 2048], g.wo_b[l, c, :, q * 2048:(q + 1) * 2048]))
        for i, (src, dst) in enumerate(jobs):
            ki, ni = rin.next()
            ko, no = rout.next()
            P.dma('sp', ni, tin[:, ki, :], src, writes=[ni])
            e = engs[i % 3]
            if e == 'act':
                P.op(e, lambda en: en.activation(out=tout[:, ko, :], in_=tin[:, ki, :], func=AF.Copy),
                     reads=[ni], writes=[no])
            else:
                P.op(e, lambda en: en.tensor_copy(out=tout[:, ko, :], in_=tin[:, ki, :]), reads=[ni], writes=[no])
            P.dma('sp', no, dst, tout[:, ko, :], reads=[no], writes=[])
        P.barrier()


def phase_mod(P, g, K):
    with ExitStack() as es:
        cT = P.sbuf(es, "m_cT", [128, KC, 2], F32)
        sc = P.sbuf(es, "m_sc", [128, KC, 2], F32)
        scb = P.sbuf(es, "m_scb", [128, KC, 2, 128], F32)
        wm = P.sbuf(es, "m_wm", [128, 2, KC, 512], F32)
        bmf = P.sbuf(es, "m_bmf", [128, DEPTH, 32], F32)
        gpr = P.sbuf(es, "m_gpr", [128, DEPTH, KC], F32)
        modfm = P.sbuf(es, "m_modfm", [128, 32, 2], F32)
        gssh = P.sbuf(es, "m_gssh", [128, 2, KC, 2], F32)
        bmg = P.sbuf(es, "m_bmg", [128, 2, 512], F32)
        gpo = P.sbuf(es, "m_gpo", [128, 2, 512], F32)
        ggt = P.sbuf(es, "m_ggt", [128, 2, 512], F32)
        psfm = P.psum(es, "m_psfm", [128, 64], F32)
        psg = P.psum(es, "m_psg", [128, 2, 512], F32)
        P.dma('sp', 'm_cT', cT[:], g.cT, writes=['m_cT'])
        P.dma('sp', 'm_bmf', bmf[:], g.bm_fm, writes=['m_bmf'])
        P.dma('sp', 'm_gpr', gpr[:], g.gpre, writes=['m_gpr'])
        P.op('act', lambda e: e.activation(out=sc[:], in_=cT[:], func=AF.Silu), reads=['m_cT'], writes=['m_sc'])
        P.op('dve', lambda e: e.tensor_copy(out=scb[:], in_=sc[:].unsqueeze(3).broadcast_to([128, KC, 2, 128])),
             reads=['m_sc'], writes=['m_scb'])
        rw = Rot("m_wm", 2)
        rg = Rot("m_psg", 2)
        rb = Rot("m_bg", 2)
        rt = Rot("m_ggt", 2)
        for l in range(DEPTH):
            for j in range(12):
                kw, nw = rw.next()
                P.dma('sp', nw, wm[:, kw], g.wmod[l, j], writes=[nw])
                if j < 8:
                    for fc in range(4):
                        col = (j * 4 + fc) * 2
                        for kc in range(KC):
                            P.op('pe', lambda e: e.matmul(psfm[:, col:col + 2], lhsT=wm[:, kw, kc, fc * 128:(fc + 1) * 128],
                                                          rhs=sc[:, kc, :], start=(kc == 0), stop=(kc == KC - 1)),
                                 reads=[nw, 'm_sc'], writes=['m_psfm'])
                    if j == 7:
                        P.op('dve', lambda e: e.tensor_tensor(
                            out=modfm[:], in0=psfm[:].rearrange("p (c j) -> p c j", j=2),
                            in1=bmf[:, l, :].unsqueeze(2).broadcast_to([128, 32, 2]), op=ALU.add),
                            reads=['m_psfm', 'm_bmf'], writes=['m_modfm'])
                        P.op('dve', lambda e: e.tensor_scalar(out=gssh[:, 0], in0=modfm[:, 16:32, :], scalar1=1.0,
                                                               scalar2=None, op0=ALU.add),
                             reads=['m_modfm'], writes=['m_gssh'])
                        P.op('dve', lambda e: e.tensor_tensor(
                            out=gssh[:, 0], in0=gssh[:, 0],
                            in1=gpr[:, l, :].unsqueeze(2).broadcast_to([128, KC, 2]), op=ALU.mult),
                            reads=['m_gssh', 'm_gpr'], writes=['m_gssh'])
                        P.op('dve', lambda e: e.tensor_copy(out=gssh[:, 1], in_=modfm[:, 0:16, :]),
                             reads=['m_modfm'], writes=['m_gssh'])
                        P.dma('sp', 'm_gssh', g.gsd[l].rearrange("a p k j -> p a k j"), gssh[:],
                              reads=['m_gssh'], writes=[])
                else:
                    jg = j - 8
                    kb, nb = rb.next()
                    P.dma('sp', nb, bmg[:, kb], g.bm_g[:, l, jg * 512:(jg + 1) * 512], writes=[nb])
                    P.dma('sp', nb, gpo[:, kb], g.gpost[:, l, jg * 512:(jg + 1) * 512], writes=[nb])
                    for jj in range(2):
                        kg, ng = rg.next()
                        for kc in range(KC):
                            P.op('pe', lambda e: e.matmul(psg[:, kg, :], lhsT=scb[:, kc, jj, :], rhs=wm[:, kw, kc, :],
                                                          start=(kc == 0), stop=(kc == KC - 1)),
                                 reads=[nw, 'm_scb'], writes=[ng])
                        kt, ntt = rt.next()
                        P.op('dve', lambda e: e.tensor_tensor(out=ggt[:, kt], in0=psg[:, kg, :], in1=bmg[:, kb],
                                                               op=ALU.add), reads=[ng, nb], writes=[ntt])
                        P.op('dve', lambda e: e.tensor_tensor(out=ggt[:, kt], in0=ggt[:, kt], in1=gpo[:, kb],
                                                               op=ALU.mult), reads=[ntt, nb], writes=[ntt])
                        P.dma('sp', ntt, g.ggd[l, jj, :, jg * 512:(jg + 1) * 512], ggt[:, kt], reads=[ntt], writes=[])
        P.barrier()


def phase_inproj(P, g, K, l, src):
    with ExitStack() as es:
        gssh = P.sbuf(es, "i_gssh", [128, 2, KC, 2], F32)
        xt = P.sbuf(es, "i_xt", [128, 2, D], F32)
        junk = P.sbuf(es, "i_junk", [128, D], BF16)
        xb = P.sbuf(es, "i_xb", [128, 2, D], BF16)
        st = P.sbuf(es, "i_st", [128, 2, 4], F32)
        xnT = P.sbuf(es, "i_xnT", [128, 2, KC, 1024], BF16)
        wf = P.sbuf(es, "i_wf", [128, 3, KC, 128], BF16)
        wt = P.sbuf(es, "i_wt", [128, 2, KC, 512], BF16)
        sfm = P.sbuf(es, "i_sfm", [128, 2, 1024], BF16)
        sg = P.sbuf(es, "i_sg", [4, 4, 1024], F32)
        stm = P.sbuf(es, "i_stm", [128, 4, 512], BF16)
        psT = P.psum(es, "i_psT", [128, 2, 8, 128], BF16)
        psF = P.psum(es, "i_psF", [128, 4, 512], F32)
        psM = P.psum(es, "i_psM", [128, 2, 512], F32)
        P.dma('sp', 'i_gssh', gssh[:], g.gsd[l].rearrange("a p k j -> p a k j"), writes=['i_gssh'])
        rx, rT, rwf, rwt = Rot("i_xt", 2), Rot("i_psT", 2), Rot("i_wf", 3), Rot("i_wt", 2)
        rF, rM, rsf, rsm = Rot("i_psF", 2), Rot("i_psM", 2), Rot("i_sfm", 2), Rot("i_stm", 4)
        evi = [0]

        def evac(out, in_, reads, writes, par=None):
            evi[0] += 1
            if (evi[0] if par is None else par) % 2 == 0:
                P.op('act', lambda e: e.activation(out=out, in_=in_, func=AF.Copy), reads=reads, writes=writes)
            else:
                P.op('dve', lambda e: e.tensor_copy(out=out, in_=in_), reads=reads, writes=writes)

        import os
        NG = int(os.environ.get("NGRP", "5"))

        def prep_tile(gi, ti):
            t0, gn = GROUPS[gi]
            jj = 1 if t0 < TC else 0
            xbuf = gi % 2
            kx, nx = rx.next()
            r0 = t0 + ti * 128
            P.dma('sp', nx, xt[:, kx, :], src[r0:r0 + 128, :], writes=[nx])
            nst = "i_st%d" % kx
            P.op('act', lambda e: e.activation(out=junk[:], in_=xt[:, kx, :], func=AF.Square,
                                               accum_out=st[:, kx, 0:1]), reads=[nx], writes=['i_junk', nst])
            P.op('dve', lambda e: e.tensor_scalar(out=st[:, kx, 1:2], in0=st[:, kx, 0:1], scalar1=1.0 / D,
                                                   scalar2=EPS, op0=ALU.mult, op1=ALU.add),
                 reads=[nst], writes=[nst])
            P.op('act', lambda e: e.activation(out=st[:, kx, 2:3], in_=st[:, kx, 1:2], func=AF.Sqrt),
                 reads=[nst], writes=[nst])
            P.op('dve', lambda e: e.reciprocal(out=st[:, kx, 3:4], in_=st[:, kx, 2:3]), reads=[nst], writes=[nst])
            nxb = "i_xb%d" % kx
            P.op('act', lambda e: e.activation(out=xb[:, kx, :], in_=xt[:, kx, :], func=AF.Copy,
                                               scale=st[:, kx, 3:4]), reads=[nx, nst], writes=[nxb])
            for q4 in range(2):
                kT, nT = rT.next()
                for i in range(8):
                    kc = q4 * 8 + i
                    P.op('pe', lambda e: e.transpose(out=psT[:, kT, i, :], in_=xb[:, kx, kc * 128:(kc + 1) * 128],
                                                     identity=K.identb[:]), reads=[nxb, 'identb'], writes=[nT])
                for i in range(8):
                    kc = q4 * 8 + i
                    dst = xnT[:, xbuf, kc, ti * 128:(ti + 1) * 128]
                    if os.environ.get('DSTJ'):
                        dst = junk[:, kc * 128:(kc + 1) * 128]
                    s1 = gssh[:, 0, kc, jj:jj + 1] if not os.environ.get('CSC') else 1.5
                    s2 = gssh[:, 1, kc, jj:jj + 1] if not os.environ.get('CSC') else 0.5
                    EV = os.environ.get('EVMODE', 'x')
                    if (q4 % 2 == 0 and EV == 'x') or EV == 'd':
                        P.op('dve', lambda e: e.tensor_scalar(out=dst, in0=psT[:, kT, i, :],
                                                               scalar1=s1,
                                                               scalar2=s2,
                                                               op0=ALU.mult, op1=ALU.add),
                             reads=[nT, 'i_gssh'], writes=['i_xn%d_%d_0' % (xbuf, ti)])
                    else:
                        P.op('act', lambda e: e.activation(out=dst, in_=psT[:, kT, i, :], func=AF.Identity,
                                                           scale=s1,
                                                           bias=s2),
                             reads=[nT, 'i_gssh'], writes=['i_xn%d_%d_1' % (xbuf, ti)])

        def prep_list(gi):
            return [(lambda gi=gi, ti=ti: prep_tile(gi, ti)) for ti in range(GROUPS[gi][1] // 128)]

        for f_ in prep_list(0):
            f_()
        for gi, (t0, gn) in enumerate(GROUPS[:NG]):
            ntile = gn // 128
            jj = 1 if t0 < TC else 0
            xbuf = gi % 2
            nxt_prep = prep_list(gi + 1) if gi + 1 < NG else []
            xkeys = ['i_xn%d_%d_%d' % (xbuf, a, b) for a in range(ntile) for b in range(2)]
            nsub = (gn + 511) // 512
            subw = min(gn, 512)
            PARTS = os.environ.get('IP_PARTS', 'fgt')
            for c in range(NFM):
                if (c < 30 and 'f' not in PARTS) or (c == 30 and 'g' not in PARTS):
                    continue
                if c % 3 == 2 and nxt_prep:
                    nxt_prep.pop(0)()
                kw, nw = rwf.next()
                P.dma('sp', nw, wf[:, kw].rearrange("p k c -> p (k c)"), g.wfm_b[l, c, :, :], writes=[nw])
                if c < 30:
                    kF, nF = rF.next()
                    for s in range(nsub):
                        for kc in range(KC):
                            P.op('pe', lambda e: e.matmul(psF[:, kF * 2 + s, 0:subw], lhsT=wf[:, kw, kc, :],
                                                          rhs=xnT[:, xbuf, kc, s * 512:s * 512 + subw],
                                                          start=(kc == 0), stop=(kc == KC - 1)),
                                 reads=[nw] + xkeys, writes=[nF])
                    ks, ns = rsf.next()
                    for s in range(nsub):
                        evac(sfm[:, ks, s * 512:s * 512 + subw], psF[:, kF * 2 + s, 0:subw], [nF], [ns], par=c)
                    P.dma('pool', ns, g.PT[c * 128:(c + 1) * 128, t0:t0 + gn], sfm[:, ks, 0:gn], reads=[ns],
                          writes=[])
                else:
                    for j4 in range(4):
                        kF, nF = rF.next()
                        for s in range(nsub):
                            for kc in range(KC):
                                P.op('pe', lambda e: e.matmul(psF[0:4, kF * 2 + s, 0:subw],
                                                              lhsT=wf[:, kw, kc, j4 * 4:(j4 + 1) * 4],
                                                              rhs=xnT[:, xbuf, kc, s * 512:s * 512 + subw],
                                                              start=(kc == 0), stop=(kc == KC - 1)),
                                     reads=[nw] + xkeys, writes=[nF])
                        for s in range(nsub):
                            evac(sg[0:4, j4, s * 512:s * 512 + subw], psF[0:4, kF * 2 + s, 0:subw], [nF],
                                 ['i_sg%d' % j4], par=j4)
                        P.dma('pool', 'i_sg%d' % j4, g.PG[j4, :, t0:t0 + gn], sg[0:4, j4, 0:gn],
                              reads=['i_sg%d' % j4], writes=[])
            for f_ in nxt_prep:
                f_()
            for c in range(NTM):
                if 't' not in PARTS:
                    continue
                kw, nw = rwt.next()
                P.dma('sp', nw, wt[:, kw].rearrange("p k c -> p (k c)"), g.wtm_b[l, c, :, :], writes=[nw])
                cw = 512 if c < NTM - 1 else 256
                for ti in range(ntile):
                    kM, nM = rM.next()
                    for kc in range(KC):
                        P.op('pe', lambda e: e.matmul(psM[:, kM, 0:cw], lhsT=xnT[:, xbuf, kc, ti * 128:(ti + 1) * 128],
                                                      rhs=wt[:, kw, kc, 0:cw], start=(kc == 0), stop=(kc == KC - 1)),
                             reads=[nw, 'i_xn%d_%d_0' % (xbuf, ti), 'i_xn%d_%d_1' % (xbuf, ti)], writes=[nM])
                    ks, ns = rsm.next()
                    evac(stm[:, ks, 0:cw], psM[:, kM, 0:cw], [nM], [ns])
                    r0 = t0 + ti * 128
                    P.dma('pool', ns, g.PM[r0:r0 + 128, c * 512:c * 512 + cw], stm[:, ks, 0:cw], reads=[ns], writes=[])
        P.barrier()


def setup_consts(P, g, es):
    K = Ctx()
    tmp = P.sbuf(es, "k_tmp", [128, 128], F32)
    K.identb = P.sbuf(es, "k_identb", [128, 128], BF16)
    K.identf = P.sbuf(es, "k_identf", [128, 128], F32)
    P.dma('sp', 'identf', K.identf[:], g.ident, writes=['identf'])
    P.op('dve', lambda e: e.tensor_copy(out=K.identb[:], in_=K.identf[:]), reads=['identf'], writes=['identb'])
    return K


SCALE = 128 ** -0.5


def phase_attn(P, g, K, l, with_ctx, cast_layers=()):
    with ExitStack() as es:
        caster = PoolCaster(P, g, es, cast_layers) if cast_layers else None

        def wc_jobs(n):
            if caster is not None:
                caster.emit(n)
        ropeC = P.sbuf(es, "c_ropeC", [128, TL], F32)
        ropeS = P.sbuf(es, "c_ropeS", [128, TL], F32)
        permb = P.sbuf(es, "c_permb", [128, 128], BF16)
        bandb = P.sbuf(es, "c_bandb", [128, 384], BF16)
        ctmp = P.sbuf(es, "c_ctmp", [128, 384], F32)
        sink = P.sbuf(es, "c_sink", [128, DEPTH, 8], F32)
        nsink = P.sbuf(es, "c_nsink", [128, 8], F32)
        kraw = P.sbuf(es, "c_kraw", [128, T], BF16)
        kr = P.sbuf(es, "c_kr", [128, NKV, TL], BF16)
        kctx = P.sbuf(es, "c_kctx", [128, NKV, TC], BF16)
        vv = P.sbuf(es, "c_v", [128, NKV, NT, 128], BF16)
        qraw = P.sbuf(es, "c_qraw", [128, T], BF16)
        qr = P.sbuf(es, "c_qr", [128, TL], BF16)
        czr = P.sbuf(es, "c_czr", [128, NT, 128], BF16)
        szh = P.sbuf(es, "c_szh", [128, NT, 128], BF16)
        t1 = P.sbuf(es, "c_t1", [128, 2, 512], F32)
        t2 = P.sbuf(es, "c_t2", [128, 2, 512], F32)
        Pt = P.sbuf(es, "c_Pt", [128, 2, 640], BF16)
        PTs = P.sbuf(es, "c_PTs", [128, 2, 5, 128], BF16)
        stt = P.sbuf(es, "c_st", [128, 4, 8], F32)
        ot = P.sbuf(es, "c_ot", [128, 4, 128], BF16)
        psR = P.psum(es, "c_psR", [128, 2, 512], F32)
        psS = P.psum(es, "c_psS", [128, 2, 2, 512], F32)
        psPT = P.psum(es, "c_psPT", [128, 2, 8, 128], BF16)
        P.dma('sp', 'c_ropeC', ropeC[:], g.ropec, writes=['c_ropeC'])
        P.dma('sp', 'c_ropeS', ropeS[:], g.ropes, writes=['c_ropeS'])
        P.dma('sp', 'c_ctmp', ctmp[:, 0:128], g.perm, writes=['c_ctmp'])
        P.op('dve', lambda e: e.tensor_copy(out=permb[:], in_=ctmp[:, 0:128]), reads=['c_ctmp'], writes=['c_permb'])
        P.dma('sp', 'c_ctmp', ctmp[:], g.band, writes=['c_ctmp'])
        P.op('dve', lambda e: e.tensor_copy(out=bandb[:], in_=ctmp[:]), reads=['c_ctmp'], writes=['c_bandb'])
        P.dma('sp', 'c_sink', sink[:], g.sink, writes=['c_sink'])
        P.op('dve', lambda e: e.tensor_scalar(out=nsink[:], in0=sink[:, l, :], scalar1=-1.0, scalar2=None,
                                               op0=ALU.mult), reads=['c_sink'], writes=['c_nsink'])
        rR, rt = Rot("c_psR", 2), Rot("c_t", 2)
        PMv = g.PM.rearrange("(n p) c -> p n c", p=128)

        def rope(dst, src, n512, skey, dkey):
            for i in range(n512):
                kR, nR = rR.next()
                kt, nt = rt.next()
                sl = slice(i * 512, (i + 1) * 512)
                P.op('pe', lambda e: e.matmul(psR[:, kR, :], lhsT=permb[:], rhs=src[:, sl], start=True, stop=True),
                     reads=[skey, 'c_permb'], writes=[nR])
                P.op('dve', lambda e: e.tensor_tensor(out=t1[:, kt, :], in0=psR[:, kR, :], in1=ropeS[:, sl],
                                                       op=ALU.mult), reads=['c_ropeS'], writes=[nR, nt + 'a'])
                P.op('pool', lambda e: e.tensor_tensor(out=t2[:, kt, :], in0=src[:, sl], in1=ropeC[:, sl],
                                                        op=ALU.mult), reads=[skey, 'c_ropeC'], writes=[nt + 'b'])
                P.op('dve', lambda e: e.tensor_tensor(out=dst[:, sl], in0=t1[:, kt, :], in1=t2[:, kt, :],
                                                       op=ALU.add), reads=[nt + 'a', nt + 'b'], writes=[dkey])

        for kv in range(NKV):
            P.dma('sp', 'c_kraw', kraw[:], g.PT[(C_CK + kv) * 128:(C_CK + kv + 1) * 128, :], writes=['c_kraw'])
            P.op('pool', lambda e: e.tensor_copy(out=kctx[:, kv, :], in_=kraw[:, 0:TC]), reads=['c_kraw'],
                 writes=['c_kctx'])
            rope(kr[:, kv, :], kraw[:, TC:], 8, 'c_kraw', 'c_kr')
            P.dma('sp', 'c_v', vv[:, kv], PMv[:, :, O_CV + kv * 128:O_CV + (kv + 1) * 128], writes=['c_v'])
        rS, rP, rPT, rst, rot = Rot("c_psS", 2), Rot("c_Pt", 2), Rot("c_PTs", 2), Rot("c_st", 4), Rot("c_ot", 4)

        def block(qT, qkey, kv, hq, kloc, nk, b0, vtiles, sz, dst):
            kS, nS = rS.next()
            kP, nP = rP.next()
            kst, nst = rst.next()
            kT, nT = rPT.next()
            kO, nO = rR.next()
            ko, no = rot.next()
            st = stt[:, kst, :]
            W = nk * 128
            npt = 'c_psPT%d' % kT

            def s1():
                if nk > 0:
                    P.op('pe', lambda e: e.matmul(psS[:, kS, 0, 0:W], lhsT=qT, rhs=kloc, start=True, stop=False),
                         reads=[qkey, 'c_kr'], writes=[nS])
                    P.op('pe', lambda e: e.matmul(psS[:, kS, 0, 0:W], lhsT=K.identb[:], rhs=bandb[:, b0:b0 + W],
                                                  start=False, stop=True), reads=['identb', 'c_bandb'], writes=[nS])
                P.op('pe', lambda e: e.matmul(psS[:, kS, 1, 0:TC], lhsT=qT, rhs=kctx[:, kv, :], start=True, stop=True),
                     reads=[qkey, 'c_kctx'], writes=[nS])
                P.op('dve', lambda e: e.tensor_reduce(out=st[:, 1:2], in_=psS[:, kS, 1, 0:TC], axis=AX.X, op=ALU.max),
                     writes=[nS, nst])
                P.op('dve', lambda e: e.tensor_scalar(out=st[:, 2:3], in0=st[:, 1:2], scalar1=-SCALE,
                                                       scalar2=nsink[:, hq:hq + 1], op0=ALU.mult, op1=ALU.min),
                     reads=['c_nsink'], writes=[nst])
                if nk > 0:
                    P.op('dve', lambda e: e.tensor_reduce(out=st[:, 0:1], in_=psS[:, kS, 0, 0:W], axis=AX.X, op=ALU.max),
                         writes=[nS, nst])
                    P.op('dve', lambda e: e.tensor_scalar(out=st[:, 2:3], in0=st[:, 0:1], scalar1=-SCALE,
                                                           scalar2=st[:, 2:3], op0=ALU.mult, op1=ALU.min),
                         writes=[nst])
                else:
                    P.op('dve', lambda e: e.memset(st[:, 3:4], 0.0), writes=[nst])

            def s2():
                if nk > 0:
                    P.op('act', lambda e: e.activation(out=Pt[:, kP, 0:W], in_=psS[:, kS, 0, 0:W], func=AF.Exp,
                                                       scale=SCALE, bias=st[:, 2:3], accum_out=st[:, 3:4]),
                         reads=[], writes=[nS, nP, nst])
                P.op('act', lambda e: e.activation(out=Pt[:, kP, 384:640], in_=psS[:, kS, 1, 0:TC], func=AF.Exp,
                                                   scale=SCALE, bias=st[:, 2:3], accum_out=st[:, 4:5]),
                     reads=[], writes=[nS, nP, nst])
                P.op('act', lambda e: e.activation(out=st[:, 5:6], in_=st[:, 2:3], func=AF.Exp, scale=1.0,
                                                   bias=sink[:, l, hq:hq + 1]), reads=['c_sink'], writes=[nst])
                P.op('dve', lambda e: e.tensor_reduce(out=st[:, 6:7], in_=st[:, 3:6], axis=AX.X, op=ALU.add),
                     writes=[nst])
                P.op('dve', lambda e: e.reciprocal(out=st[:, 7:8], in_=st[:, 6:7]), writes=[nst])
                for j in range(nk):
                    P.op('pe', lambda e: e.transpose(out=psPT[:, kT, j, :], in_=Pt[:, kP, j * 128:(j + 1) * 128],
                                                     identity=K.identb[:]), reads=[nP, 'identb'], writes=[npt])
                for j in range(2):
                    P.op('pe', lambda e: e.transpose(out=psPT[:, kT, 3 + j, :],
                                                     in_=Pt[:, kP, 384 + j * 128:384 + (j + 1) * 128],
                                                     identity=K.identb[:]), reads=[nP, 'identb'], writes=[npt])

            def s3():
                if nk == 3:
                    P.op('act', lambda e: e.activation(out=PTs[:, kT], in_=psPT[:, kT, 0:5, :], func=AF.Copy),
                         writes=[npt, nT])
                else:
                    if nk > 0:
                        P.op('act', lambda e: e.activation(out=PTs[:, kT, 0:nk], in_=psPT[:, kT, 0:nk, :], func=AF.Copy),
                             writes=[npt, nT])
                    P.op('act', lambda e: e.activation(out=PTs[:, kT, 3:5], in_=psPT[:, kT, 3:5, :], func=AF.Copy),
                         writes=[npt, nT])
                seq = [(j, vtiles[j]) for j in range(nk)] + [(3, 0), (4, 1)]
                for i, (j, vt) in enumerate(seq):
                    P.op('pe', lambda e: e.matmul(psR[:, kO, 0:128], lhsT=PTs[:, kT, j, :], rhs=vv[:, kv, vt, :],
                                                  start=(i == 0), stop=(i == len(seq) - 1)),
                         reads=[nT, 'c_v'], writes=[nO])
                P.op('dve', lambda e: e.scalar_tensor_tensor(out=ot[:, ko, :], in0=psR[:, kO, 0:128], scalar=st[:, 7:8],
                                                              in1=sz, op0=ALU.mult, op1=ALU.mult),
                     reads=[nst, 'c_szh'], writes=[nO, no])
                P.dma('sp', no, dst, ot[:, ko, :], reads=[no], writes=[])
            return (s1, s2, s3)

        def run_pipe(blocks):
            n = len(blocks)
            for i in range(n + 2):
                if i < n:
                    blocks[i][0]()
                if 0 <= i - 1 < n:
                    blocks[i - 1][1]()
                if 0 <= i - 2 < n:
                    blocks[i - 2][2]()

        for hq in range(NQ):
            kv = hq // (NQ // NKV)
            P.dma('sp', 'c_qraw', qraw[:], g.PT[(C_CQ + hq) * 128:(C_CQ + hq + 1) * 128, :], writes=['c_qraw'])
            P.dma('sp', 'c_czr', czr[:], PMv[:, :, O_CZ + hq * 128:O_CZ + (hq + 1) * 128], writes=['c_czr'])
            P.op('act', lambda e: e.activation(out=szh[:], in_=czr[:], func=AF.Silu), reads=['c_czr'], writes=['c_szh'])
            rope(qr, qraw[:, TC:], 8, 'c_qraw', 'c_qr')
            blks = []
            if with_ctx:
                for ct in range(2):
                    blks.append(block(qraw[:, ct * 128:(ct + 1) * 128], 'c_qraw', kv, hq, None, 0, 0, [], szh[:, ct, :],
                                      g.mix[ct * 128:(ct + 1) * 128, 1024 + hq * 128:1024 + (hq + 1) * 128]))
            for n in range(TL // 128):
                lo, hi = max(n - 1, 0), min(n + 1, TL // 128 - 1)
                nk = hi - lo + 1
                b0 = (lo - (n - 1)) * 128
                blks.append(block(qr[:, n * 128:(n + 1) * 128], 'c_qr', kv, hq, kr[:, kv, lo * 128:(hi + 1) * 128], nk, b0,
                                  [2 + lo + j for j in range(nk)], szh[:, 2 + n, :],
                                  g.mix[TC + n * 128:TC + (n + 1) * 128, 1024 + hq * 128:1024 + (hq + 1) * 128]))
            run_pipe(blks)
            wc_jobs(10)
        wc_jobs(1000)
        P.barrier()


def order_chunks(d, with_ctx_first=True):
    if d == 0:
        return list(range(NT))
    return [1, 0] + list(range(NT - 1, 1, -1))


def phase_mlstm(P, g, K, l):
    with ExitStack() as es:
        gcol = P.sbuf(es, "b_gcol", [128, NT, 16], F32)
        cb = P.sbuf(es, "b_cb", [128, 512], F32)
        convw = P.sbuf(es, "b_convw", [128, DEPTH, 3, 8], F32)
        maskf = P.sbuf(es, "b_maskf", [128, 2, 128], F32)
        es2 = ExitStack()
        gb = P.sbuf(es2, "b_gb", [4, DEPTH, 4], F32)
        rmask = P.sbuf(es2, "b_rmask", [4, T], F32)
        G1 = P.sbuf(es2, "b_G1", [4, T], F32)
        G2 = P.sbuf(es2, "b_G2", [4, T], F32)
        G3 = P.sbuf(es2, "b_G3", [4, T], F32)
        G4 = P.sbuf(es2, "b_G4", [4, T], F32)
        sm = P.sbuf(es2, "b_sm", [4, 6, NT + 1], F32)
        comb = P.sbuf(es2, "b_comb", [16, T], F32)
        psA = P.psum(es, "b_psA", [128, 2, 512], F32)
        psK = P.psum(es, "b_psK", [128, 2, 1024], BF16)
        psN = P.psum(es, "b_psN", [128, 2, 512], F32)
        psU = P.psum(es, "b_psU", [128, 2, 512], F32)
        P.dma('sp', 'b_gb', gb[:], g.gbias, writes=['b_gb'])
        P.dma('sp', 'b_rmask', rmask[:], g.rmask, writes=['b_rmask'])
        P.dma('sp', 'b_convw', convw[:], g.convw, writes=['b_convw'])
        P.dma('sp', 'b_maskf', maskf[:, 0, :], g.maskf, writes=['b_maskf'])
        P.dma('sp', 'b_maskf', maskf[:, 1, :], g.maskb, writes=['b_maskf'])
        v3 = lambda t: t[:].rearrange("p (c s) -> p c s", s=128)
        for d in range(2):
            ordr = order_chunks(d)
            P.dma('sp', 'b_G1', G1[:], g.PG[2 * d], writes=['b_G1'])
            P.dma('sp', 'b_G2', G2[:], g.PG[2 * d + 1], writes=['b_G2'])
            P.op('dve', lambda e: e.tensor_scalar(out=G1[:], in0=G1[:], scalar1=gb[:, l, 2 * d:2 * d + 1], scalar2=None,
                                                   op0=ALU.add), reads=['b_gb'], writes=['b_G1'])
            P.op('act', lambda e: e.activation(out=G2[:], in_=G2[:], func=AF.Sigmoid,
                                               bias=gb[:, l, 2 * d + 1:2 * d + 2], scale=1.0),
                 reads=['b_gb'], writes=['b_G2'])
            P.op('act', lambda e: e.activation(out=G2[:], in_=G2[:], func=AF.Ln), writes=['b_G2'])
            P.op('dve', lambda e: e.tensor_tensor_scan(out=G3[:], data0=rmask[:], data1=G2[:], initial=0.0,
                                                        op0=ALU.mult, op1=ALU.add),
                 reads=['b_rmask', 'b_G2'], writes=['b_G3'])
            if d == 1:
                P.op('dve', lambda e: e.tensor_tensor(out=G2[:], in0=G2[:], in1=G3[:], op=ALU.subtract),
                     reads=['b_G3'], writes=['b_G2'])
                P.op('dve', lambda e: e.tensor_tensor(out=v3(G3), in0=v3(G2),
                                                       in1=v3(G3)[:, :, 127:128].broadcast_to([4, NT, 128]),
                                                       op=ALU.add), reads=['b_G2'], writes=['b_G3'])
            P.op('dve', lambda e: e.tensor_tensor(out=G1[:], in0=G1[:], in1=G3[:], op=ALU.subtract),
                 reads=['b_G3'], writes=['b_G1'])
            P.op('dve', lambda e: e.tensor_reduce(out=sm[:, 0, 0:NT], in_=v3(G1), axis=AX.X, op=ALU.max),
                 reads=['b_G1'], writes=['b_sm'])
            ex = 127 if d == 0 else 0
            P.op('dve', lambda e: e.tensor_copy(out=sm[:, 1, 0:NT], in_=v3(G3)[:, :, ex]), reads=['b_G3'],
                 writes=['b_sm'])
            P.op('dve', lambda e: e.memset(sm[:, 2, :], 0.0), writes=['b_sm'])
            for i, c in enumerate(ordr):
                P.op('dve', lambda e: e.tensor_tensor(out=sm[:, 3, c:c + 1], in0=sm[:, 2, c:c + 1],
                                                       in1=sm[:, 0, c:c + 1], op=ALU.max), writes=['b_sm'])
                nx = ordr[i + 1] if i + 1 < NT else NT
                P.op('dve', lambda e: e.tensor_tensor(out=sm[:, 2, nx:nx + 1], in0=sm[:, 1, c:c + 1],
                                                       in1=sm[:, 3, c:c + 1], op=ALU.add), writes=['b_sm'])
            P.op('dve', lambda e: e.tensor_tensor(out=sm[:, 4, 0:NT], in0=sm[:, 2, 0:NT], in1=sm[:, 3, 0:NT],
                                                   op=ALU.subtract), writes=['b_sm'])
            P.op('act', lambda e: e.activation(out=sm[:, 4, 0:NT], in_=sm[:, 4, 0:NT], func=AF.Exp), writes=['b_sm'])
            P.dma('sp', 'b_sm', g.carry[0, d * 256:d * 256 + 4 * 64].rearrange("(h c) -> h c", c=64)[:, 0:NT],
                  sm[:, 4, 0:NT], reads=['b_sm'], writes=['carry'])
            Gb = sm[:, 3, 0:NT].unsqueeze(2).broadcast_to([4, NT, 128])
            P.op('dve', lambda e: e.tensor_tensor(out=v3(G4), in0=v3(G1), in1=Gb, op=ALU.subtract),
                 reads=['b_G1', 'b_sm'], writes=['b_G4'])
            P.op('act', lambda e: e.activation(out=G4[:], in_=G4[:], func=AF.Exp), writes=['b_G4'])
            P.op('dve', lambda e: e.tensor_tensor(out=v3(G2), in0=v3(G3), in1=Gb, op=ALU.add),
                 reads=['b_G3', 'b_sm'], writes=['b_G2'])
            P.op('act', lambda e: e.activation(out=G2[:], in_=G2[:], func=AF.Exp, scale=-1.0), writes=['b_G2'])
            P.dma('sp', 'b_comb%da' % d, comb[d * 8:d * 8 + 4, :], G4[:], reads=['b_G4'], writes=['b_comb%da' % d])
            P.dma('sp', 'b_comb%db' % d, comb[d * 8 + 4:d * 8 + 8, :], G2[:], reads=['b_G2'], writes=['b_comb%db' % d])
        for half in range(2):
            for cc in range(17):
                c = half * 17 + cc
                P.op('pe', lambda e: e.matmul(psA[:, half, cc * 16:(cc + 1) * 16], lhsT=comb[:, c * 128:(c + 1) * 128],
                                              rhs=K.identf[0:16, 0:16], start=True, stop=True),
                     reads=['b_comb0a', 'b_comb0b', 'b_comb1a', 'b_comb1b', 'identf'], writes=['b_psA%d' % half])
            P.op('dve', lambda e: e.tensor_copy(out=gcol[:, half * 17:(half + 1) * 17, :].rearrange("p c r -> p (c r)"),
                                                 in_=psA[:, half, 0:272]), writes=['b_psA%d' % half, 'b_gcol'])
        P.dma('sp', 'b_cb', cb[:], g.carry[0].partition_broadcast(128), reads=['carry'], writes=['b_cb'])
        P.barrier()
        es2.close()
        raw = P.sbuf(es, "b_raw", [128, T], BF16)
        acc = P.sbuf(es, "b_acc", [128, T], F32)
        qT = P.sbuf(es, "b_qT", [128, NB, T], BF16)
        kT = P.sbuf(es, "b_kT", [128, NB, T], BF16)
        vaug = P.sbuf(es, "b_vaug", [128, NB, NT, 129], BF16)
        Cst = P.sbuf(es, "b_C", [128, 2 * NB, 129], F32)
        Cbf = P.sbuf(es, "b_Cbf", [128, 2 * NB, 129], BF16)
        Wt = P.sbuf(es, "b_Wt", [128, 4, 128], BF16)
        kw = P.sbuf(es, "b_kw", [128, 4, 128], BF16)
        dm = P.sbuf(es, "b_dm", [128, 4, 2], F32)
        ho = P.sbuf(es, "b_ho", [128, 4, 128], F32)
        PMv = g.PM.rearrange("(n p) c -> p n c", p=128)
        for h in range(NB):
            for qk in range(2):
                ch = (C_BQ if qk == 0 else C_BK) + h
                P.dma('sp', 'b_raw', raw[:], g.PT[ch * 128:(ch + 1) * 128, :], writes=['b_raw'])
                w = lambda j: convw[:, l, j, qk * 4 + h:qk * 4 + h + 1]
                P.op('dve', lambda e: e.tensor_scalar(out=acc[:], in0=raw[:], scalar1=w(1), scalar2=None, op0=ALU.mult),
                     reads=['b_raw', 'b_convw'], writes=['b_acc'])
                for (s0, e0) in ((0, TC), (TC, T)):
                    P.op('dve', lambda e: e.scalar_tensor_tensor(out=acc[:, s0 + 1:e0], in0=raw[:, s0:e0 - 1], scalar=w(0),
                                                                  in1=acc[:, s0 + 1:e0], op0=ALU.mult, op1=ALU.add),
                         reads=['b_raw', 'b_convw'], writes=['b_acc'])
                    P.op('dve', lambda e: e.scalar_tensor_tensor(out=acc[:, s0:e0 - 1], in0=raw[:, s0 + 1:e0], scalar=w(2),
                                                                  in1=acc[:, s0:e0 - 1], op0=ALU.mult, op1=ALU.add),
                         reads=['b_raw', 'b_convw'], writes=['b_acc'])
                if qk == 0:
                    P.op('act', lambda e: e.activation(out=acc[:], in_=acc[:], func=AF.Silu), writes=['b_acc'])
                    P.op('pool', lambda e: e.tensor_scalar(out=qT[:, h, :], in0=acc[:], scalar1=SCALE, scalar2=None,
                                                            op0=ALU.mult), reads=['b_acc'], writes=['b_qT%d' % h])
                else:
                    P.op('act', lambda e: e.activation(out=kT[:, h, :], in_=acc[:], func=AF.Silu), reads=['b_acc'],
                         writes=['b_kT%d' % h])
            P.op('pool', lambda e: e.memset(vaug[:, h, :, 128:129], 1.0), writes=['b_v%d' % h])
            P.dma('sp', 'b_v%d' % h, vaug[:, h, :, 0:128], PMv[:, :, O_BV + h * 128:O_BV + (h + 1) * 128],
                  writes=['b_v%d' % h])
        P.op('pool', lambda e: e.memset(Cst[:], 0.0), writes=['b_C%d' % i for i in range(2 * NB)])
        rA, rK, rN, rU = Rot("b_psAr", 2), Rot("b_psK", 2), Rot("b_psN", 2), Rot("b_psU", 2)
        rW, rkw, rdm, rho = Rot("b_Wt", 4), Rot("b_kw", 4), Rot("b_dm", 4), Rot("b_ho", 4)
        mpend = [None]

        def mstage2(h, d, c, ci, ts, thr, car, kN, nN, kU, nU, kW, nW, kk, nkw, kd, ndm, kh, nho, qk_):
            P.op('pe', lambda e: e.matmul(psN[:, kN, 0:129], lhsT=qT[:, h, ts], rhs=Cbf[:, ci, :], start=True,
                                          stop=False), reads=[qk_, 'b_Cbf%d' % ci], writes=[nN])
            P.op('pe', lambda e: e.matmul(psN[:, kN, 0:129], lhsT=Wt[:, kW, :], rhs=vaug[:, h, c, :], start=False,
                                          stop=True), reads=[nW, 'b_v%d' % h], writes=[nN])
            P.op('pe', lambda e: e.matmul(psU[:, kU, 0:129], lhsT=kw[:, kk, :], rhs=vaug[:, h, c, :], start=True,
                                          stop=True), reads=[nkw, 'b_v%d' % h], writes=[nU])
            P.op('dve', lambda e: e.scalar_tensor_tensor(out=Cst[:, ci, :], in0=Cst[:, ci, :], scalar=car,
                                                          in1=psU[:, kU, 0:129], op0=ALU.mult, op1=ALU.add),
                 reads=['b_cb'], writes=[nU, 'b_C%d' % ci])
            P.op('dve', lambda e: e.tensor_scalar(out=dm[:, kd, 1:2], in0=psN[:, kN, 128:129], scalar1=-1.0,
                                                   scalar2=None, op0=ALU.mult), writes=[nN, ndm])
            P.op('dve', lambda e: e.scalar_tensor_tensor(out=dm[:, kd, 0:1], in0=psN[:, kN, 128:129], scalar=thr,
                                                          in1=dm[:, kd, 1:2], op0=ALU.max, op1=ALU.max),
                 reads=['b_gcol'], writes=[nN, ndm])
            P.op('dve', lambda e: e.reciprocal(out=dm[:, kd, 1:2], in_=dm[:, kd, 0:1]), writes=[ndm])
            P.op('act', lambda e: e.activation(out=ho[:, kh, :], in_=psN[:, kN, 0:128], func=AF.Copy,
                                               scale=dm[:, kd, 1:2]), reads=[ndm], writes=[nN, nho])
            P.dma('sp', nho, g.yB[d, c * 128:(c + 1) * 128, h * 128:(h + 1) * 128], ho[:, kh, :], reads=[nho],
                  writes=[])

        orders = [order_chunks(0), order_chunks(1)]
        for i in range(NT):
            for h in range(NB):
                for d in range(2):
                    c = orders[d][i]
                    ci = d * NB + h
                    ts = slice(c * 128, (c + 1) * 128)
                    ea = gcol[:, c, d * 8 + h:d * 8 + h + 1]
                    thr = gcol[:, c, d * 8 + 4 + h:d * 8 + 4 + h + 1]
                    car = cb[:, d * 256 + h * 64 + c:d * 256 + h * 64 + c + 1]
                    kA, nA = rA.next()
                    kK, nK = rK.next()
                    kN, nN = rN.next()
                    kU, nU = rU.next()
                    kW, nW = rW.next()
                    kk, nkw = rkw.next()
                    kd, ndm = rdm.next()
                    kh, nho = rho.next()
                    qk_ = 'b_qT%d' % h
                    kk_ = 'b_kT%d' % h
                    P.op('pe', lambda e: e.matmul(psA[:, kA, 0:128], lhsT=kT[:, h, ts], rhs=qT[:, h, ts], start=True,
                                                  stop=True), reads=[qk_, kk_], writes=[nA])
                    P.op('dve', lambda e: e.scalar_tensor_tensor(out=Wt[:, kW, :], in0=psA[:, kA, 0:128], scalar=ea,
                                                                  in1=maskf[:, d, :], op0=ALU.mult, op1=ALU.mult),
                         reads=['b_gcol', 'b_maskf'], writes=[nA, nW])
                    P.op('pe', lambda e: e.transpose(out=psK[:, kK, 0:128], in_=kT[:, h, ts], identity=K.identb[:]),
                         reads=[kk_, 'identb'], writes=[nK])
                    P.op('act', lambda e: e.activation(out=kw[:, kk, :], in_=psK[:, kK, 0:128], func=AF.Copy, scale=ea),
                         reads=['b_gcol'], writes=[nK, nkw])
                    P.op('act', lambda e: e.activation(out=Cbf[:, ci, :], in_=Cst[:, ci, :], func=AF.Copy, scale=car),
                         reads=['b_cb', 'b_C%d' % ci], writes=['b_Cbf%d' % ci])
                    nxt = (lambda h=h, d=d, c=c, ci=ci, ts=ts, thr=thr, car=car, kN=kN, nN=nN, kU=kU, nU=nU, kW=kW, nW=nW,
                           kk=kk, nkw=nkw, kd=kd, ndm=ndm, kh=kh, nho=nho, qk_=qk_:
                           mstage2(h, d, c, ci, ts, thr, car, kN, nN, kU, nU, kW, nW, kk, nkw, kd, ndm, kh, nho, qk_))
                    if mpend[0] is not None:
                        mpend[0]()
                    mpend[0] = nxt
        mpend[0]()
        P.barrier()


def phase_readout(P, g, K, which, with_ctx):
    with ExitStack() as es:
        y = g.yA if which == 'a' else g.yB
        ocol = O_AG if which == 'a' else O_BZ
        mcol = 0 if which == 'a' else 512
        yfb = P.sbuf(es, "r_yfb", [128, 3, 2, 512], F32)
        yf = yfb[:, :, 0, :]
        yb = yfb[:, :, 1, :]
        goz = P.sbuf(es, "r_goz", [128, 3, 2, 512], BF16)
        go = goz[:, :, 0, :]
        gz = goz[:, :, 1, :]
        sz = P.sbuf(es, "r_sz", [128, 3, 512], F32)
        so = P.sbuf(es, "r_so", [128, 3, 512], F32)
        junk = P.sbuf(es, "r_junk", [128, 128], F32)
        nhalf = P.sbuf(es, "r_nhalf", [128, 4], F32)
        P.op('pool', lambda e: e.memset(nhalf[:], -0.5), writes=['r_nhalf'])
        st = P.sbuf(es, "r_st", [128, 3, 16], F32)
        om = P.sbuf(es, "r_om", [128, 3, 512], BF16)
        rr = Rot("r", 3)
        rpend = [None]
        for tt in range(0 if with_ctx else 2, NT):
            k, n = rr.next()
            rows = slice(tt * 128, (tt + 1) * 128)
            def rs2(k=k, n=n, rows=rows):
                P.op('dve', lambda e: e.tensor_scalar(out=st[:, k, 4:8], in0=st[:, k, 0:4], scalar1=1.0 / 128, scalar2=EPS,
                                                       op0=ALU.mult, op1=ALU.add), writes=[n + 'st'])
                P.op('pool', lambda e: e.tensor_tensor(out=st[:, k, 12:16], in0=st[:, k, 4:8], in1=nhalf[:, 0:4],
                                                        op=ALU.pow), reads=['r_nhalf'], writes=[n + 'st'])
                if which == 'b':
                    P.op('act', lambda e: e.activation(out=sz[:, k, :], in_=gz[:, k, :], func=AF.Sigmoid),
                         reads=[n + 'gz'], writes=[n + 'sz'])
                    P.op('pool', lambda e: e.tensor_tensor(out=sz[:, k, :], in0=sz[:, k, :], in1=gz[:, k, :],
                                                            op=ALU.mult), reads=[n + 'gz'], writes=[n + 'sz'])
                else:
                    P.op('act', lambda e: e.activation(out=sz[:, k, :], in_=gz[:, k, :], func=AF.Silu),
                         reads=[n + 'gz'], writes=[n + 'sz'])
                for h in range(4):
                    hs = slice(h * 128, (h + 1) * 128)
                    P.op('dve', lambda e: e.scalar_tensor_tensor(out=om[:, k, hs], in0=yf[:, k, hs],
                                                                  scalar=st[:, k, 12 + h:13 + h], in1=sz[:, k, hs],
                                                                  op0=ALU.mult, op1=ALU.mult),
                         reads=[n + 'yf', n + 'st', n + 'sz'], writes=[n + 'om'])
                P.dma('pool', n + 'om', g.mix[rows, mcol:mcol + 512], om[:, k, :], reads=[n + 'om'], writes=[])
            P.dma('sp', n + 'yf', yfb[:, k], y[:, rows, :].rearrange("d p c -> p d c"), writes=[n + 'yf', n + 'yb'])
            if which == 'b':
                P.dma('sp', n + 'gz', goz[:, k].rearrange("p a c -> p (a c)"), g.PM[rows, O_BO:O_BO + 1024],
                      writes=[n + 'gz', n + 'go'])
            else:
                P.dma('sp', n + 'gz', gz[:, k, :], g.PM[rows, ocol:ocol + 512], writes=[n + 'gz'])
            P.op('dve', lambda e: e.tensor_tensor(out=yf[:, k, :], in0=yf[:, k, :], in1=yb[:, k, :], op=ALU.add),
                 reads=[n + 'yb'], writes=[n + 'yf'])
            if which == 'b':
                P.op('act', lambda e: e.activation(out=so[:, k, :], in_=go[:, k, :], func=AF.Sigmoid), reads=[n + 'go'],
                     writes=[n + 'so'])
                P.op('dve', lambda e: e.tensor_tensor(out=yf[:, k, :], in0=yf[:, k, :], in1=so[:, k, :], op=ALU.mult),
                     reads=[n + 'so'], writes=[n + 'yf'])
            for h in range(4):
                P.op('dve', lambda e: e.scalar_tensor_tensor(out=junk[:], in0=yf[:, k, h * 128:(h + 1) * 128], scalar=1.0,
                                                              in1=yf[:, k, h * 128:(h + 1) * 128], op0=ALU.mult,
                                                              op1=ALU.mult, accum_out=st[:, k, h:h + 1]),
                     reads=[n + 'yf'], writes=['r_junk', n + 'st'])
            if rpend[0] is not None:
                rpend[0]()
            rpend[0] = rs2
        rpend[0]()
        P.barrier()


NBLK = T // 32


def order_blocks(d):
    if d == 0:
        return list(range(NBLK))
    return list(range(7, -1, -1)) + list(range(NBLK - 1, 7, -1))


def phase_hgrn(P, g, K, l):
    with ExitStack() as es:
        lbl = P.sbuf(es, "a_lbl", [128, DEPTH, 8], F32)
        lb = P.sbuf(es, "a_lb", [128, 3, 8], F32)
        msk = P.sbuf(es, "a_msk", [128, 2, 128], F32)
        ones = P.sbuf(es, "a_ones", [128, T], BF16)
        zq = P.sbuf(es, "a_zq", [128, T], BF16)
        qs = P.sbuf(es, "a_qs", [128, T], BF16)
        zf = P.sbuf(es, "a_zf", [128, T], BF16)
        Bt = P.sbuf(es, "a_B", [128, T], F32)
        Ct = P.sbuf(es, "a_C", [128, T], F32)
        Dt = P.sbuf(es, "a_D", [128, T], F32)
        kk = P.sbuf(es, "a_k", [128, T], BF16)
        qt = P.sbuf(es, "a_qt", [128, 2, T], BF16)
        kt = P.sbuf(es, "a_kt", [128, 2, T], BF16)
        EH = P.sbuf(es, "a_EH", [128, 2, NBLK], F32)
        gin = P.sbuf(es, "a_gin", [128, 2, NBLK], F32)
        V = P.sbuf(es, "a_V", [32, NBLK, 128], BF16)
        R = P.sbuf(es, "a_R", [128, 2, 2, 128], F32)
        Rbf = P.sbuf(es, "a_Rbf", [128, 2, 128], BF16)
        KTM = P.sbuf(es, "a_KTM", [32, 2, 2, 4, 128], BF16)
        Wt = P.sbuf(es, "a_Wt", [32, 2, 2, 4, 32], BF16)
        osb = P.sbuf(es, "a_osb", [32, 2, 2, 4, 128], F32)
        psKT = P.psum(es, "a_psKT", [128, 2, 1024], BF16)
        psU = P.psum(es, "a_psU", [128, 2, 512], F32)
        psS = P.psum(es, "a_psS", [128, 2, 512], F32)
        psO = P.psum(es, "a_psO", [128, 2, 512], F32)
        P.dma('sp', 'a_lbl', lbl[:], g.lbl, writes=['a_lbl'])
        P.dma('sp', 'a_msk', msk[:, 0, :], g.maskf, writes=['a_msk'])
        P.dma('sp', 'a_msk', msk[:, 1, :], g.maskb, writes=['a_msk'])
        P.op('pool', lambda e: e.memset(ones[:], 1.0), writes=['a_ones'])
        if l == 0:
            P.op('dve', lambda e: e.memset(lb[:, 0, :], 0.0), writes=['a_lb'])
        else:
            P.op('dve', lambda e: e.tensor_tensor(out=lb[:, 0, :], in0=lbl[:, 1, :], in1=lbl[:, 0, :], op=ALU.subtract),
                 reads=['a_lbl'], writes=['a_lb'])
            P.op('act', lambda e: e.activation(out=lb[:, 0, :], in_=lb[:, 0, :], func=AF.Sigmoid), writes=['a_lb'])
        P.op('dve', lambda e: e.tensor_scalar(out=lb[:, 1, :], in0=lb[:, 0, :], scalar1=-1.0, scalar2=1.0, op0=ALU.mult,
                                               op1=ALU.add), writes=['a_lb'])
        P.op('dve', lambda e: e.tensor_scalar(out=lb[:, 2, :], in0=lb[:, 1, :], scalar1=-1.0, scalar2=None,
                                               op0=ALU.mult), writes=['a_lb'])
        b3 = lambda t: t.rearrange("p (c s) -> p c s", s=32)
        PMb = g.PM.rearrange("(c s) w -> s c w", s=32)
        for h in range(NA):
            P.dma('sp', 'a_zq', zq[:], g.PT[(C_AQ + h) * 128:(C_AQ + h + 1) * 128, :], writes=['a_zq'])
            P.op('act', lambda e: e.activation(out=qs[:], in_=zq[:], func=AF.Silu), reads=['a_zq'], writes=['a_qs'])
            P.dma('sp', 'a_V', V[:], PMb[:, :, O_AI + h * 128:O_AI + (h + 1) * 128], writes=['a_V'])
            for d in range(2):
                ch = (C_AFF if d == 0 else C_AFB) + h
                col = d * 4 + h
                lbc, om, nom = lb[:, 0, col:col + 1], lb[:, 1, col:col + 1], lb[:, 2, col:col + 1]
                P.dma('sp', 'a_zf', zf[:], g.PT[ch * 128:(ch + 1) * 128, :], writes=['a_zf'])
                P.op('act', lambda e: e.activation(out=Bt[:], in_=zf[:], func=AF.Sigmoid), reads=['a_zf'], writes=['a_B'])
                P.op('dve', lambda e: e.tensor_scalar(out=Ct[:], in0=Bt[:], scalar1=om, scalar2=lbc, op0=ALU.mult,
                                                       op1=ALU.add), reads=['a_B', 'a_lb'], writes=['a_C'])
                P.op('act', lambda e: e.activation(out=Ct[:], in_=Ct[:], func=AF.Ln), writes=['a_C'])
                P.op('dve', lambda e: e.tensor_scalar(out=kk[:], in0=Bt[:], scalar1=nom, scalar2=om, op0=ALU.mult,
                                                       op1=ALU.add), reads=['a_B', 'a_lb'], writes=['a_k'])
                P.op('dve', lambda e: e.tensor_tensor_scan(out=Dt[:], data0=ones[:], data1=Ct[:], initial=0.0,
                                                            op0=ALU.mult, op1=ALU.add),
                     reads=['a_ones', 'a_C'], writes=['a_D'])
                P.op('dve', lambda e: e.tensor_tensor(out=Ct[:], in0=Dt[:], in1=Ct[:], op=ALU.subtract), reads=['a_D'],
                     writes=['a_C'])
                D3, X3, B3 = b3(Dt[:]), b3(Ct[:]), b3(Bt[:])
                if d == 0:
                    P.op('dve', lambda e: e.tensor_tensor(out=B3, in0=D3, in1=D3[:, :, 15:16].broadcast_to([128, NBLK, 32]),
                                                           op=ALU.subtract), reads=['a_D'], writes=['a_B'])
                    P.op('dve', lambda e: e.tensor_copy(out=EH[:, 0, :], in_=B3[:, :, 31]), reads=['a_B'], writes=['a_EH'])
                    P.op('dve', lambda e: e.tensor_tensor(out=EH[:, 1, :], in0=D3[:, :, 15], in1=X3[:, :, 0],
                                                           op=ALU.subtract), reads=['a_D', 'a_C'], writes=['a_EH'])
                    P.op('dve', lambda e: e.tensor_tensor(out=gin[:, d, 1:NBLK], in0=EH[:, 0, 0:NBLK - 1],
                                                           in1=EH[:, 1, 1:NBLK], op=ALU.add), writes=['a_EH', 'a_gin%d' % d])
                    P.op('dve', lambda e: e.memset(gin[:, d, 0:1], 0.0), writes=['a_gin%d' % d])
                else:
                    P.op('dve', lambda e: e.tensor_tensor(out=B3, in0=X3[:, :, 16:17].broadcast_to([128, NBLK, 32]),
                                                           in1=X3, op=ALU.subtract), reads=['a_C'], writes=['a_B'])
                    P.op('dve', lambda e: e.tensor_copy(out=EH[:, 0, :], in_=B3[:, :, 0]), reads=['a_B'], writes=['a_EH'])
                    P.op('dve', lambda e: e.tensor_tensor(out=EH[:, 1, :], in0=D3[:, :, 31], in1=X3[:, :, 16],
                                                           op=ALU.subtract), reads=['a_D', 'a_C'], writes=['a_EH'])
                    P.op('dve', lambda e: e.memset(gin[:, d, 7:8], 0.0), writes=['a_gin%d' % d])
                    P.op('dve', lambda e: e.tensor_tensor(out=gin[:, d, 0:7], in0=EH[:, 0, 1:8], in1=EH[:, 1, 0:7],
                                                           op=ALU.add), writes=['a_EH', 'a_gin%d' % d])
                    P.op('dve', lambda e: e.tensor_tensor(out=gin[:, d, 8:NBLK - 1], in0=EH[:, 0, 9:NBLK],
                                                           in1=EH[:, 1, 8:NBLK - 1], op=ALU.add),
                         writes=['a_EH', 'a_gin%d' % d])
                    P.op('dve', lambda e: e.tensor_tensor(out=gin[:, d, NBLK - 1:NBLK], in0=EH[:, 0, 0:1],
                                                           in1=EH[:, 1, NBLK - 1:NBLK], op=ALU.add),
                         writes=['a_EH', 'a_gin%d' % d])
                P.op('act', lambda e: e.activation(out=gin[:, d, :], in_=gin[:, d, :], func=AF.Exp),
                     writes=['a_gin%d' % d])
                P.op('act', lambda e: e.activation(out=Dt[:], in_=Bt[:], func=AF.Exp), reads=['a_B'], writes=['a_D'])
                P.op('act', lambda e: e.activation(out=Ct[:], in_=Bt[:], func=AF.Exp, scale=-1.0), reads=['a_B'],
                     writes=['a_C'])
                P.op('dve', lambda e: e.tensor_tensor(out=qt[:, d, :], in0=qs[:], in1=Dt[:], op=ALU.mult),
                     reads=['a_qs', 'a_D'], writes=['a_qt%d' % d])
                P.op('dve', lambda e: e.tensor_tensor(out=kt[:, d, :], in0=kk[:], in1=Ct[:], op=ALU.mult),
                     reads=['a_k', 'a_C'], writes=['a_kt%d' % d])
            P.op('pool', lambda e: e.memset(R[:], 0.0), writes=['a_R0_0', 'a_R0_1', 'a_R1_0', 'a_R1_1'])
            orders = [order_blocks(0), order_blocks(1)]
            rK = [Rot("a_KTM0_", 2), Rot("a_KTM1_", 2)]
            rW = [Rot("a_Wt0_", 2), Rot("a_Wt1_", 2)]
            ro = [Rot("a_osb0_", 2), Rot("a_osb1_", 2)]
            cur = [None, None]
            curo = [None, None]
            apend = [None]
            for i in range(NBLK):
                pp = i % 2
                for d in range(2):
                    c = orders[d][i]
                    tile_, j = c // 4, c % 4
                    bs = slice(c * 32, (c + 1) * 32)
                    nqt, nkt = 'a_qt%d' % d, 'a_kt%d' % d
                    if cur[d] is None or cur[d][0] != tile_:
                        kK, nK = rK[d].next()
                        for jj in range(4):
                            cs = slice(tile_ * 128 + jj * 32, tile_ * 128 + (jj + 1) * 32)
                            P.op('pe', lambda e: e.transpose(out=psKT[0:32, d, jj * 128:(jj + 1) * 128], in_=kt[:, d, cs],
                                                             identity=K.identb[:]), reads=[nkt, 'identb'],
                                 writes=['a_psKT%d' % d])
                        P.op('act', lambda e: e.activation(out=KTM[:, d, kK].rearrange("p j c -> p (j c)"),
                                                           in_=psKT[0:32, d, 0:512], func=AF.Copy),
                             writes=['a_psKT%d' % d, nK])
                        for jj in range(4):
                            cs = slice(tile_ * 128 + jj * 32, tile_ * 128 + (jj + 1) * 32)
                            P.op('pe', lambda e: e.matmul(psS[0:32, d, jj * 32:(jj + 1) * 32], lhsT=kt[:, d, cs],
                                                          rhs=qt[:, d, cs], start=True, stop=True),
                                 reads=[nqt, nkt], writes=['a_psS%d' % d])
                        kW, nW = rW[d].next()
                        P.op('dve', lambda e: e.tensor_tensor(
                            out=Wt[:, d, kW], in0=psS[0:32, d, 0:128].rearrange("p (j t) -> p j t", t=32),
                            in1=msk[0:32, d, 0:32].unsqueeze(1).broadcast_to([32, 4, 32]), op=ALU.mult),
                            reads=['a_msk'], writes=['a_psS%d' % d, nW])
                        ko, no = ro[d].next()
                        cur[d] = (tile_, kK, nK, kW, nW)
                        curo[d] = (ko, no, 0)
                    _, kK, nK, kW, nW = cur[d]
                    ko, no, cnt = curo[d]
                    P.op('pe', lambda e: e.matmul(psU[:, d, 0:128], lhsT=KTM[:, d, kK, j, :], rhs=V[:, c, :], start=True,
                                                  stop=True), reads=[nK, 'a_V'], writes=['a_psU%d' % d])
                    P.op('act', lambda e: e.activation(out=Rbf[:, d, :], in_=R[:, d, pp, :], func=AF.Copy,
                                                       scale=gin[:, d, c:c + 1]),
                         reads=['a_R%d_%d' % (d, pp), 'a_gin%d' % d], writes=['a_Rbf%d' % d])
                    cnt += 1
                    curo[d] = (ko, no, cnt)

                    def st2(d=d, c=c, bs=bs, nqt=nqt, kW=kW, nW=nW, ko=ko, no=no, j=j, cnt=cnt, tile_=tile_, h=h, pp=pp):
                        P.op('pe', lambda e: e.matmul(psO[0:32, d, j * 128:(j + 1) * 128], lhsT=qt[:, d, bs], rhs=Rbf[:, d, :],
                                                      start=True, stop=False),
                             reads=[nqt, 'a_Rbf%d' % d], writes=['a_psO%d' % d])
                        P.op('pe', lambda e: e.matmul(psO[0:32, d, j * 128:(j + 1) * 128], lhsT=Wt[:, d, kW, j, :],
                                                      rhs=V[:, c, :], start=False, stop=True),
                             reads=[nW, 'a_V'], writes=['a_psO%d' % d])
                        P.op('dve', lambda e: e.scalar_tensor_tensor(out=R[:, d, 1 - pp, :], in0=R[:, d, pp, :],
                                                                      scalar=gin[:, d, c:c + 1], in1=psU[:, d, 0:128],
                                                                      op0=ALU.mult, op1=ALU.add),
                             reads=['a_gin%d' % d, 'a_R%d_%d' % (d, pp)],
                             writes=['a_psU%d' % d, 'a_R%d_%d' % (d, 1 - pp)])
                        if cnt == 4:
                            P.op('dve', lambda e: e.tensor_copy(out=osb[:, d, ko].rearrange("p j v -> p (j v)"),
                                                                 in_=psO[0:32, d, 0:512]),
                                 writes=['a_psO%d' % d, no])
                            P.dma('sp', no, g.yA[d, tile_ * 128:(tile_ + 1) * 128, h * 128:(h + 1) * 128].rearrange(
                                "(j s) v -> s j v", s=32), osb[:, d, ko], reads=[no], writes=[])
                    if apend[0] is not None:
                        apend[0]()
                    apend[0] = st2
            apend[0]()
            apend[0] = None
        P.barrier()


def phase_outproj(P, g, K, l, src, with_ctx, final):
    with ExitStack() as es:
        wo = P.sbuf(es, "o_wo", [128, 4, KC, 512], BF16)
        gg = P.sbuf(es, "o_gg", [128, 2, D], F32)
        mt = P.sbuf(es, "o_mt", [128, 2, D], BF16)
        mixT = P.sbuf(es, "o_mixT", [128, 2, KC, 128], BF16)
        xt = P.sbuf(es, "o_xt", [128, 2, D], F32)
        tmp = P.sbuf(es, "o_tmp", [128, 2, D], F32)
        ysb = P.sbuf(es, "o_ysb", [128, 2, D], F32)
        junk = P.sbuf(es, "o_junk", [128, 512], BF16)
        st = P.sbuf(es, "o_st", [128, 2, 8], F32)
        psT = P.psum(es, "o_psT", [128, 2, 8, 128], BF16)
        psY = P.psum(es, "o_psY", [128, 4, 512], F32)
        for jc in range(4):
            P.dma('sp', 'o_wo%d' % jc, wo[:, jc].rearrange("p k c -> p (k c)"), g.wo_b[l, jc, :, :], writes=['o_wo%d' % jc])
        P.dma('sp', 'o_gg0', gg[:, 0, :], g.ggd[l, 0], writes=['o_gg0'])
        P.dma('sp', 'o_gg1', gg[:, 1, :], g.ggd[l, 1], writes=['o_gg1'])
        rr = Rot("o", 2)
        for tt in range(0 if with_ctx else 2, NT):
            k, n = rr.next()
            jj = 1 if tt < 2 else 0
            rows = slice(tt * 128, (tt + 1) * 128)
            P.dma('sp', n + 'mt', mt[:, k, :], g.mix[rows, :], writes=[n + 'mt'])
            P.dma('sp', n + 'xt', xt[:, k, :], src[rows, :], writes=[n + 'xt'])
            for q2 in range(2):
                for i in range(8):
                    kc = q2 * 8 + i
                    P.op('pe', lambda e: e.transpose(out=psT[:, q2, i, :], in_=mt[:, k, kc * 128:(kc + 1) * 128],
                                                     identity=K.identb[:]), reads=[n + 'mt', 'identb'],
                         writes=['o_psT%d' % q2])
                if q2 == 0:
                    P.op('dve', lambda e: e.tensor_copy(out=mixT[:, k, 0:8, :], in_=psT[:, 0]),
                         writes=['o_psT0', n + 'mixTa'])
                else:
                    P.op('act', lambda e: e.activation(out=mixT[:, k, 8:16, :], in_=psT[:, 1], func=AF.Copy),
                         writes=['o_psT1', n + 'mixTb'])
            for jc in range(4):
                for kc in range(KC):
                    P.op('pe', lambda e: e.matmul(psY[:, jc, :], lhsT=mixT[:, k, kc, :], rhs=wo[:, jc, kc, :],
                                                  start=(kc == 0), stop=(kc == KC - 1)),
                         reads=[n + 'mixTa', n + 'mixTb', 'o_wo%d' % jc], writes=['o_psY%d' % jc])
            for jc in range(4):
                P.op('act', lambda e: e.activation(out=junk[:], in_=psY[:, jc, :], func=AF.Square,
                                                   accum_out=st[:, k, jc:jc + 1]),
                     writes=['o_psY%d' % jc, 'o_junk', n + 'st'])
                P.op('dve', lambda e: e.tensor_copy(out=ysb[:, k, jc * 512:(jc + 1) * 512], in_=psY[:, jc, :]),
                     writes=['o_psY%d' % jc, n + 'ysb%d' % jc])
            P.op('dve', lambda e: e.tensor_reduce(out=st[:, k, 4:5], in_=st[:, k, 0:4], axis=AX.X, op=ALU.add),
                 writes=[n + 'st'])
            P.op('dve', lambda e: e.tensor_scalar(out=st[:, k, 5:6], in0=st[:, k, 4:5], scalar1=1.0 / D, scalar2=EPS,
                                                   op0=ALU.mult, op1=ALU.add), writes=[n + 'st'])
            P.op('act', lambda e: e.activation(out=st[:, k, 6:7], in_=st[:, k, 5:6], func=AF.Sqrt), writes=[n + 'st'])
            P.op('dve', lambda e: e.reciprocal(out=st[:, k, 7:8], in_=st[:, k, 6:7]), writes=[n + 'st'])
            for jc in range(4):
                cs = slice(jc * 512, (jc + 1) * 512)
                P.op('dve', lambda e: e.scalar_tensor_tensor(out=tmp[:, k, cs], in0=ysb[:, k, cs], scalar=st[:, k, 7:8],
                                                              in1=gg[:, jj, cs], op0=ALU.mult, op1=ALU.mult),
                     reads=[n + 'st', 'o_gg%d' % jj, n + 'ysb%d' % jc], writes=[n + 'tmp'])
            P.op('pool', lambda e: e.tensor_tensor(out=tmp[:, k, :], in0=tmp[:, k, :], in1=xt[:, k, :], op=ALU.add),
                 reads=[n + 'xt'], writes=[n + 'tmp'])
            if final:
                dst = g.out[(tt - 2) * 128:(tt - 1) * 128, :]
            else:
                dst = g.xs[rows, :]
            P.dma('pool', n + 'tmp', dst, tmp[:, k, :], reads=[n + 'tmp'], writes=[])
        P.barrier()


def build(dbg=False, phases=None):
    nc = bass.Bass("TRN2", target_bir_lowering=False)
    g = declare(nc, dbg)
    if phases is None:
        phases = ("w", "m", "i0", "a0", "b0", "c0", "o0", "i1", "a1", "b1", "c1", "o1")
    with ExitStack() as es:
        P = Prog(nc, es)
        K = setup_consts(P, g, es)
        if "w" in phases:
            phase_wcast(P, g, layers=(0,) if "c0" in phases else (0, 1))
        if "m" in phases:
            phase_mod(P, g, K)
        for l in range(DEPTH):
            wc = (l < DEPTH - 1)
            src = g.xin if l == 0 else g.xs
            if "i%d" % l in phases:
                phase_inproj(P, g, K, l, src)
            if "a%d" % l in phases:
                phase_hgrn(P, g, K, l)
                phase_readout(P, g, K, 'a', wc)
            if "b%d" % l in phases:
                phase_mlstm(P, g, K, l)
                phase_readout(P, g, K, 'b', wc)
            if "c%d" % l in phases:
                phase_attn(P, g, K, l, wc, cast_layers=(1,) if (l == 0 and "w" in phases) else ())
            if "o%d" % l in phases:
                phase_outproj(P, g, K, l, src, wc, l == DEPTH - 1)
        P.finish()
        print("instructions", P.ninst, "waits", P.nwait, P.cnt, "dma sems", len(P.dsem), flush=True)
    return nc


_NC_CACHE = {}


def kernel(**inputs):
    inputs = {k: np.asarray(v) for k, v in inputs.items()}
    per = host_prep(**inputs)
    if 'nc' not in _NC_CACHE:
        _NC_CACHE['nc'] = build()
    nc = _NC_CACHE['nc']
    n = len(per)
    res = run_bass_kernel_spmd(nc, per, core_ids=list(range(n)))
    out = np.stack([np.asarray(res.results[i]['out']) for i in range(n)], axis=0)
    return out.astype(np.float32, copy=False)
```

```python
import numpy as np
from contextlib import ExitStack
import concourse.bass as bass
import concourse.mybir as mybir
from concourse.bass_utils import run_bass_kernel_spmd

F32 = mybir.dt.float32
BF16 = mybir.dt.bfloat16
AF = mybir.ActivationFunctionType
ALU = mybir.AluOpType
AX = mybir.AxisListType

D = 2048
TC = 256
TL = 4096
T = TC + TL
NT = T // 128
KC = D // 128
DEPTH = 2
NA, NB, NQ, NKV = 4, 4, 8, 2
EPS = 1e-6
NFM = 31
NTM = 8
TMW = 3840
O_AI, O_AG, O_BV, O_BO, O_BZ, O_CZ, O_CV = 0, 512, 1024, 1536, 2048, 2560, 3584
C_AQ, C_AFF, C_AFB, C_BQ, C_BK, C_CQ, C_CK, C_G = 0, 4, 8, 12, 16, 20, 28, 30
GROUPS = [(0, 256)] + [(256 + 1024 * i, 1024) for i in range(4)]
NEGM = -30000.0


class Prog:
    def __init__(self, nc, es):
        self.nc = nc
        self.es = es
        self.eng = {'pe': nc.tensor, 'act': nc.scalar, 'dve': nc.vector, 'pool': nc.gpsimd, 'sp': nc.sync}
        self.esem = {e: es.enter_context(nc.semaphore("s_" + e)) for e in ('pe', 'act', 'dve', 'pool')}
        self.cnt = {e: 0 for e in self.esem}
        self.waited = {e: {} for e in self.eng}
        self.last_w = {}
        self.readers = {}
        self.dsem = {}
        self.dcnt = {}
        self.sems = {}
        self.nwait = 0
        self.ninst = 0

    def sbuf(self, es, name, shape, dt):
        self.uid = getattr(self, 'uid', 0) + 1
        return es.enter_context(self.nc.sbuf_tensor("%s_%d" % (name, self.uid), shape, dt))

    def psum(self, es, name, shape, dt):
        self.uid = getattr(self, 'uid', 0) + 1
        return es.enter_context(self.nc.psum_tensor("%s_%d" % (name, self.uid), shape, dt))

    def _wait(self, e, toks):
        need = {}
        for t in toks:
            if t is None:
                continue
            s, v = t
            if v > need.get(id(s), (s, 0))[1]:
                need[id(s)] = (s, v)
        for k, (s, v) in need.items():
            if self.waited[e].get(k, 0) < v:
                self.eng[e].wait_ge(s, v)
                self.waited[e][k] = v
                self.nwait += 1

    def _deps(self, e, reads, writes, own=None):
        toks = []
        for r in reads:
            toks.append(self.last_w.get(r))
        for w in writes:
            lw = self.last_w.get(w)
            if lw is not None and not (own is not None and lw[0] is own):
                if not (e == 'pe' and lw[0] is self.esem[e]):
                    toks.append(lw)
            for t in self.readers.get(w, ()):
                if e == 'pe' and t[0] is self.esem[e]:
                    continue
                toks.append(t)
        return toks

    def _commit(self, tok, reads, writes):
        for r in reads:
            self.readers.setdefault(r, []).append(tok)
        for w in writes:
            self.last_w[w] = tok
            self.readers[w] = []

    def op(self, e, fn, reads=(), writes=()):
        self._wait(e, self._deps(e, reads, writes))
        inst = fn(self.eng[e])
        self.cnt[e] += 1
        self.ninst += 1
        inst.then_inc(self.esem[e], 1)
        tok = (self.esem[e], self.cnt[e])
        self._commit(tok, reads, writes)
        return tok

    def dma(self, q, slot, out, in_, reads=(), writes=()):
        if slot not in self.dsem:
            fl = getattr(self, 'free_sems_' + q, None)
            if fl:
                self.dsem[slot] = fl.pop()
                self.dq[slot] = q
            else:
                if not hasattr(self, 'dq'):
                    self.dq = {}
                self.dq[slot] = q
                self.nsem = getattr(self, 'nsem', 0) + 1
                sm = self.es.enter_context(self.nc.semaphore("d_%d" % self.nsem))
                self.dsem[slot] = sm
                self.dcnt[id(sm)] = 0
        s = self.dsem[slot]
        self._wait(q, self._deps(q, reads, writes, own=s))
        self.dcnt[id(s)] += 16
        self.ninst += 1
        self.eng[q].dma_start(out=out, in_=in_).then_inc(s, 16)
        tok = (s, self.dcnt[id(s)])
        self._commit(tok, reads, writes)
        return tok

    def barrier(self):
        toks = [(self.esem[e], self.cnt[e]) for e in self.esem if self.cnt[e] > 0]
        toks += [(self.dsem[s], self.dcnt[id(self.dsem[s])]) for s in self.dsem]
        for e in self.eng:
            self._wait(e, toks)
        self.last_w = {}
        self.readers = {}
        for slot, sm in self.dsem.items():
            q = self.dq[slot]
            if not hasattr(self, 'free_sems_' + q):
                setattr(self, 'free_sems_' + q, [])
            getattr(self, 'free_sems_' + q).append(sm)
        self.dsem = {}
        self.dq = {}
        if not hasattr(self, 'marks'):
            self.marks = []
        self.marks.append(dict(self.cnt))

    def finish(self):
        toks = [(self.esem[e], self.cnt[e]) for e in self.esem if self.cnt[e] > 0]
        toks += [(self.dsem[s], self.dcnt[id(self.dsem[s])]) for s in self.dsem]
        self._wait('sp', toks)


class Rot:
    def __init__(self, name, n):
        self.name, self.n, self.i = name, n, 0

    def next(self):
        k = self.i % self.n
        self.i += 1
        return k, "%s%d" % (self.name, k)


def dram(nc, name, shape, dt, kind="Internal"):
    return nc.dram_tensor(name, list(shape), dt, kind=kind).ap()


def _proj_offsets():
    off, start = {}, 0
    for name, width in (('a_q', 512), ('a_f_fwd', 512), ('a_f_bwd', 512), ('a_i', 512), ('a_gate', 512),
                        ('b_q', 512), ('b_k', 512), ('b_v', 512), ('b_o', 512), ('b_gates', 16), ('b_z', 512),
                        ('c_q', 1024), ('c_k', 256), ('c_v', 256), ('c_z', 1024)):
        off[name] = start
        start += width
    return off


def host_consts():
    c = {}
    c['ident'] = np.eye(128, dtype=np.float32)
    s = np.arange(128)[:, None]
    t = np.arange(128)[None, :]
    c['maskf'] = (s <= t).astype(np.float32)
    c['maskb'] = (s >= t).astype(np.float32)
    qi = np.arange(128)[:, None]
    kj = np.arange(384)[None, :]
    c['band'] = np.where(np.abs(kj - qi - 128) <= 128, 0.0, NEGM).astype(np.float32)
    perm = np.zeros((128, 128), np.float32)
    for dp in range(128):
        blk = dp // 32
        src = dp + 32 if blk % 2 == 0 else dp - 32
        perm[src, dp] = 1.0
    c['perm'] = perm
    inv_freq = (np.float32(10000.0) ** (-np.arange(0, 64, 2, dtype=np.float32) / np.float32(64))).astype(np.float32)
    pos = np.arange(TL)
    row = (pos // 64).astype(np.float32)
    col = (pos % 64).astype(np.float32)
    ang_r = (row[:, None] * inv_freq[None, :]).astype(np.float32)
    ang_c = (col[:, None] * inv_freq[None, :]).astype(np.float32)
    cos = np.zeros((128, TL), np.float32)
    sin = np.zeros((128, TL), np.float32)
    for d in range(128):
        ang = ang_r if d < 64 else ang_c
        j = d % 32
        cos[d] = np.cos(ang[:, j])
        sg = -1.0 if (d // 32) % 2 == 0 else 1.0
        sin[d] = sg * np.sin(ang[:, j])
    c['ropec'] = cos
    c['ropes'] = sin
    rm = np.ones((4, T), np.float32)
    rm[:, ::128] = 0.0
    c['rmask'] = rm
    return c


def host_prep(x, c, ctx, c_ctx, w_mod, b_mod, g_pre, g_post, w_in, hgrn_lb_logits,
              mlstm_conv_w, mlstm_gate_bias, attn_sink, w_out):
    f = np.float32
    off = _proj_offsets()
    shared = {}
    shared['wmod'] = np.ascontiguousarray(
        w_mod.reshape(DEPTH, KC, 128, 12, 512).transpose(0, 3, 2, 1, 4)).astype(f, copy=False)
    shared['bm_fm'] = np.ascontiguousarray(b_mod[:, :4096].reshape(DEPTH, 32, 128).transpose(2, 0, 1))
    shared['bm_g'] = np.ascontiguousarray(np.broadcast_to(b_mod[None, :, 4096:], (128, DEPTH, 2048)))
    shared['gpre'] = np.ascontiguousarray(g_pre.reshape(DEPTH, KC, 128).transpose(2, 0, 1))
    shared['gpost'] = np.ascontiguousarray(np.broadcast_to(g_post[None], (128, DEPTH, 2048)))
    fm_cols = []
    for nm, nh in (('a_q', 4), ('a_f_fwd', 4), ('a_f_bwd', 4), ('b_q', 4), ('b_k', 4), ('c_q', 8), ('c_k', 2)):
        for h in range(nh):
            fm_cols.append(np.arange(off[nm] + h * 128, off[nm] + (h + 1) * 128))
    fm_cols = np.concatenate(fm_cols)
    gcols = np.arange(off['b_gates'], off['b_gates'] + 16)
    wfm = np.zeros((DEPTH, D, NFM * 128), f)
    wfm[:, :, :30 * 128] = w_in[:, :, fm_cols]
    wfm[:, :, 30 * 128:30 * 128 + 16] = w_in[:, :, gcols]
    shared['wfm'] = np.ascontiguousarray(wfm.reshape(DEPTH, KC, 128, NFM, 128).transpose(0, 3, 2, 1, 4)).reshape(DEPTH, NFM, 128, KC * 128)
    tm_cols = np.concatenate([np.arange(off[nm], off[nm] + w) for nm, w in
                              (('a_i', 512), ('a_gate', 512), ('b_v', 512), ('b_o', 512), ('b_z', 512),
                               ('c_z', 1024), ('c_v', 256))])
    wtm = np.zeros((DEPTH, D, NTM * 512), f)
    wtm[:, :, :TMW] = w_in[:, :, tm_cols]
    shared['wtm'] = np.ascontiguousarray(wtm.reshape(DEPTH, KC, 128, NTM, 512).transpose(0, 3, 2, 1, 4)).reshape(DEPTH, NTM, 128, KC * 512)
    shared['wo'] = np.ascontiguousarray(w_out.reshape(DEPTH, KC, 128, 4, 512).transpose(0, 3, 2, 1, 4)).reshape(DEPTH, 4, 128, KC * 512)
    shared['lbl'] = np.ascontiguousarray(hgrn_lb_logits.reshape(DEPTH, 8, 128).transpose(2, 0, 1))
    shared['convw'] = np.ascontiguousarray(mlstm_conv_w.reshape(DEPTH, 3, 8, 128).transpose(3, 0, 1, 2))
    shared['gbias'] = np.ascontiguousarray(mlstm_gate_bias.reshape(DEPTH, 4, 4).transpose(2, 0, 1))
    shared['sink'] = np.ascontiguousarray(np.broadcast_to(attn_sink[None], (128, DEPTH, 8)))
    shared.update(host_consts())
    per = []
    for b in range(x.shape[0]):
        d = dict(shared)
        d['xin'] = np.ascontiguousarray(np.concatenate([ctx[b], x[b]], axis=0))
        cT = np.stack([c[b].reshape(KC, 128).T, c_ctx.reshape(KC, 128).T], axis=-1)
        d['cT'] = np.ascontiguousarray(cT.astype(f))
        per.append(d)
    return per


class Ctx:
    pass


def declare(nc, dbg=False):
    g = Ctx()
    I = lambda n, s, dt=F32: dram(nc, n, s, dt, "ExternalInput")
    g.xin = I("xin", [T, D])
    g.cT = I("cT", [128, KC, 2])
    g.wmod = I("wmod", [DEPTH, 12, 128, KC, 512])
    g.bm_fm = I("bm_fm", [128, DEPTH, 32])
    g.bm_g = I("bm_g", [128, DEPTH, 2048])
    g.gpre = I("gpre", [128, DEPTH, KC])
    g.gpost = I("gpost", [128, DEPTH, 2048])
    g.wfm = I("wfm", [DEPTH, NFM, 128, KC * 128])
    g.wtm = I("wtm", [DEPTH, NTM, 128, KC * 512])
    g.wo = I("wo", [DEPTH, 4, 128, KC * 512])
    g.lbl = I("lbl", [128, DEPTH, 8])
    g.convw = I("convw", [128, DEPTH, 3, 8])
    g.gbias = I("gbias", [4, DEPTH, 4])
    g.sink = I("sink", [128, DEPTH, 8])
    g.ident = I("ident", [128, 128])
    g.maskf = I("maskf", [128, 128])
    g.maskb = I("maskb", [128, 128])
    g.band = I("band", [128, 384])
    g.perm = I("perm", [128, 128])
    g.ropec = I("ropec", [128, TL])
    g.ropes = I("ropes", [128, TL])
    g.rmask = I("rmask", [4, T])
    g.out = dram(nc, "out", [TL, D], F32, "ExternalOutput")
    dbgset = set(dbg) if dbg else set()
    S = lambda n, s, dt: dram(nc, n, s, dt, "ExternalOutput" if n in dbgset else "Internal")
    g.xs = S("xs", [T, D], F32)
    g.wfm_b = S("wfm_b", [DEPTH, NFM, 128, KC * 128], BF16)
    g.wtm_b = S("wtm_b", [DEPTH, NTM, 128, KC * 512], BF16)
    g.wo_b = S("wo_b", [DEPTH, 4, 128, KC * 512], BF16)
    g.PT = S("PT", [30 * 128, T], BF16)
    g.PG = S("PG", [4, 4, T], F32)
    g.PM = S("PM", [T, TMW], BF16)
    g.mix = S("mix", [T, 2048], BF16)
    g.yA = S("yA", [2, T, 512], F32)
    g.yB = S("yB", [2, T, 512], F32)
    g.ggd = S("ggd", [DEPTH, 2, 128, 2048], F32)
    g.gsd = S("gsd", [DEPTH, 2, 128, KC, 2], F32)
    g.carry = S("carry", [1, 512], F32)
    return g


def wcast_jobs(g, layers):
    jobs = []
    for l in layers:
        for c in range(NFM):
            jobs.append((g.wfm[l, c, :, :], g.wfm_b[l, c, :, :]))
        for c in range(NTM):
            for q in range(4):
                jobs.append((g.wtm[l, c, :, q * 2048:(q + 1) * 2048], g.wtm_b[l, c, :, q * 2048:(q + 1) * 2048]))
        for c in range(4):
            for q in range(4):
                jobs.append((g.wo[l, c, :, q * 2048:(q + 1) * 2048], g.wo_b[l, c, :, q * 2048:(q + 1) * 2048]))
    return jobs


class PoolCaster:
    def __init__(self, P, g, es, layers):
        self.P = P
        self.jobs = wcast_jobs(g, layers)
        self.tin = P.sbuf(es, "pc_in", [128, 2, 2048], F32)
        self.tout = P.sbuf(es, "pc_out", [128, 2, 2048], BF16)
        self.rin, self.rout = Rot("pci", 2), Rot("pco", 2)

    def emit(self, n):
        P = self.P
        for _ in range(n):
            if not self.jobs:
                return
            src, dst = self.jobs.pop(0)
            ki, ni = self.rin.next()
            ko, no = self.rout.next()
            P.dma('pool', ni, self.tin[:, ki, :], src, writes=[ni])
            P.op('pool', lambda en: en.tensor_copy(out=self.tout[:, ko, :], in_=self.tin[:, ki, :]), reads=[ni],
                 writes=[no])
            P.dma('pool', no, dst, self.tout[:, ko, :], reads=[no], writes=[])


def phase_wcast(P, g, layers=(0, 1)):
    with ExitStack() as es:
        tin = P.sbuf(es, "wc_in", [128, 3, 2048], F32)
        tout = P.sbuf(es, "wc_out", [128, 3, 2048], BF16)
        rin, rout = Rot("wci", 3), Rot("wco", 3)
        engs = ['pool', 'act', 'dve']
        jobs = []
        for l in layers:
            for c in range(NFM):
                jobs.append((g.wfm[l, c, :, :], g.wfm_b[l, c, :, :]))
            for c in range(NTM):
                for q in range(4):
                    jobs.append((g.wtm[l, c, :, q * 2048:(q + 1) * 2048], g.wtm_b[l, c, :, q * 2048:(q + 1) * 2048]))
            for c in range(4):
                for q in range(4):
                    jobs.append((g.wo[l, c, :, q * 2048:(q + 1) * 2048], g.wo_b[l, c, :, q * 2048:(q + 1) * 2048]))
        for i, (src, dst) in enumerate(jobs):
            ki, ni = rin.next()
            ko, no = rout.next()
            P.dma('sp', ni, tin[:, ki, :], src, writes=[ni])
            e = engs[i % 3]
            if e == 'act':
                P.op(e, lambda en: en.activation(out=tout[:, ko, :], in_=tin[:, ki, :], func=AF.Copy),
                     reads=[ni], writes=[no])
            else:
                P.op(e, lambda en: en.tensor_copy(out=tout[:, ko, :], in_=tin[:, ki, :]), reads=[ni], writes=[no])
            P.dma('sp', no, dst, tout[:, ko, :], reads=[no], writes=[])
        P.barrier()


def phase_mod(P, g, K):
    with ExitStack() as es:
        cT = P.sbuf(es, "m_cT", [128, KC, 2], F32)
        sc = P.sbuf(es, "m_sc", [128, KC, 2], F32)
        scb = P.sbuf(es, "m_scb", [128, KC, 2, 128], F32)
        wm = P.sbuf(es, "m_wm", [128, 2, KC, 512], F32)
        bmf = P.sbuf(es, "m_bmf", [128, DEPTH, 32], F32)
        gpr = P.sbuf(es, "m_gpr", [128, DEPTH, KC], F32)
        modfm = P.sbuf(es, "m_modfm", [128, 32, 2], F32)
        gssh = P.sbuf(es, "m_gssh", [128, 2, KC, 2], F32)
        bmg = P.sbuf(es, "m_bmg", [128, 2, 512], F32)
        gpo = P.sbuf(es, "m_gpo", [128, 2, 512], F32)
        ggt = P.sbuf(es, "m_ggt", [128, 2, 512], F32)
        psfm = P.psum(es, "m_psfm", [128, 64], F32)
        psg = P.psum(es, "m_psg", [128, 2, 512], F32)
        P.dma('sp', 'm_cT', cT[:], g.cT, writes=['m_cT'])
        P.dma('sp', 'm_bmf', bmf[:], g.bm_fm, writes=['m_bmf'])
        P.dma('sp', 'm_gpr', gpr[:], g.gpre, writes=['m_gpr'])
        P.op('act', lambda e: e.activation(out=sc[:], in_=cT[:], func=AF.Silu), reads=['m_cT'], writes=['m_sc'])
        P.op('dve', lambda e: e.tensor_copy(out=scb[:], in_=sc[:].unsqueeze(3).broadcast_to([128, KC, 2, 128])),
             reads=['m_sc'], writes=['m_scb'])
        rw = Rot("m_wm", 2)
        rg = Rot("m_psg", 2)
        rb = Rot("m_bg", 2)
        rt = Rot("m_ggt", 2)
        for l in range(DEPTH):
            for j in range(12):
                kw, nw = rw.next()
                P.dma('sp', nw, wm[:, kw], g.wmod[l, j], writes=[nw])
                if j < 8:
                    for fc in range(4):
                        col = (j * 4 + fc) * 2
                        for kc in range(KC):
                            P.op('pe', lambda e: e.matmul(psfm[:, col:col + 2], lhsT=wm[:, kw, kc, fc * 128:(fc + 1) * 128],
                                                          rhs=sc[:, kc, :], start=(kc == 0), stop=(kc == KC - 1)),
                                 reads=[nw, 'm_sc'], writes=['m_psfm'])
                    if j == 7:
                        P.op('dve', lambda e: e.tensor_tensor(
                            out=modfm[:], in0=psfm[:].rearrange("p (c j) -> p c j", j=2),
                            in1=bmf[:, l, :].unsqueeze(2).broadcast_to([128, 32, 2]), op=ALU.add),
                            reads=['m_psfm', 'm_bmf'], writes=['m_modfm'])
                        P.op('dve', lambda e: e.tensor_scalar(out=gssh[:, 0], in0=modfm[:, 16:32, :], scalar1=1.0,
                                                               scalar2=None, op0=ALU.add),
                             reads=['m_modfm'], writes=['m_gssh'])
                        P.op('dve', lambda e: e.tensor_tensor(
                            out=gssh[:, 0], in0=gssh[:, 0],
                            in1=gpr[:, l, :].unsqueeze(2).broadcast_to([128, KC, 2]), op=ALU.mult),
                            reads=['m_gssh', 'm_gpr'], writes=['m_gssh'])
                        P.op('dve', lambda e: e.tensor_copy(out=gssh[:, 1], in_=modfm[:, 0:16, :]),
                             reads=['m_modfm'], writes=['m_gssh'])
                        P.dma('sp', 'm_gssh', g.gsd[l].rearrange("a p k j -> p a k j"), gssh[:],
                              reads=['m_gssh'], writes=[])
                else:
                    jg = j - 8
                    kb, nb = rb.next()
                    P.dma('sp', nb, bmg[:, kb], g.bm_g[:, l, jg * 512:(jg + 1) * 512], writes=[nb])
                    P.dma('sp', nb, gpo[:, kb], g.gpost[:, l, jg * 512:(jg + 1) * 512], writes=[nb])
                    for jj in range(2):
                        kg, ng = rg.next()
                        for kc in range(KC):
                            P.op('pe', lambda e: e.matmul(psg[:, kg, :], lhsT=scb[:, kc, jj, :], rhs=wm[:, kw, kc, :],
                                                          start=(kc == 0), stop=(kc == KC - 1)),
                                 reads=[nw, 'm_scb'], writes=[ng])
                        kt, ntt = rt.next()
                        P.op('dve', lambda e: e.tensor_tensor(out=ggt[:, kt], in0=psg[:, kg, :], in1=bmg[:, kb],
                                                               op=ALU.add), reads=[ng, nb], writes=[ntt])
                        P.op('dve', lambda e: e.tensor_tensor(out=ggt[:, kt], in0=ggt[:, kt], in1=gpo[:, kb],
                                                               op=ALU.mult), reads=[ntt, nb], writes=[ntt])
                        P.dma('sp', ntt, g.ggd[l, jj, :, jg * 512:(jg + 1) * 512], ggt[:, kt], reads=[ntt], writes=[])
        P.barrier()


def phase_inproj(P, g, K, l, src):
    with ExitStack() as es:
        gssh = P.sbuf(es, "i_gssh", [128, 2, KC, 2], F32)
        xt = P.sbuf(es, "i_xt", [128, 2, D], F32)
        junk = P.sbuf(es, "i_junk", [128, D], BF16)
        xb = P.sbuf(es, "i_xb", [128, 2, D], BF16)
        st = P.sbuf(es, "i_st", [128, 2, 4], F32)
        xnT = P.sbuf(es, "i_xnT", [128, 2, KC, 1024], BF16)
        wf = P.sbuf(es, "i_wf", [128, 3, KC, 128], BF16)
        wt = P.sbuf(es, "i_wt", [128, 2, KC, 512], BF16)
        sfm = P.sbuf(es, "i_sfm", [128, 2, 1024], BF16)
        sg = P.sbuf(es, "i_sg", [4, 4, 1024], F32)
        stm = P.sbuf(es, "i_stm", [128, 4, 512], BF16)
        psT = P.psum(es, "i_psT", [128, 2, 8, 128], BF16)
        psF = P.psum(es, "i_psF", [128, 4, 512], F32)
        psM = P.psum(es, "i_psM", [128, 2, 512], F32)
        P.dma('sp', 'i_gssh', gssh[:], g.gsd[l].rearrange("a p k j -> p a k j"), writes=['i_gssh'])
        rx, rT, rwf, rwt = Rot("i_xt", 2), Rot("i_psT", 2), Rot("i_wf", 3), Rot("i_wt", 2)
        rF, rM, rsf, rsm = Rot("i_psF", 2), Rot("i_psM", 2), Rot("i_sfm", 2), Rot("i_stm", 4)
        evi = [0]

        def evac(out, in_, reads, writes, par=None):
            evi[0] += 1
            if (evi[0] if par is None else par) % 2 == 0:
                P.op('act', lambda e: e.activation(out=out, in_=in_, func=AF.Copy), reads=reads, writes=writes)
            else:
                P.op('dve', lambda e: e.tensor_copy(out=out, in_=in_), reads=reads, writes=writes)

        import os
        NG = int(os.environ.get("NGRP", "5"))

        def prep_tile(gi, ti):
            t0, gn = GROUPS[gi]
            jj = 1 if t0 < TC else 0
            xbuf = gi % 2
            kx, nx = rx.next()
            r0 = t0 + ti * 128
            P.dma('sp', nx, xt[:, kx, :], src[r0:r0 + 128, :], writes=[nx])
            nst = "i_st%d" % kx
            P.op('act', lambda e: e.activation(out=junk[:], in_=xt[:, kx, :], func=AF.Square,
                                               accum_out=st[:, kx, 0:1]), reads=[nx], writes=['i_junk', nst])
            P.op('dve', lambda e: e.tensor_scalar(out=st[:, kx, 1:2], in0=st[:, kx, 0:1], scalar1=1.0 / D,
                                                   scalar2=EPS, op0=ALU.mult, op1=ALU.add),
                 reads=[nst], writes=[nst])
            P.op('act', lambda e: e.activation(out=st[:, kx, 2:3], in_=st[:, kx, 1:2], func=AF.Sqrt),
                 reads=[nst], writes=[nst])
            P.op('dve', lambda e: e.reciprocal(out=st[:, kx, 3:4], in_=st[:, kx, 2:3]), reads=[nst], writes=[nst])
            nxb = "i_xb%d" % kx
            P.op('act', lambda e: e.activation(out=xb[:, kx, :], in_=xt[:, kx, :], func=AF.Copy,
                                               scale=st[:, kx, 3:4]), reads=[nx, nst], writes=[nxb])
            for q4 in range(2):
                kT, nT = rT.next()
                for i in range(8):
                    kc = q4 * 8 + i
                    P.op('pe', lambda e: e.transpose(out=psT[:, kT, i, :], in_=xb[:, kx, kc * 128:(kc + 1) * 128],
                                                     identity=K.identb[:]), reads=[nxb, 'identb'], writes=[nT])
                for i in range(8):
                    kc = q4 * 8 + i
                    dst = xnT[:, xbuf, kc, ti * 128:(ti + 1) * 128]
                    if os.environ.get('DSTJ'):
                        dst = junk[:, kc * 128:(kc + 1) * 128]
                    s1 = gssh[:, 0, kc, jj:jj + 1] if not os.environ.get('CSC') else 1.5
                    s2 = gssh[:, 1, kc, jj:jj + 1] if not os.environ.get('CSC') else 0.5
                    EV = os.environ.get('EVMODE', 'x')
                    if (q4 % 2 == 0 and EV == 'x') or EV == 'd':
                        P.op('dve', lambda e: e.tensor_scalar(out=dst, in0=psT[:, kT, i, :],
                                                               scalar1=s1,
                                                               scalar2=s2,
                                                               op0=ALU.mult, op1=ALU.add),
                             reads=[nT, 'i_gssh'], writes=['i_xn%d_%d_0' % (xbuf, ti)])
                    else:
                        P.op('act', lambda e: e.activation(out=dst, in_=psT[:, kT, i, :], func=AF.Identity,
                                                           scale=s1,
                                                           bias=s2),
                             reads=[nT, 'i_gssh'], writes=['i_xn%d_%d_1' % (xbuf, ti)])

        def prep_list(gi):
            return [(lambda gi=gi, ti=ti: prep_tile(gi, ti)) for ti in range(GROUPS[gi][1] // 128)]

        for f_ in prep_list(0):
            f_()
        for gi, (t0, gn) in enumerate(GROUPS[:NG]):
            ntile = gn // 128
            jj = 1 if t0 < TC else 0
            xbuf = gi % 2
            nxt_prep = prep_list(gi + 1) if gi + 1 < NG else []
            xkeys = ['i_xn%d_%d_%d' % (xbuf, a, b) for a in range(ntile) for b in range(2)]
            nsub = (gn + 511) // 512
            subw = min(gn, 512)
            PARTS = os.environ.get('IP_PARTS', 'fgt')
            for c in range(NFM):
                if (c < 30 and 'f' not in PARTS) or (c == 30 and 'g' not in PARTS):
                    continue
                if c % 3 == 2 and nxt_prep:
                    nxt_prep.pop(0)()
                kw, nw = rwf.next()
                P.dma('sp', nw, wf[:, kw].rearrange("p k c -> p (k c)"), g.wfm_b[l, c, :, :], writes=[nw])
                if c < 30:
                    kF, nF = rF.next()
                    for s in range(nsub):
                        for kc in range(KC):
                            P.op('pe', lambda e: e.matmul(psF[:, kF * 2 + s, 0:subw], lhsT=wf[:, kw, kc, :],
                                                          rhs=xnT[:, xbuf, kc, s * 512:s * 512 + subw],
                                                          start=(kc == 0), stop=(kc == KC - 1)),
                                 reads=[nw] + xkeys, writes=[nF])
                    ks, ns = rsf.next()
                    for s in range(nsub):
                        evac(sfm[:, ks, s * 512:s * 512 + subw], psF[:, kF * 2 + s, 0:subw], [nF], [ns], par=c)
                    P.dma('pool', ns, g.PT[c * 128:(c + 1) * 128, t0:t0 + gn], sfm[:, ks, 0:gn], reads=[ns],
                          writes=[])
                else:
                    for j4 in range(4):
                        kF, nF = rF.next()
                        for s in range(nsub):
                            for kc in range(KC):
                                P.op('pe', lambda e: e.matmul(psF[0:4, kF * 2 + s, 0:subw],
                                                              lhsT=wf[:, kw, kc, j4 * 4:(j4 + 1) * 4],
                                                              rhs=xnT[:, xbuf, kc, s * 512:s * 512 + subw],
                                                              start=(kc == 0), stop=(kc == KC - 1)),
                                     reads=[nw] + xkeys, writes=[nF])
                        for s in range(nsub):
                            evac(sg[0:4, j4, s * 512:s * 512 + subw], psF[0:4, kF * 2 + s, 0:subw], [nF],
                                 ['i_sg%d' % j4], par=j4)
                        P.dma('pool', 'i_sg%d' % j4, g.PG[j4, :, t0:t0 + gn], sg[0:4, j4, 0:gn],
                              reads=['i_sg%d' % j4], writes=[])
            for f_ in nxt_prep:
                f_()
            for c in range(NTM):
                if 't' not in PARTS:
                    continue
                kw, nw = rwt.next()
                P.dma('sp', nw, wt[:, kw].rearrange("p k c -> p (k c)"), g.wtm_b[l, c, :, :], writes=[nw])
                cw = 512 if c < NTM - 1 else 256
                for ti in range(ntile):
                    kM, nM = rM.next()
                    for kc in range(KC):
                        P.op('pe', lambda e: e.matmul(psM[:, kM, 0:cw], lhsT=xnT[:, xbuf, kc, ti * 128:(ti + 1) * 128],
                                                      rhs=wt[:, kw, kc, 0:cw], start=(kc == 0), stop=(kc == KC - 1)),
                             reads=[nw, 'i_xn%d_%d_0' % (xbuf, ti), 'i_xn%d_%d_1' % (xbuf, ti)], writes=[nM])
                    ks, ns = rsm.next()
                    evac(stm[:, ks, 0:cw], psM[:, kM, 0:cw], [nM], [ns])
                    r0 = t0 + ti * 128
                    P.dma('pool', ns, g.PM[r0:r0 + 128, c * 512:c * 512 + cw], stm[:, ks, 0:cw], reads=[ns], writes=[])
        P.barrier()


def setup_consts(P, g, es):
    K = Ctx()
    tmp = P.sbuf(es, "k_tmp", [128, 128], F32)
    K.identb = P.sbuf(es, "k_identb", [128, 128], BF16)
    K.identf = P.sbuf(es, "k_identf", [128, 128], F32)
    P.dma('sp', 'identf', K.identf[:], g.ident, writes=['identf'])
    P.op('dve', lambda e: e.tensor_copy(out=K.identb[:], in_=K.identf[:]), reads=['identf'], writes=['identb'])
    return K


SCALE = 128 ** -0.5


def phase_attn(P, g, K, l, with_ctx, cast_layers=()):
    with ExitStack() as es:
        caster = PoolCaster(P, g, es, cast_layers) if cast_layers else None

        def wc_jobs(n):
            if caster is not None:
                caster.emit(n)
        ropeC = P.sbuf(es, "c_ropeC", [128, TL], F32)
        ropeS = P.sbuf(es, "c_ropeS", [128, TL], F32)
        permb = P.sbuf(es, "c_permb", [128, 128], BF16)
        bandb = P.sbuf(es, "c_bandb", [128, 384], BF16)
        ctmp = P.sbuf(es, "c_ctmp", [128, 384], F32)
        sink = P.sbuf(es, "c_sink", [128, DEPTH, 8], F32)
        nsink = P.sbuf(es, "c_nsink", [128, 8], F32)
        kraw = P.sbuf(es, "c_kraw", [128, T], BF16)
        kr = P.sbuf(es, "c_kr", [128, NKV, TL], BF16)
        kctx = P.sbuf(es, "c_kctx", [128, NKV, TC], BF16)
        vv = P.sbuf(es, "c_v", [128, NKV, NT, 128], BF16)
        qraw = P.sbuf(es, "c_qraw", [128, T], BF16)
        qr = P.sbuf(es, "c_qr", [128, TL], BF16)
        czr = P.sbuf(es, "c_czr", [128, NT, 128], BF16)
        szh = P.sbuf(es, "c_szh", [128, NT, 128], BF16)
        t1 = P.sbuf(es, "c_t1", [128, 2, 512], F32)
        t2 = P.sbuf(es, "c_t2", [128, 2, 512], F32)
        Pt = P.sbuf(es, "c_Pt", [128, 2, 640], BF16)
        PTs = P.sbuf(es, "c_PTs", [128, 2, 5, 128], BF16)
        stt = P.sbuf(es, "c_st", [128, 4, 8], F32)
        ot = P.sbuf(es, "c_ot", [128, 4, 128], BF16)
        psR = P.psum(es, "c_psR", [128, 2, 512], F32)
        psS = P.psum(es, "c_psS", [128, 2, 2, 512], F32)
        psPT = P.psum(es, "c_psPT", [128, 2, 8, 128], BF16)
        P.dma('sp', 'c_ropeC', ropeC[:], g.ropec, writes=['c_ropeC'])
        P.dma('sp', 'c_ropeS', ropeS[:], g.ropes, writes=['c_ropeS'])
        P.dma('sp', 'c_ctmp', ctmp[:, 0:128], g.perm, writes=['c_ctmp'])
        P.op('dve', lambda e: e.tensor_copy(out=permb[:], in_=ctmp[:, 0:128]), reads=['c_ctmp'], writes=['c_permb'])
        P.dma('sp', 'c_ctmp', ctmp[:], g.band, writes=['c_ctmp'])
        P.op('dve', lambda e: e.tensor_copy(out=bandb[:], in_=ctmp[:]), reads=['c_ctmp'], writes=['c_bandb'])
        P.dma('sp', 'c_sink', sink[:], g.sink, writes=['c_sink'])
        P.op('dve', lambda e: e.tensor_scalar(out=nsink[:], in0=sink[:, l, :], scalar1=-1.0, scalar2=None,
                                               op0=ALU.mult), reads=['c_sink'], writes=['c_nsink'])
        rR, rt = Rot("c_psR", 2), Rot("c_t", 2)
        PMv = g.PM.rearrange("(n p) c -> p n c", p=128)

        def rope(dst, src, n512, skey, dkey):
            for i in range(n512):
                kR, nR = rR.next()
                kt, nt = rt.next()
                sl = slice(i * 512, (i + 1) * 512)
                P.op('pe', lambda e: e.matmul(psR[:, kR, :], lhsT=permb[:], rhs=src[:, sl], start=True, stop=True),
                     reads=[skey, 'c_permb'], writes=[nR])
                P.op('dve', lambda e: e.tensor_tensor(out=t1[:, kt, :], in0=psR[:, kR, :], in1=ropeS[:, sl],
                                                       op=ALU.mult), reads=['c_ropeS'], writes=[nR, nt + 'a'])
                P.op('pool', lambda e: e.tensor_tensor(out=t2[:, kt, :], in0=src[:, sl], in1=ropeC[:, sl],
                                                        op=ALU.mult), reads=[skey, 'c_ropeC'], writes=[nt + 'b'])
                P.op('dve', lambda e: e.tensor_tensor(out=dst[:, sl], in0=t1[:, kt, :], in1=t2[:, kt, :],
                                                       op=ALU.add), reads=[nt + 'a', nt + 'b'], writes=[dkey])

        for kv in range(NKV):
            P.dma('sp', 'c_kraw', kraw[:], g.PT[(C_CK + kv) * 128:(C_CK + kv + 1) * 128, :], writes=['c_kraw'])
            P.op('pool', lambda e: e.tensor_copy(out=kctx[:, kv, :], in_=kraw[:, 0:TC]), reads=['c_kraw'],
                 writes=['c_kctx'])
            rope(kr[:, kv, :], kraw[:, TC:], 8, 'c_kraw', 'c_kr')
            P.dma('sp', 'c_v', vv[:, kv], PMv[:, :, O_CV + kv * 128:O_CV + (kv + 1) * 128], writes=['c_v'])
        rS, rP, rPT, rst, rot = Rot("c_psS", 2), Rot("c_Pt", 2), Rot("c_PTs", 2), Rot("c_st", 4), Rot("c_ot", 4)

        def block(qT, qkey, kv, hq, kloc, nk, b0, vtiles, sz, dst):
            kS, nS = rS.next()
            kP, nP = rP.next()
            kst, nst = rst.next()
            kT, nT = rPT.next()
            kO, nO = rR.next()
            ko, no = rot.next()
            st = stt[:, kst, :]
            W = nk * 128
            npt = 'c_psPT%d' % kT

            def s1():
                if nk > 0:
                    P.op('pe', lambda e: e.matmul(psS[:, kS, 0, 0:W], lhsT=qT, rhs=kloc, start=True, stop=False),
                         reads=[qkey, 'c_kr'], writes=[nS])
                    P.op('pe', lambda e: e.matmul(psS[:, kS, 0, 0:W], lhsT=K.identb[:], rhs=bandb[:, b0:b0 + W],
                                                  start=False, stop=True), reads=['identb', 'c_bandb'], writes=[nS])
                P.op('pe', lambda e: e.matmul(psS[:, kS, 1, 0:TC], lhsT=qT, rhs=kctx[:, kv, :], start=True, stop=True),
                     reads=[qkey, 'c_kctx'], writes=[nS])
                P.op('dve', lambda e: e.tensor_reduce(out=st[:, 1:2], in_=psS[:, kS, 1, 0:TC], axis=AX.X, op=ALU.max),
                     writes=[nS, nst])
                P.op('dve', lambda e: e.tensor_scalar(out=st[:, 2:3], in0=st[:, 1:2], scalar1=-SCALE,
                                                       scalar2=nsink[:, hq:hq + 1], op0=ALU.mult, op1=ALU.min),
                     reads=['c_nsink'], writes=[nst])
                if nk > 0:
                    P.op('dve', lambda e: e.tensor_reduce(out=st[:, 0:1], in_=psS[:, kS, 0, 0:W], axis=AX.X, op=ALU.max),
                         writes=[nS, nst])
                    P.op('dve', lambda e: e.tensor_scalar(out=st[:, 2:3], in0=st[:, 0:1], scalar1=-SCALE,
                                                           scalar2=st[:, 2:3], op0=ALU.mult, op1=ALU.min),
                         writes=[nst])
                else:
                    P.op('dve', lambda e: e.memset(st[:, 3:4], 0.0), writes=[nst])

            def s2():
                if nk > 0:
                    P.op('act', lambda e: e.activation(out=Pt[:, kP, 0:W], in_=psS[:, kS, 0, 0:W], func=AF.Exp,
                                                       scale=SCALE, bias=st[:, 2:3], accum_out=st[:, 3:4]),
                         reads=[], writes=[nS, nP, nst])
                P.op('act', lambda e: e.activation(out=Pt[:, kP, 384:640], in_=psS[:, kS, 1, 0:TC], func=AF.Exp,
                                                   scale=SCALE, bias=st[:, 2:3], accum_out=st[:, 4:5]),
                     reads=[], writes=[nS, nP, nst])
                P.op('act', lambda e: e.activation(out=st[:, 5:6], in_=st[:, 2:3], func=AF.Exp, scale=1.0,
                                                   bias=sink[:, l, hq:hq + 1]), reads=['c_sink'], writes=[nst])
                P.op('dve', lambda e: e.tensor_reduce(out=st[:, 6:7], in_=st[:, 3:6], axis=AX.X, op=ALU.add),
                     writes=[nst])
                P.op('dve', lambda e: e.reciprocal(out=st[:, 7:8], in_=st[:, 6:7]), writes=[nst])
                for j in range(nk):
                    P.op('pe', lambda e: e.transpose(out=psPT[:, kT, j, :], in_=Pt[:, kP, j * 128:(j + 1) * 128],
                                                     identity=K.identb[:]), reads=[nP, 'identb'], writes=[npt])
                for j in range(2):
                    P.op('pe', lambda e: e.transpose(out=psPT[:, kT, 3 + j, :],
                                                     in_=Pt[:, kP, 384 + j * 128:384 + (j + 1) * 128],
                                                     identity=K.identb[:]), reads=[nP, 'identb'], writes=[npt])

            def s3():
                if nk == 3:
                    P.op('act', lambda e: e.activation(out=PTs[:, kT], in_=psPT[:, kT, 0:5, :], func=AF.Copy),
                         writes=[npt, nT])
                else:
                    if nk > 0:
                        P.op('act', lambda e: e.activation(out=PTs[:, kT, 0:nk], in_=psPT[:, kT, 0:nk, :], func=AF.Copy),
                             writes=[npt, nT])
                    P.op('act', lambda e: e.activation(out=PTs[:, kT, 3:5], in_=psPT[:, kT, 3:5, :], func=AF.Copy),
                         writes=[npt, nT])
                seq = [(j, vtiles[j]) for j in range(nk)] + [(3, 0), (4, 1)]
                for i, (j, vt) in enumerate(seq):
                    P.op('pe', lambda e: e.matmul(psR[:, kO, 0:128], lhsT=PTs[:, kT, j, :], rhs=vv[:, kv, vt, :],
                                                  start=(i == 0), stop=(i == len(seq) - 1)),
                         reads=[nT, 'c_v'], writes=[nO])
                P.op('dve', lambda e: e.scalar_tensor_tensor(out=ot[:, ko, :], in0=psR[:, kO, 0:128], scalar=st[:, 7:8],
                                                              in1=sz, op0=ALU.mult, op1=ALU.mult),
                     reads=[nst, 'c_szh'], writes=[nO, no])
                P.dma('sp', no, dst, ot[:, ko, :], reads=[no], writes=[])
            return (s1, s2, s3)

        def run_pipe(blocks):
            n = len(blocks)
            for i in range(n + 2):
                if i < n:
                    blocks[i][0]()
                if 0 <= i - 1 < n:
                    blocks[i - 1][1]()
                if 0 <= i - 2 < n:
                    blocks[i - 2][2]()

        for hq in range(NQ):
            kv = hq // (NQ // NKV)
            P.dma('sp', 'c_qraw', qraw[:], g.PT[(C_CQ + hq) * 128:(C_CQ + hq + 1) * 128, :], writes=['c_qraw'])
            P.dma('sp', 'c_czr', czr[:], PMv[:, :, O_CZ + hq * 128:O_CZ + (hq + 1) * 128], writes=['c_czr'])
            P.op('act', lambda e: e.activation(out=szh[:], in_=czr[:], func=AF.Silu), reads=['c_czr'], writes=['c_szh'])
            rope(qr, qraw[:, TC:], 8, 'c_qraw', 'c_qr')
            blks = []
            if with_ctx:
                for ct in range(2):
                    blks.append(block(qraw[:, ct * 128:(ct + 1) * 128], 'c_qraw', kv, hq, None, 0, 0, [], szh[:, ct, :],
                                      g.mix[ct * 128:(ct + 1) * 128, 1024 + hq * 128:1024 + (hq + 1) * 128]))
            for n in range(TL // 128):
                lo, hi = max(n - 1, 0), min(n + 1, TL // 128 - 1)
                nk = hi - lo + 1
                b0 = (lo - (n - 1)) * 128
                blks.append(block(qr[:, n * 128:(n + 1) * 128], 'c_qr', kv, hq, kr[:, kv, lo * 128:(hi + 1) * 128], nk, b0,
                                  [2 + lo + j for j in range(nk)], szh[:, 2 + n, :],
                                  g.mix[TC + n * 128:TC + (n + 1) * 128, 1024 + hq * 128:1024 + (hq + 1) * 128]))
            run_pipe(blks)
            wc_jobs(10)
        wc_jobs(1000)
        P.barrier()


def order_chunks(d, with_ctx_first=True):
    if d == 0:
        return list(range(NT))
    return [1, 0] + list(range(NT - 1, 1, -1))


def phase_mlstm(P, g, K, l):
    with ExitStack() as es:
        gcol = P.sbuf(es, "b_gcol", [128, NT, 16], F32)
        cb = P.sbuf(es, "b_cb", [128, 512], F32)
        convw = P.sbuf(es, "b_convw", [128, DEPTH, 3, 8], F32)
        maskf = P.sbuf(es, "b_maskf", [128, 2, 128], F32)
        es2 = ExitStack()
        gb = P.sbuf(es2, "b_gb", [4, DEPTH, 4], F32)
        rmask = P.sbuf(es2, "b_rmask", [4, T], F32)
        G1 = P.sbuf(es2, "b_G1", [4, T], F32)
        G2 = P.sbuf(es2, "b_G2", [4, T], F32)
        G3 = P.sbuf(es2, "b_G3", [4, T], F32)
        G4 = P.sbuf(es2, "b_G4", [4, T], F32)
        sm = P.sbuf(es2, "b_sm", [4, 6, NT + 1], F32)
        comb = P.sbuf(es2, "b_comb", [16, T], F32)
        psA = P.psum(es, "b_psA", [128, 2, 512], F32)
        psK = P.psum(es, "b_psK", [128, 2, 1024], BF16)
        psN = P.psum(es, "b_psN", [128, 2, 512], F32)
        psU = P.psum(es, "b_psU", [128, 2, 512], F32)
        P.dma('sp', 'b_gb', gb[:], g.gbias, writes=['b_gb'])
        P.dma('sp', 'b_rmask', rmask[:], g.rmask, writes=['b_rmask'])
        P.dma('sp', 'b_convw', convw[:], g.convw, writes=['b_convw'])
        P.dma('sp', 'b_maskf', maskf[:, 0, :], g.maskf, writes=['b_maskf'])
        P.dma('sp', 'b_maskf', maskf[:, 1, :], g.maskb, writes=['b_maskf'])
        v3 = lambda t: t[:].rearrange("p (c s) -> p c s", s=128)
        for d in range(2):
            ordr = order_chunks(d)
            P.dma('sp', 'b_G1', G1[:], g.PG[2 * d], writes=['b_G1'])
            P.dma('sp', 'b_G2', G2[:], g.PG[2 * d + 1], writes=['b_G2'])
            P.op('dve', lambda e: e.tensor_scalar(out=G1[:], in0=G1[:], scalar1=gb[:, l, 2 * d:2 * d + 1], scalar2=None,
                                                   op0=ALU.add), reads=['b_gb'], writes=['b_G1'])
            P.op('act', lambda e: e.activation(out=G2[:], in_=G2[:], func=AF.Sigmoid,
                                               bias=gb[:, l, 2 * d + 1:2 * d + 2], scale=1.0),
                 reads=['b_gb'], writes=['b_G2'])
            P.op('act', lambda e: e.activation(out=G2[:], in_=G2[:], func=AF.Ln), writes=['b_G2'])
            P.op('dve', lambda e: e.tensor_tensor_scan(out=G3[:], data0=rmask[:], data1=G2[:], initial=0.0,
                                                        op0=ALU.mult, op1=ALU.add),
                 reads=['b_rmask', 'b_G2'], writes=['b_G3'])
            if d == 1:
                P.op('dve', lambda e: e.tensor_tensor(out=G2[:], in0=G2[:], in1=G3[:], op=ALU.subtract),
                     reads=['b_G3'], writes=['b_G2'])
                P.op('dve', lambda e: e.tensor_tensor(out=v3(G3), in0=v3(G2),
                                                       in1=v3(G3)[:, :, 127:128].broadcast_to([4, NT, 128]),
                                                       op=ALU.add), reads=['b_G2'], writes=['b_G3'])
            P.op('dve', lambda e: e.tensor_tensor(out=G1[:], in0=G1[:], in1=G3[:], op=ALU.subtract),
                 reads=['b_G3'], writes=['b_G1'])
            P.op('dve', lambda e: e.tensor_reduce(out=sm[:, 0, 0:NT], in_=v3(G1), axis=AX.X, op=ALU.max),
                 reads=['b_G1'], writes=['b_sm'])
            ex = 127 if d == 0 else 0
            P.op('dve', lambda e: e.tensor_copy(out=sm[:, 1, 0:NT], in_=v3(G3)[:, :, ex]), reads=['b_G3'],
                 writes=['b_sm'])
            P.op('dve', lambda e: e.memset(sm[:, 2, :], 0.0), writes=['b_sm'])
            for i, c in enumerate(ordr):
                P.op('dve', lambda e: e.tensor_tensor(out=sm[:, 3, c:c + 1], in0=sm[:, 2, c:c + 1],
                                                       in1=sm[:, 0, c:c + 1], op=ALU.max), writes=['b_sm'])
                nx = ordr[i + 1] if i + 1 < NT else NT
                P.op('dve', lambda e: e.tensor_tensor(out=sm[:, 2, nx:nx + 1], in0=sm[:, 1, c:c + 1],
                                                       in1=sm[:, 3, c:c + 1], op=ALU.add), writes=['b_sm'])
            P.op('dve', lambda e: e.tensor_tensor(out=sm[:, 4, 0:NT], in0=sm[:, 2, 0:NT], in1=sm[:, 3, 0:NT],
                                                   op=ALU.subtract), writes=['b_sm'])
            P.op('act', lambda e: e.activation(out=sm[:, 4, 0:NT], in_=sm[:, 4, 0:NT], func=AF.Exp), writes=['b_sm'])
            P.dma('sp', 'b_sm', g.carry[0, d * 256:d * 256 + 4 * 64].rearrange("(h c) -> h c", c=64)[:, 0:NT],
                  sm[:, 4, 0:NT], reads=['b_sm'], writes=['carry'])
            Gb = sm[:, 3, 0:NT].unsqueeze(2).broadcast_to([4, NT, 128])
            P.op('dve', lambda e: e.tensor_tensor(out=v3(G4), in0=v3(G1), in1=Gb, op=ALU.subtract),
                 reads=['b_G1', 'b_sm'], writes=['b_G4'])
            P.op('act', lambda e: e.activation(out=G4[:], in_=G4[:], func=AF.Exp), writes=['b_G4'])
            P.op('dve', lambda e: e.tensor_tensor(out=v3(G2), in0=v3(G3), in1=Gb, op=ALU.add),
                 reads=['b_G3', 'b_sm'], writes=['b_G2'])
            P.op('act', lambda e: e.activation(out=G2[:], in_=G2[:], func=AF.Exp, scale=-1.0), writes=['b_G2'])
            P.dma('sp', 'b_comb%da' % d, comb[d * 8:d * 8 + 4, :], G4[:], reads=['b_G4'], writes=['b_comb%da' % d])
            P.dma('sp', 'b_comb%db' % d, comb[d * 8 + 4:d * 8 + 8, :], G2[:], reads=['b_G2'], writes=['b_comb%db' % d])
        for half in range(2):
            for cc in range(17):
                c = half * 17 + cc
                P.op('pe', lambda e: e.matmul(psA[:, half, cc * 16:(cc + 1) * 16], lhsT=comb[:, c * 128:(c + 1) * 128],
                                              rhs=K.identf[0:16, 0:16], start=True, stop=True),
                     reads=['b_comb0a', 'b_comb0b', 'b_comb1a', 'b_comb1b', 'identf'], writes=['b_psA%d' % half])
            P.op('dve', lambda e: e.tensor_copy(out=gcol[:, half * 17:(half + 1) * 17, :].rearrange("p c r -> p (c r)"),
                                                 in_=psA[:, half, 0:272]), writes=['b_psA%d' % half, 'b_gcol'])
        P.dma('sp', 'b_cb', cb[:], g.carry[0].partition_broadcast(128), reads=['carry'], writes=['b_cb'])
        P.barrier()
        es2.close()
        raw = P.sbuf(es, "b_raw", [128, T], BF16)
        acc = P.sbuf(es, "b_acc", [128, T], F32)
        qT = P.sbuf(es, "b_qT", [128, NB, T], BF16)
        kT = P.sbuf(es, "b_kT", [128, NB, T], BF16)
        vaug = P.sbuf(es, "b_vaug", [128, NB, NT, 129], BF16)
        Cst = P.sbuf(es, "b_C", [128, 2 * NB, 129], F32)
        Cbf = P.sbuf(es, "b_Cbf", [128, 2 * NB, 129], BF16)
        Wt = P.sbuf(es, "b_Wt", [128, 4, 128], BF16)
        kw = P.sbuf(es, "b_kw", [128, 4, 128], BF16)
        dm = P.sbuf(es, "b_dm", [128, 4, 2], F32)
        ho = P.sbuf(es, "b_ho", [128, 4, 128], F32)
        PMv = g.PM.rearrange("(n p) c -> p n c", p=128)
        for h in range(NB):
            for qk in range(2):
                ch = (C_BQ if qk == 0 else C_BK) + h
                P.dma('sp', 'b_raw', raw[:], g.PT[ch * 128:(ch + 1) * 128, :], writes=['b_raw'])
                w = lambda j: convw[:, l, j, qk * 4 + h:qk * 4 + h + 1]
                P.op('dve', lambda e: e.tensor_scalar(out=acc[:], in0=raw[:], scalar1=w(1), scalar2=None, op0=ALU.mult),
                     reads=['b_raw', 'b_convw'], writes=['b_acc'])
                for (s0, e0) in ((0, TC), (TC, T)):
                    P.op('dve', lambda e: e.scalar_tensor_tensor(out=acc[:, s0 + 1:e0], in0=raw[:, s0:e0 - 1], scalar=w(0),
                                                                  in1=acc[:, s0 + 1:e0], op0=ALU.mult, op1=ALU.add),
                         reads=['b_raw', 'b_convw'], writes=['b_acc'])
                    P.op('dve', lambda e: e.scalar_tensor_tensor(out=acc[:, s0:e0 - 1], in0=raw[:, s0 + 1:e0], scalar=w(2),
                                                                  in1=acc[:, s0:e0 - 1], op0=ALU.mult, op1=ALU.add),
                         reads=['b_raw', 'b_convw'], writes=['b_acc'])
                if qk == 0:
                    P.op('act', lambda e: e.activation(out=acc[:], in_=acc[:], func=AF.Silu), writes=['b_acc'])
                    P.op('pool', lambda e: e.tensor_scalar(out=qT[:, h, :], in0=acc[:], scalar1=SCALE, scalar2=None,
                                                            op0=ALU.mult), reads=['b_acc'], writes=['b_qT%d' % h])
                else:
                    P.op('act', lambda e: e.activation(out=kT[:, h, :], in_=acc[:], func=AF.Silu), reads=['b_acc'],
                         writes=['b_kT%d' % h])
            P.op('pool', lambda e: e.memset(vaug[:, h, :, 128:129], 1.0), writes=['b_v%d' % h])
            P.dma('sp', 'b_v%d' % h, vaug[:, h, :, 0:128], PMv[:, :, O_BV + h * 128:O_BV + (h + 1) * 128],
                  writes=['b_v%d' % h])
        P.op('pool', lambda e: e.memset(Cst[:], 0.0), writes=['b_C%d' % i for i in range(2 * NB)])
        rA, rK, rN, rU = Rot("b_psAr", 2), Rot("b_psK", 2), Rot("b_psN", 2), Rot("b_psU", 2)
        rW, rkw, rdm, rho = Rot("b_Wt", 4), Rot("b_kw", 4), Rot("b_dm", 4), Rot("b_ho", 4)
        mpend = [None]

        def mstage2(h, d, c, ci, ts, thr, car, kN, nN, kU, nU, kW, nW, kk, nkw, kd, ndm, kh, nho, qk_):
            P.op('pe', lambda e: e.matmul(psN[:, kN, 0:129], lhsT=qT[:, h, ts], rhs=Cbf[:, ci, :], start=True,
                                          stop=False), reads=[qk_, 'b_Cbf%d' % ci], writes=[nN])
            P.op('pe', lambda e: e.matmul(psN[:, kN, 0:129], lhsT=Wt[:, kW, :], rhs=vaug[:, h, c, :], start=False,
                                          stop=True), reads=[nW, 'b_v%d' % h], writes=[nN])
            P.op('pe', lambda e: e.matmul(psU[:, kU, 0:129], lhsT=kw[:, kk, :], rhs=vaug[:, h, c, :], start=True,
                                          stop=True), reads=[nkw, 'b_v%d' % h], writes=[nU])
            P.op('dve', lambda e: e.scalar_tensor_tensor(out=Cst[:, ci, :], in0=Cst[:, ci, :], scalar=car,
                                                          in1=psU[:, kU, 0:129], op0=ALU.mult, op1=ALU.add),
                 reads=['b_cb'], writes=[nU, 'b_C%d' % ci])
            P.op('dve', lambda e: e.tensor_scalar(out=dm[:, kd, 1:2], in0=psN[:, kN, 128:129], scalar1=-1.0,
                                                   scalar2=None, op0=ALU.mult), writes=[nN, ndm])
            P.op('dve', lambda e: e.scalar_tensor_tensor(out=dm[:, kd, 0:1], in0=psN[:, kN, 128:129], scalar=thr,
                                                          in1=dm[:, kd, 1:2], op0=ALU.max, op1=ALU.max),
                 reads=['b_gcol'], writes=[nN, ndm])
            P.op('dve', lambda e: e.reciprocal(out=dm[:, kd, 1:2], in_=dm[:, kd, 0:1]), writes=[ndm])
            P.op('act', lambda e: e.activation(out=ho[:, kh, :], in_=psN[:, kN, 0:128], func=AF.Copy,
                                               scale=dm[:, kd, 1:2]), reads=[ndm], writes=[nN, nho])
            P.dma('sp', nho, g.yB[d, c * 128:(c + 1) * 128, h * 128:(h + 1) * 128], ho[:, kh, :], reads=[nho],
                  writes=[])

        orders = [order_chunks(0), order_chunks(1)]
        for i in range(NT):
            for h in range(NB):
                for d in range(2):
                    c = orders[d][i]
                    ci = d * NB + h
                    ts = slice(c * 128, (c + 1) * 128)
                    ea = gcol[:, c, d * 8 + h:d * 8 + h + 1]
                    thr = gcol[:, c, d * 8 + 4 + h:d * 8 + 4 + h + 1]
                    car = cb[:, d * 256 + h * 64 + c:d * 256 + h * 64 + c + 1]
                    kA, nA = rA.next()
                    kK, nK = rK.next()
                    kN, nN = rN.next()
                    kU, nU = rU.next()
                    kW, nW = rW.next()
                    kk, nkw = rkw.next()
                    kd, ndm = rdm.next()
                    kh, nho = rho.next()
                    qk_ = 'b_qT%d' % h
                    kk_ = 'b_kT%d' % h
                    P.op('pe', lambda e: e.matmul(psA[:, kA, 0:128], lhsT=kT[:, h, ts], rhs=qT[:, h, ts], start=True,
                                                  stop=True), reads=[qk_, kk_], writes=[nA])
                    P.op('dve', lambda e: e.scalar_tensor_tensor(out=Wt[:, kW, :], in0=psA[:, kA, 0:128], scalar=ea,
                                                                  in1=maskf[:, d, :], op0=ALU.mult, op1=ALU.mult),
                         reads=['b_gcol', 'b_maskf'], writes=[nA, nW])
                    P.op('pe', lambda e: e.transpose(out=psK[:, kK, 0:128], in_=kT[:, h, ts], identity=K.identb[:]),
                         reads=[kk_, 'identb'], writes=[nK])
                    P.op('act', lambda e: e.activation(out=kw[:, kk, :], in_=psK[:, kK, 0:128], func=AF.Copy, scale=ea),
                         reads=['b_gcol'], writes=[nK, nkw])
                    P.op('act', lambda e: e.activation(out=Cbf[:, ci, :], in_=Cst[:, ci, :], func=AF.Copy, scale=car),
                         reads=['b_cb', 'b_C%d' % ci], writes=['b_Cbf%d' % ci])
                    nxt = (lambda h=h, d=d, c=c, ci=ci, ts=ts, thr=thr, car=car, kN=kN, nN=nN, kU=kU, nU=nU, kW=kW, nW=nW,
                           kk=kk, nkw=nkw, kd=kd, ndm=ndm, kh=kh, nho=nho, qk_=qk_:
                           mstage2(h, d, c, ci, ts, thr, car, kN, nN, kU, nU, kW, nW, kk, nkw, kd, ndm, kh, nho, qk_))
                    if mpend[0] is not None:
                        mpend[0]()
                    mpend[0] = nxt
        mpend[0]()
        P.barrier()


def phase_readout(P, g, K, which, with_ctx):
    with ExitStack() as es:
        y = g.yA if which == 'a' else g.yB
        ocol = O_AG if which == 'a' else O_BZ
        mcol = 0 if which == 'a' else 512
        yfb = P.sbuf(es, "r_yfb", [128, 3, 2, 512], F32)
        yf = yfb[:, :, 0, :]
        yb = yfb[:, :, 1, :]
        goz = P.sbuf(es, "r_goz", [128, 3, 2, 512], BF16)
        go = goz[:, :, 0, :]
        gz = goz[:, :, 1, :]
        sz = P.sbuf(es, "r_sz", [128, 3, 512], F32)
        so = P.sbuf(es, "r_so", [128, 3, 512], F32)
        junk = P.sbuf(es, "r_junk", [128, 128], F32)
        nhalf = P.sbuf(es, "r_nhalf", [128, 4], F32)
        P.op('pool', lambda e: e.memset(nhalf[:], -0.5), writes=['r_nhalf'])
        st = P.sbuf(es, "r_st", [128, 3, 16], F32)
        om = P.sbuf(es, "r_om", [128, 3, 512], BF16)
        rr = Rot("r", 3)
        rpend = [None]
        for tt in range(0 if with_ctx else 2, NT):
            k, n = rr.next()
            rows = slice(tt * 128, (tt + 1) * 128)
            def rs2(k=k, n=n, rows=rows):
                P.op('dve', lambda e: e.tensor_scalar(out=st[:, k, 4:8], in0=st[:, k, 0:4], scalar1=1.0 / 128, scalar2=EPS,
                                                       op0=ALU.mult, op1=ALU.add), writes=[n + 'st'])
                P.op('pool', lambda e: e.tensor_tensor(out=st[:, k, 12:16], in0=st[:, k, 4:8], in1=nhalf[:, 0:4],
                                                        op=ALU.pow), reads=['r_nhalf'], writes=[n + 'st'])
                if which == 'b':
                    P.op('act', lambda e: e.activation(out=sz[:, k, :], in_=gz[:, k, :], func=AF.Sigmoid),
                         reads=[n + 'gz'], writes=[n + 'sz'])
                    P.op('pool', lambda e: e.tensor_tensor(out=sz[:, k, :], in0=sz[:, k, :], in1=gz[:, k, :],
                                                            op=ALU.mult), reads=[n + 'gz'], writes=[n + 'sz'])
                else:
                    P.op('act', lambda e: e.activation(out=sz[:, k, :], in_=gz[:, k, :], func=AF.Silu),
                         reads=[n + 'gz'], writes=[n + 'sz'])
                for h in range(4):
                    hs = slice(h * 128, (h + 1) * 128)
                    P.op('dve', lambda e: e.scalar_tensor_tensor(out=om[:, k, hs], in0=yf[:, k, hs],
                                                                  scalar=st[:, k, 12 + h:13 + h], in1=sz[:, k, hs],
                                                                  op0=ALU.mult, op1=ALU.mult),
                         reads=[n + 'yf', n + 'st', n + 'sz'], writes=[n + 'om'])
                P.dma('pool', n + 'om', g.mix[rows, mcol:mcol + 512], om[:, k, :], reads=[n + 'om'], writes=[])
            P.dma('sp', n + 'yf', yfb[:, k], y[:, rows, :].rearrange("d p c -> p d c"), writes=[n + 'yf', n + 'yb'])
            if which == 'b':
                P.dma('sp', n + 'gz', goz[:, k].rearrange("p a c -> p (a c)"), g.PM[rows, O_BO:O_BO + 1024],
                      writes=[n + 'gz', n + 'go'])
            else:
                P.dma('sp', n + 'gz', gz[:, k, :], g.PM[rows, ocol:ocol + 512], writes=[n + 'gz'])
            P.op('dve', lambda e: e.tensor_tensor(out=yf[:, k, :], in0=yf[:, k, :], in1=yb[:, k, :], op=ALU.add),
                 reads=[n + 'yb'], writes=[n + 'yf'])
            if which == 'b':
                P.op('act', lambda e: e.activation(out=so[:, k, :], in_=go[:, k, :], func=AF.Sigmoid), reads=[n + 'go'],
                     writes=[n + 'so'])
                P.op('dve', lambda e: e.tensor_tensor(out=yf[:, k, :], in0=yf[:, k, :], in1=so[:, k, :], op=ALU.mult),
                     reads=[n + 'so'], writes=[n + 'yf'])
            for h in range(4):
                P.op('dve', lambda e: e.scalar_tensor_tensor(out=junk[:], in0=yf[:, k, h * 128:(h + 1) * 128], scalar=1.0,
                                                              in1=yf[:, k, h * 128:(h + 1) * 128], op0=ALU.mult,
                                                              op1=ALU.mult, accum_out=st[:, k, h:h + 1]),
                     reads=[n + 'yf'], writes=['r_junk', n + 'st'])
            if rpend[0] is not None:
                rpend[0]()
            rpend[0] = rs2
        rpend[0]()
        P.barrier()


NBLK = T // 32


def order_blocks(d):
    if d == 0:
        return list(range(NBLK))
    return list(range(7, -1, -1)) + list(range(NBLK - 1, 7, -1))


def phase_hgrn(P, g, K, l):
    with ExitStack() as es:
        lbl = P.sbuf(es, "a_lbl", [128, DEPTH, 8], F32)
        lb = P.sbuf(es, "a_lb", [128, 3, 8], F32)
        msk = P.sbuf(es, "a_msk", [128, 2, 128], F32)
        ones = P.sbuf(es, "a_ones", [128, T], BF16)
        zq = P.sbuf(es, "a_zq", [128, T], BF16)
        qs = P.sbuf(es, "a_qs", [128, T], BF16)
        zf = P.sbuf(es, "a_zf", [128, T], BF16)
        Bt = P.sbuf(es, "a_B", [128, T], F32)
        Ct = P.sbuf(es, "a_C", [128, T], F32)
        Dt = P.sbuf(es, "a_D", [128, T], F32)
        kk = P.sbuf(es, "a_k", [128, T], BF16)
        qt = P.sbuf(es, "a_qt", [128, 2, T], BF16)
        kt = P.sbuf(es, "a_kt", [128, 2, T], BF16)
        EH = P.sbuf(es, "a_EH", [128, 2, NBLK], F32)
        gin = P.sbuf(es, "a_gin", [128, 2, NBLK], F32)
        V = P.sbuf(es, "a_V", [32, NBLK, 128], BF16)
        R = P.sbuf(es, "a_R", [128, 2, 2, 128], F32)
        Rbf = P.sbuf(es, "a_Rbf", [128, 2, 128], BF16)
        KTM = P.sbuf(es, "a_KTM", [32, 2, 2, 4, 128], BF16)
        Wt = P.sbuf(es, "a_Wt", [32, 2, 2, 4, 32], BF16)
        osb = P.sbuf(es, "a_osb", [32, 2, 2, 4, 128], F32)
        psKT = P.psum(es, "a_psKT", [128, 2, 1024], BF16)
        psU = P.psum(es, "a_psU", [128, 2, 512], F32)
        psS = P.psum(es, "a_psS", [128, 2, 512], F32)
        psO = P.psum(es, "a_psO", [128, 2, 512], F32)
        P.dma('sp', 'a_lbl', lbl[:], g.lbl, writes=['a_lbl'])
        P.dma('sp', 'a_msk', msk[:, 0, :], g.maskf, writes=['a_msk'])
        P.dma('sp', 'a_msk', msk[:, 1, :], g.maskb, writes=['a_msk'])
        P.op('pool', lambda e: e.memset(ones[:], 1.0), writes=['a_ones'])
        if l == 0:
            P.op('dve', lambda e: e.memset(lb[:, 0, :], 0.0), writes=['a_lb'])
        else:
            P.op('dve', lambda e: e.tensor_tensor(out=lb[:, 0, :], in0=lbl[:, 1, :], in1=lbl[:, 0, :], op=ALU.subtract),
                 reads=['a_lbl'], writes=['a_lb'])
            P.op('act', lambda e: e.activation(out=lb[:, 0, :], in_=lb[:, 0, :], func=AF.Sigmoid), writes=['a_lb'])
        P.op('dve', lambda e: e.tensor_scalar(out=lb[:, 1, :], in0=lb[:, 0, :], scalar1=-1.0, scalar2=1.0, op0=ALU.mult,
                                               op1=ALU.add), writes=['a_lb'])
        P.op('dve', lambda e: e.tensor_scalar(out=lb[:, 2, :], in0=lb[:, 1, :], scalar1=-1.0, scalar2=None,
                                               op0=ALU.mult), writes=['a_lb'])
        b3 = lambda t: t.rearrange("p (c s) -> p c s", s=32)
        PMb = g.PM.rearrange("(c s) w -> s c w", s=32)
        for h in range(NA):
            P.dma('sp', 'a_zq', zq[:], g.PT[(C_AQ + h) * 128:(C_AQ + h + 1) * 128, :], writes=['a_zq'])
            P.op('act', lambda e: e.activation(out=qs[:], in_=zq[:], func=AF.Silu), reads=['a_zq'], writes=['a_qs'])
            P.dma('sp', 'a_V', V[:], PMb[:, :, O_AI + h * 128:O_AI + (h + 1) * 128], writes=['a_V'])
            for d in range(2):
                ch = (C_AFF if d == 0 else C_AFB) + h
                col = d * 4 + h
                lbc, om, nom = lb[:, 0, col:col + 1], lb[:, 1, col:col + 1], lb[:, 2, col:col + 1]
                P.dma('sp', 'a_zf', zf[:], g.PT[ch * 128:(ch + 1) * 128, :], writes=['a_zf'])
                P.op('act', lambda e: e.activation(out=Bt[:], in_=zf[:], func=AF.Sigmoid), reads=['a_zf'], writes=['a_B'])
                P.op('dve', lambda e: e.tensor_scalar(out=Ct[:], in0=Bt[:], scalar1=om, scalar2=lbc, op0=ALU.mult,
                                                       op1=ALU.add), reads=['a_B', 'a_lb'], writes=['a_C'])
                P.op('act', lambda e: e.activation(out=Ct[:], in_=Ct[:], func=AF.Ln), writes=['a_C'])
                P.op('dve', lambda e: e.tensor_scalar(out=kk[:], in0=Bt[:], scalar1=nom, scalar2=om, op0=ALU.mult,
                                                       op1=ALU.add), reads=['a_B', 'a_lb'], writes=['a_k'])
                P.op('dve', lambda e: e.tensor_tensor_scan(out=Dt[:], data0=ones[:], data1=Ct[:], initial=0.0,
                                                            op0=ALU.mult, op1=ALU.add),
                     reads=['a_ones', 'a_C'], writes=['a_D'])
                P.op('dve', lambda e: e.tensor_tensor(out=Ct[:], in0=Dt[:], in1=Ct[:], op=ALU.subtract), reads=['a_D'],
                     writes=['a_C'])
                D3, X3, B3 = b3(Dt[:]), b3(Ct[:]), b3(Bt[:])
                if d == 0:
                    P.op('dve', lambda e: e.tensor_tensor(out=B3, in0=D3, in1=D3[:, :, 15:16].broadcast_to([128, NBLK, 32]),
                                                           op=ALU.subtract), reads=['a_D'], writes=['a_B'])
                    P.op('dve', lambda e: e.tensor_copy(out=EH[:, 0, :], in_=B3[:, :, 31]), reads=['a_B'], writes=['a_EH'])
                    P.op('dve', lambda e: e.tensor_tensor(out=EH[:, 1, :], in0=D3[:, :, 15], in1=X3[:, :, 0],
                                                           op=ALU.subtract), reads=['a_D', 'a_C'], writes=['a_EH'])
                    P.op('dve', lambda e: e.tensor_tensor(out=gin[:, d, 1:NBLK], in0=EH[:, 0, 0:NBLK - 1],
                                                           in1=EH[:, 1, 1:NBLK], op=ALU.add), writes=['a_EH', 'a_gin%d' % d])
                    P.op('dve', lambda e: e.memset(gin[:, d, 0:1], 0.0), writes=['a_gin%d' % d])
                else:
                    P.op('dve', lambda e: e.tensor_tensor(out=B3, in0=X3[:, :, 16:17].broadcast_to([128, NBLK, 32]),
                                                           in1=X3, op=ALU.subtract), reads=['a_C'], writes=['a_B'])
                    P.op('dve', lambda e: e.tensor_copy(out=EH[:, 0, :], in_=B3[:, :, 0]), reads=['a_B'], writes=['a_EH'])
                    P.op('dve', lambda e: e.tensor_tensor(out=EH[:, 1, :], in0=D3[:, :, 31], in1=X3[:, :, 16],
                                                           op=ALU.subtract), reads=['a_D', 'a_C'], writes=['a_EH'])
                    P.op('dve', lambda e: e.memset(gin[:, d, 7:8], 0.0), writes=['a_gin%d' % d])
                    P.op('dve', lambda e: e.tensor_tensor(out=gin[:, d, 0:7], in0=EH[:, 0, 1:8], in1=EH[:, 1, 0:7],
                                                           op=ALU.add), writes=['a_EH', 'a_gin%d' % d])
                    P.op('dve', lambda e: e.tensor_tensor(out=gin[:, d, 8:NBLK - 1], in0=EH[:, 0, 9:NBLK],
                                                           in1=EH[:, 1, 8:NBLK - 1], op=ALU.add),
                         writes=['a_EH', 'a_gin%d' % d])
                    P.op('dve', lambda e: e.tensor_tensor(out=gin[:, d, NBLK - 1:NBLK], in0=EH[:, 0, 0:1],
                                                           in1=EH[:, 1, NBLK - 1:NBLK], op=ALU.add),
                         writes=['a_EH', 'a_gin%d' % d])
                P.op('act', lambda e: e.activation(out=gin[:, d, :], in_=gin[:, d, :], func=AF.Exp),
                     writes=['a_gin%d' % d])
                P.op('act', lambda e: e.activation(out=Dt[:], in_=Bt[:], func=AF.Exp), reads=['a_B'], writes=['a_D'])
                P.op('act', lambda e: e.activation(out=Ct[:], in_=Bt[:], func=AF.Exp, scale=-1.0), reads=['a_B'],
                     writes=['a_C'])
                P.op('dve', lambda e: e.tensor_tensor(out=qt[:, d, :], in0=qs[:], in1=Dt[:], op=ALU.mult),
                     reads=['a_qs', 'a_D'], writes=['a_qt%d' % d])
                P.op('dve', lambda e: e.tensor_tensor(out=kt[:, d, :], in0=kk[:], in1=Ct[:], op=ALU.mult),
                     reads=['a_k', 'a_C'], writes=['a_kt%d' % d])
            P.op('pool', lambda e: e.memset(R[:], 0.0), writes=['a_R0_0', 'a_R0_1', 'a_R1_0', 'a_R1_1'])
            orders = [order_blocks(0), order_blocks(1)]
            rK = [Rot("a_KTM0_", 2), Rot("a_KTM1_", 2)]
            rW = [Rot("a_Wt0_", 2), Rot("a_Wt1_", 2)]
            ro = [Rot("a_osb0_", 2), Rot("a_osb1_", 2)]
            cur = [None, None]
            curo = [None, None]
            apend = [None]
            for i in range(NBLK):
                pp = i % 2
                for d in range(2):
                    c = orders[d][i]
                    tile_, j = c // 4, c % 4
                    bs = slice(c * 32, (c + 1) * 32)
                    nqt, nkt = 'a_qt%d' % d, 'a_kt%d' % d
                    if cur[d] is None or cur[d][0] != tile_:
                        kK, nK = rK[d].next()
                        for jj in range(4):
                            cs = slice(tile_ * 128 + jj * 32, tile_ * 128 + (jj + 1) * 32)
                            P.op('pe', lambda e: e.transpose(out=psKT[0:32, d, jj * 128:(jj + 1) * 128], in_=kt[:, d, cs],
                                                             identity=K.identb[:]), reads=[nkt, 'identb'],
                                 writes=['a_psKT%d' % d])
                        P.op('act', lambda e: e.activation(out=KTM[:, d, kK].rearrange("p j c -> p (j c)"),
                                                           in_=psKT[0:32, d, 0:512], func=AF.Copy),
                             writes=['a_psKT%d' % d, nK])
                        for jj in range(4):
                            cs = slice(tile_ * 128 + jj * 32, tile_ * 128 + (jj + 1) * 32)
                            P.op('pe', lambda e: e.matmul(psS[0:32, d, jj * 32:(jj + 1) * 32], lhsT=kt[:, d, cs],
                                                          rhs=qt[:, d, cs], start=True, stop=True),
                                 reads=[nqt, nkt], writes=['a_psS%d' % d])
                        kW, nW = rW[d].next()
                        P.op('dve', lambda e: e.tensor_tensor(
                            out=Wt[:, d, kW], in0=psS[0:32, d, 0:128].rearrange("p (j t) -> p j t", t=32),
                            in1=msk[0:32, d, 0:32].unsqueeze(1).broadcast_to([32, 4, 32]), op=ALU.mult),
                            reads=['a_msk'], writes=['a_psS%d' % d, nW])
                        ko, no = ro[d].next()
                        cur[d] = (tile_, kK, nK, kW, nW)
                        curo[d] = (ko, no, 0)
                    _, kK, nK, kW, nW = cur[d]
                    ko, no, cnt = curo[d]
                    P.op('pe', lambda e: e.matmul(psU[:, d, 0:128], lhsT=KTM[:, d, kK, j, :], rhs=V[:, c, :], start=True,
                                                  stop=True), reads=[nK, 'a_V'], writes=['a_psU%d' % d])
                    P.op('act', lambda e: e.activation(out=Rbf[:, d, :], in_=R[:, d, pp, :], func=AF.Copy,
                                                       scale=gin[:, d, c:c + 1]),
                         reads=['a_R%d_%d' % (d, pp), 'a_gin%d' % d], writes=['a_Rbf%d' % d])
                    cnt += 1
                    curo[d] = (ko, no, cnt)

                    def st2(d=d, c=c, bs=bs, nqt=nqt, kW=kW, nW=nW, ko=ko, no=no, j=j, cnt=cnt, tile_=tile_, h=h, pp=pp):
                        P.op('pe', lambda e: e.matmul(psO[0:32, d, j * 128:(j + 1) * 128], lhsT=qt[:, d, bs], rhs=Rbf[:, d, :],
                                                      start=True, stop=False),
                             reads=[nqt, 'a_Rbf%d' % d], writes=['a_psO%d' % d])
                        P.op('pe', lambda e: e.matmul(psO[0:32, d, j * 128:(j + 1) * 128], lhsT=Wt[:, d, kW, j, :],
                                                      rhs=V[:, c, :], start=False, stop=True),
                             reads=[nW, 'a_V'], writes=['a_psO%d' % d])
                        P.op('dve', lambda e: e.scalar_tensor_tensor(out=R[:, d, 1 - pp, :], in0=R[:, d, pp, :],
                                                                      scalar=gin[:, d, c:c + 1], in1=psU[:, d, 0:128],
                                                                      op0=ALU.mult, op1=ALU.add),
                             reads=['a_gin%d' % d, 'a_R%d_%d' % (d, pp)],
                             writes=['a_psU%d' % d, 'a_R%d_%d' % (d, 1 - pp)])
                        if cnt == 4:
                            P.op('dve', lambda e: e.tensor_copy(out=osb[:, d, ko].rearrange("p j v -> p (j v)"),
                                                                 in_=psO[0:32, d, 0:512]),
                                 writes=['a_psO%d' % d, no])
                            P.dma('sp', no, g.yA[d, tile_ * 128:(tile_ + 1) * 128, h * 128:(h + 1) * 128].rearrange(
                                "(j s) v -> s j v", s=32), osb[:, d, ko], reads=[no], writes=[])
                    if apend[0] is not None:
                        apend[0]()
                    apend[0] = st2
            apend[0]()
            apend[0] = None
        P.barrier()


def phase_outproj(P, g, K, l, src, with_ctx, final):
    with ExitStack() as es:
        wo = P.sbuf(es, "o_wo", [128, 4, KC, 512], BF16)
        gg = P.sbuf(es, "o_gg", [128, 2, D], F32)
        mt = P.sbuf(es, "o_mt", [128, 2, D], BF16)
        mixT = P.sbuf(es, "o_mixT", [128, 2, KC, 128], BF16)
        xt = P.sbuf(es, "o_xt", [128, 2, D], F32)
        tmp = P.sbuf(es, "o_tmp", [128, 2, D], F32)
        ysb = P.sbuf(es, "o_ysb", [128, 2, D], F32)
        junk = P.sbuf(es, "o_junk", [128, 512], BF16)
        st = P.sbuf(es, "o_st", [128, 2, 8], F32)
        psT = P.psum(es, "o_psT", [128, 2, 8, 128], BF16)
        psY = P.psum(es, "o_psY", [128, 4, 512], F32)
        for jc in range(4):
            P.dma('sp', 'o_wo%d' % jc, wo[:, jc].rearrange("p k c -> p (k c)"), g.wo_b[l, jc, :, :], writes=['o_wo%d' % jc])
        P.dma('sp', 'o_gg0', gg[:, 0, :], g.ggd[l, 0], writes=['o_gg0'])
        P.dma('sp', 'o_gg1', gg[:, 1, :], g.ggd[l, 1], writes=['o_gg1'])
        rr = Rot("o", 2)
        opend = [None]
        for tt in range(0 if with_ctx else 2, NT):
            k, n = rr.next()
            jj = 1 if tt < 2 else 0
            rows = slice(tt * 128, (tt + 1) * 128)
            P.dma('sp', n + 'mt', mt[:, k, :], g.mix[rows, :], writes=[n + 'mt'])
            P.dma('sp', n + 'xt', xt[:, k, :], src[rows, :], writes=[n + 'xt'])
            for q2 in range(2):
                for i in range(8):
                    kc = q2 * 8 + i
                    P.op('pe', lambda e: e.transpose(out=psT[:, q2, i, :], in_=mt[:, k, kc * 128:(kc + 1) * 128],
                                                     identity=K.identb[:]), reads=[n + 'mt', 'identb'],
                         writes=['o_psT%d' % q2])
                if q2 == 0:
                    P.op('dve', lambda e: e.tensor_copy(out=mixT[:, k, 0:8, :], in_=psT[:, 0]),
                         writes=['o_psT0', n + 'mixTa'])
                else:
                    P.op('act', lambda e: e.activation(out=mixT[:, k, 8:16, :], in_=psT[:, 1], func=AF.Copy),
                         writes=['o_psT1', n + 'mixTb'])
            def oB(k=k, n=n, jj=jj, rows=rows, tt=tt):
                for jc in range(4):
                    for kc in range(KC):
                        P.op('pe', lambda e: e.matmul(psY[:, jc, :], lhsT=mixT[:, k, kc, :], rhs=wo[:, jc, kc, :],
                                                      start=(kc == 0), stop=(kc == KC - 1)),
                             reads=[n + 'mixTa', n + 'mixTb', 'o_wo%d' % jc], writes=['o_psY%d' % jc])
                for jc in range(4):
                    P.op('act', lambda e: e.activation(out=junk[:], in_=psY[:, jc, :], func=AF.Square,
                                                       accum_out=st[:, k, jc:jc + 1]),
                         writes=['o_psY%d' % jc, 'o_junk', n + 'st'])
                    P.op('dve', lambda e: e.tensor_copy(out=ysb[:, k, jc * 512:(jc + 1) * 512], in_=psY[:, jc, :]),
                         writes=['o_psY%d' % jc, n + 'ysb%d' % jc])
                P.op('dve', lambda e: e.tensor_reduce(out=st[:, k, 4:5], in_=st[:, k, 0:4], axis=AX.X, op=ALU.add),
                     writes=[n + 'st'])
                P.op('dve', lambda e: e.tensor_scalar(out=st[:, k, 5:6], in0=st[:, k, 4:5], scalar1=1.0 / D, scalar2=EPS,
                                                       op0=ALU.mult, op1=ALU.add), writes=[n + 'st'])
                P.op('act', lambda e: e.activation(out=st[:, k, 6:7], in_=st[:, k, 5:6], func=AF.Sqrt), writes=[n + 'st'])
                P.op('dve', lambda e: e.reciprocal(out=st[:, k, 7:8], in_=st[:, k, 6:7]), writes=[n + 'st'])
                for jc in range(4):
                    cs = slice(jc * 512, (jc + 1) * 512)
                    P.op('dve', lambda e: e.scalar_tensor_tensor(out=tmp[:, k, cs], in0=ysb[:, k, cs], scalar=st[:, k, 7:8],
                                                                  in1=gg[:, jj, cs], op0=ALU.mult, op1=ALU.mult),
                         reads=[n + 'st', 'o_gg%d' % jj, n + 'ysb%d' % jc], writes=[n + 'tmp'])
                P.op('pool', lambda e: e.tensor_tensor(out=tmp[:, k, :], in0=tmp[:, k, :], in1=xt[:, k, :], op=ALU.add),
                     reads=[n + 'xt'], writes=[n + 'tmp'])
                if final:
                    dst = g.out[(tt - 2) * 128:(tt - 1) * 128, :]
                else:
                    dst = g.xs[rows, :]
                P.dma('pool', n + 'tmp', dst, tmp[:, k, :], reads=[n + 'tmp'], writes=[])
            if opend[0] is not None:
                opend[0]()
            opend[0] = oB
        opend[0]()
        P.barrier()


def build(dbg=False, phases=None):
    nc = bass.Bass("TRN2", target_bir_lowering=False)
    g = declare(nc, dbg)
    if phases is None:
        phases = ("w", "m", "i0", "a0", "b0", "c0", "o0", "i1", "a1", "b1", "c1", "o1")
    with ExitStack() as es:
        P = Prog(nc, es)
        K = setup_consts(P, g, es)
        if "w" in phases:
            phase_wcast(P, g, layers=(0,) if "c0" in phases else (0, 1))
        if "m" in phases:
            phase_mod(P, g, K)
        for l in range(DEPTH):
            wc = (l < DEPTH - 1)
            src = g.xin if l == 0 else g.xs
            if "i%d" % l in phases:
                phase_inproj(P, g, K, l, src)
            if "a%d" % l in phases:
                phase_hgrn(P, g, K, l)
                phase_readout(P, g, K, 'a', wc)
            if "b%d" % l in phases:
                phase_mlstm(P, g, K, l)
                phase_readout(P, g, K, 'b', wc)
            if "c%d" % l in phases:
                phase_attn(P, g, K, l, wc, cast_layers=(1,) if (l == 0 and "w" in phases) else ())
            if "o%d" % l in phases:
                phase_outproj(P, g, K, l, src, wc, l == DEPTH - 1)
        P.finish()
        print("instructions", P.ninst, "waits", P.nwait, P.cnt, "dma sems", len(P.dsem), flush=True)
    return nc


_NC_CACHE = {}


def kernel(**inputs):
    inputs = {k: np.asarray(v) for k, v in inputs.items()}
    per = host_prep(**inputs)
    if 'nc' not in _NC_CACHE:
        _NC_CACHE['nc'] = build()
    nc = _NC_CACHE['nc']
    n = len(per)
    res = run_bass_kernel_spmd(nc, per, core_ids=list(range(n)))
    out = np.stack([np.asarray(res.results[i]['out']) for i in range(n)], axis=0)
    return out.astype(np.float32, copy=False)
```

```python
import numpy as np
from contextlib import ExitStack
import concourse.bass as bass
import concourse.mybir as mybir
from concourse.bass_utils import run_bass_kernel_spmd

F32 = mybir.dt.float32
BF16 = mybir.dt.bfloat16
AF = mybir.ActivationFunctionType
ALU = mybir.AluOpType
AX = mybir.AxisListType

D = 2048
TC = 256
TL = 4096
T = TC + TL
NT = T // 128
KC = D // 128
DEPTH = 2
NA, NB, NQ, NKV = 4, 4, 8, 2
EPS = 1e-6
NFM = 31
NTM = 8
TMW = 3840
O_AI, O_AG, O_BV, O_BO, O_BZ, O_CZ, O_CV = 0, 512, 1024, 1536, 2048, 2560, 3584
C_AQ, C_AFF, C_AFB, C_BQ, C_BK, C_CQ, C_CK, C_G = 0, 4, 8, 12, 16, 20, 28, 30
GROUPS = [(0, 256)] + [(256 + 1024 * i, 1024) for i in range(4)]
NEGM = -30000.0


class Prog:
    def __init__(self, nc, es):
        self.nc = nc
        self.es = es
        self.eng = {'pe': nc.tensor, 'act': nc.scalar, 'dve': nc.vector, 'pool': nc.gpsimd, 'sp': nc.sync}
        self.esem = {e: es.enter_context(nc.semaphore("s_" + e)) for e in ('pe', 'act', 'dve', 'pool')}
        self.cnt = {e: 0 for e in self.esem}
        self.waited = {e: {} for e in self.eng}
        self.last_w = {}
        self.readers = {}
        self.dsem = {}
        self.dcnt = {}
        self.sems = {}
        self.nwait = 0
        self.ninst = 0

    def sbuf(self, es, name, shape, dt):
        self.uid = getattr(self, 'uid', 0) + 1
        return es.enter_context(self.nc.sbuf_tensor("%s_%d" % (name, self.uid), shape, dt))

    def psum(self, es, name, shape, dt):
        self.uid = getattr(self, 'uid', 0) + 1
        return es.enter_context(self.nc.psum_tensor("%s_%d" % (name, self.uid), shape, dt))

    def _wait(self, e, toks):
        need = {}
        for t in toks:
            if t is None:
                continue
            s, v = t
            if v > need.get(id(s), (s, 0))[1]:
                need[id(s)] = (s, v)
        for k, (s, v) in need.items():
            if self.waited[e].get(k, 0) < v:
                self.eng[e].wait_ge(s, v)
                self.waited[e][k] = v
                self.nwait += 1

    def _deps(self, e, reads, writes, own=None):
        toks = []
        for r in reads:
            toks.append(self.last_w.get(r))
        for w in writes:
            lw = self.last_w.get(w)
            if lw is not None and not (own is not None and lw[0] is own):
                if not (e == 'pe' and lw[0] is self.esem[e]):
                    toks.append(lw)
            for t in self.readers.get(w, ()):
                if e == 'pe' and t[0] is self.esem[e]:
                    continue
                toks.append(t)
        return toks

    def _commit(self, tok, reads, writes):
        for r in reads:
            self.readers.setdefault(r, []).append(tok)
        for w in writes:
            self.last_w[w] = tok
            self.readers[w] = []

    def op(self, e, fn, reads=(), writes=()):
        self._wait(e, self._deps(e, reads, writes))
        inst = fn(self.eng[e])
        self.cnt[e] += 1
        self.ninst += 1
        inst.then_inc(self.esem[e], 1)
        tok = (self.esem[e], self.cnt[e])
        self._commit(tok, reads, writes)
        return tok

    def dma(self, q, slot, out, in_, reads=(), writes=()):
        if slot not in self.dsem:
            fl = getattr(self, 'free_sems_' + q, None)
            if fl:
                self.dsem[slot] = fl.pop()
                self.dq[slot] = q
            else:
                if not hasattr(self, 'dq'):
                    self.dq = {}
                self.dq[slot] = q
                self.nsem = getattr(self, 'nsem', 0) + 1
                sm = self.es.enter_context(self.nc.semaphore("d_%d" % self.nsem))
                self.dsem[slot] = sm
                self.dcnt[id(sm)] = 0
        s = self.dsem[slot]
        self._wait(q, self._deps(q, reads, writes, own=s))
        self.dcnt[id(s)] += 16
        self.ninst += 1
        self.eng[q].dma_start(out=out, in_=in_).then_inc(s, 16)
        tok = (s, self.dcnt[id(s)])
        self._commit(tok, reads, writes)
        return tok

    def barrier(self):
        toks = [(self.esem[e], self.cnt[e]) for e in self.esem if self.cnt[e] > 0]
        toks += [(self.dsem[s], self.dcnt[id(self.dsem[s])]) for s in self.dsem]
        for e in self.eng:
            self._wait(e, toks)
        self.last_w = {}
        self.readers = {}
        for slot, sm in self.dsem.items():
            q = self.dq[slot]
            if not hasattr(self, 'free_sems_' + q):
                setattr(self, 'free_sems_' + q, [])
            getattr(self, 'free_sems_' + q).append(sm)
        self.dsem = {}
        self.dq = {}
        if not hasattr(self, 'marks'):
            self.marks = []
        self.marks.append(dict(self.cnt))

    def finish(self):
        toks = [(self.esem[e], self.cnt[e]) for e in self.esem if self.cnt[e] > 0]
        toks += [(self.dsem[s], self.dcnt[id(self.dsem[s])]) for s in self.dsem]
        self._wait('sp', toks)


class Rot:
    def __init__(self, name, n):
        self.name, self.n, self.i = name, n, 0

    def next(self):
        k = self.i % self.n
        self.i += 1
        return k, "%s%d" % (self.name, k)


def dram(nc, name, shape, dt, kind="Internal"):
    return nc.dram_tensor(name, list(shape), dt, kind=kind).ap()


def _proj_offsets():
    off, start = {}, 0
    for name, width in (('a_q', 512), ('a_f_fwd', 512), ('a_f_bwd', 512), ('a_i', 512), ('a_gate', 512),
                        ('b_q', 512), ('b_k', 512), ('b_v', 512), ('b_o', 512), ('b_gates', 16), ('b_z', 512),
                        ('c_q', 1024), ('c_k', 256), ('c_v', 256), ('c_z', 1024)):
        off[name] = start
        start += width
    return off


def host_consts():
    c = {}
    c['ident'] = np.eye(128, dtype=np.float32)
    s = np.arange(128)[:, None]
    t = np.arange(128)[None, :]
    c['maskf'] = (s <= t).astype(np.float32)
    c['maskb'] = (s >= t).astype(np.float32)
    qi = np.arange(128)[:, None]
    kj = np.arange(384)[None, :]
    c['band'] = np.where(np.abs(kj - qi - 128) <= 128, 0.0, NEGM).astype(np.float32)
    perm = np.zeros((128, 128), np.float32)
    for dp in range(128):
        blk = dp // 32
        src = dp + 32 if blk % 2 == 0 else dp - 32
        perm[src, dp] = 1.0
    c['perm'] = perm
    inv_freq = (np.float32(10000.0) ** (-np.arange(0, 64, 2, dtype=np.float32) / np.float32(64))).astype(np.float32)
    pos = np.arange(TL)
    row = (pos // 64).astype(np.float32)
    col = (pos % 64).astype(np.float32)
    ang_r = (row[:, None] * inv_freq[None, :]).astype(np.float32)
    ang_c = (col[:, None] * inv_freq[None, :]).astype(np.float32)
    cos = np.zeros((128, TL), np.float32)
    sin = np.zeros((128, TL), np.float32)
    for d in range(128):
        ang = ang_r if d < 64 else ang_c
        j = d % 32
        cos[d] = np.cos(ang[:, j])
        sg = -1.0 if (d // 32) % 2 == 0 else 1.0
        sin[d] = sg * np.sin(ang[:, j])
    c['ropec'] = cos
    c['ropes'] = sin
    rm = np.ones((4, T), np.float32)
    rm[:, ::128] = 0.0
    c['rmask'] = rm
    return c


def host_prep(x, c, ctx, c_ctx, w_mod, b_mod, g_pre, g_post, w_in, hgrn_lb_logits,
              mlstm_conv_w, mlstm_gate_bias, attn_sink, w_out):
    f = np.float32
    off = _proj_offsets()
    shared = {}
    shared['wmod'] = np.ascontiguousarray(
        w_mod.reshape(DEPTH, KC, 128, 12, 512).transpose(0, 3, 2, 1, 4)).astype(f, copy=False)
    shared['bm_fm'] = np.ascontiguousarray(b_mod[:, :4096].reshape(DEPTH, 32, 128).transpose(2, 0, 1))
    shared['bm_g'] = np.ascontiguousarray(np.broadcast_to(b_mod[None, :, 4096:], (128, DEPTH, 2048)))
    shared['gpre'] = np.ascontiguousarray(g_pre.reshape(DEPTH, KC, 128).transpose(2, 0, 1))
    shared['gpost'] = np.ascontiguousarray(np.broadcast_to(g_post[None], (128, DEPTH, 2048)))
    fm_cols = []
    for nm, nh in (('a_q', 4), ('a_f_fwd', 4), ('a_f_bwd', 4), ('b_q', 4), ('b_k', 4), ('c_q', 8), ('c_k', 2)):
        for h in range(nh):
            fm_cols.append(np.arange(off[nm] + h * 128, off[nm] + (h + 1) * 128))
    fm_cols = np.concatenate(fm_cols)
    gcols = np.arange(off['b_gates'], off['b_gates'] + 16)
    wfm = np.zeros((DEPTH, D, NFM * 128), f)
    wfm[:, :, :30 * 128] = w_in[:, :, fm_cols]
    wfm[:, :, 30 * 128:30 * 128 + 16] = w_in[:, :, gcols]
    shared['wfm'] = np.ascontiguousarray(wfm.reshape(DEPTH, KC, 128, NFM, 128).transpose(0, 3, 2, 1, 4)).reshape(DEPTH, NFM, 128, KC * 128)
    tm_cols = np.concatenate([np.arange(off[nm], off[nm] + w) for nm, w in
                              (('a_i', 512), ('a_gate', 512), ('b_v', 512), ('b_o', 512), ('b_z', 512),
                               ('c_z', 1024), ('c_v', 256))])
    wtm = np.zeros((DEPTH, D, NTM * 512), f)
    wtm[:, :, :TMW] = w_in[:, :, tm_cols]
    shared['wtm'] = np.ascontiguousarray(wtm.reshape(DEPTH, KC, 128, NTM, 512).transpose(0, 3, 2, 1, 4)).reshape(DEPTH, NTM, 128, KC * 512)
    shared['wo'] = np.ascontiguousarray(w_out.reshape(DEPTH, KC, 128, 4, 512).transpose(0, 3, 2, 1, 4)).reshape(DEPTH, 4, 128, KC * 512)
    shared['lbl'] = np.ascontiguousarray(hgrn_lb_logits.reshape(DEPTH, 8, 128).transpose(2, 0, 1))
    shared['convw'] = np.ascontiguousarray(mlstm_conv_w.reshape(DEPTH, 3, 8, 128).transpose(3, 0, 1, 2))
    shared['gbias'] = np.ascontiguousarray(mlstm_gate_bias.reshape(DEPTH, 4, 4).transpose(2, 0, 1))
    shared['sink'] = np.ascontiguousarray(np.broadcast_to(attn_sink[None], (128, DEPTH, 8)))
    shared.update(host_consts())
    per = []
    for b in range(x.shape[0]):
        d = dict(shared)
        d['xin'] = np.ascontiguousarray(np.concatenate([ctx[b], x[b]], axis=0))
        cT = np.stack([c[b].reshape(KC, 128).T, c_ctx.reshape(KC, 128).T], axis=-1)
        d['cT'] = np.ascontiguousarray(cT.astype(f))
        per.append(d)
    return per


class Ctx:
    pass


def declare(nc, dbg=False):
    g = Ctx()
    I = lambda n, s, dt=F32: dram(nc, n, s, dt, "ExternalInput")
    g.xin = I("xin", [T, D])
    g.cT = I("cT", [128, KC, 2])
    g.wmod = I("wmod", [DEPTH, 12, 128, KC, 512])
    g.bm_fm = I("bm_fm", [128, DEPTH, 32])
    g.bm_g = I("bm_g", [128, DEPTH, 2048])
    g.gpre = I("gpre", [128, DEPTH, KC])
    g.gpost = I("gpost", [128, DEPTH, 2048])
    g.wfm = I("wfm", [DEPTH, NFM, 128, KC * 128])
    g.wtm = I("wtm", [DEPTH, NTM, 128, KC * 512])
    g.wo = I("wo", [DEPTH, 4, 128, KC * 512])
    g.lbl = I("lbl", [128, DEPTH, 8])
    g.convw = I("convw", [128, DEPTH, 3, 8])
    g.gbias = I("gbias", [4, DEPTH, 4])
    g.sink = I("sink", [128, DEPTH, 8])
    g.ident = I("ident", [128, 128])
    g.maskf = I("maskf", [128, 128])
    g.maskb = I("maskb", [128, 128])
    g.band = I("band", [128, 384])
    g.perm = I("perm", [128, 128])
    g.ropec = I("ropec", [128, TL])
    g.ropes = I("ropes", [128, TL])
    g.rmask = I("rmask", [4, T])
    g.out = dram(nc, "out", [TL, D], F32, "ExternalOutput")
    dbgset = set(dbg) if dbg else set()
    S = lambda n, s, dt: dram(nc, n, s, dt, "ExternalOutput" if n in dbgset else "Internal")
    g.xs = S("xs", [T, D], F32)
    g.wfm_b = S("wfm_b", [DEPTH, NFM, 128, KC * 128], BF16)
    g.wtm_b = S("wtm_b", [DEPTH, NTM, 128, KC * 512], BF16)
    g.wo_b = S("wo_b", [DEPTH, 4, 128, KC * 512], BF16)
    g.PT = S("PT", [30 * 128, T], BF16)
    g.PG = S("PG", [4, 4, T], F32)
    g.PM = S("PM", [T, TMW], BF16)
    g.mix = S("mix", [T, 2048], BF16)
    g.yA = S("yA", [2, T, 512], F32)
    g.yB = S("yB", [2, T, 512], F32)
    g.ggd = S("ggd", [DEPTH, 2, 128, 2048], F32)
    g.gsd = S("gsd", [DEPTH, 2, 128, KC, 2], F32)
    g.carry = S("carry", [1, 512], F32)
    return g


def wcast_jobs(g, layers):
    jobs = []
    for l in layers:
        for c in range(NFM):
            jobs.append((g.wfm[l, c, :, :], g.wfm_b[l, c, :, :]))
        for c in range(NTM):
            for q in range(4):
                jobs.append((g.wtm[l, c, :, q * 2048:(q + 1) * 2048], g.wtm_b[l, c, :, q * 2048:(q + 1) * 2048]))
        for c in range(4):
            for q in range(4):
                jobs.append((g.wo[l, c, :, q * 2048:(q + 1) * 2048], g.wo_b[l, c, :, q * 2048:(q + 1) * 2048]))
    return jobs


class PoolCaster:
    def __init__(self, P, g, es, layers):
        self.P = P
        self.jobs = wcast_jobs(g, layers)
        self.tin = P.sbuf(es, "pc_in", [128, 2, 2048], F32)
        self.tout = P.sbuf(es, "pc_out", [128, 2, 2048], BF16)
        self.rin, self.rout = Rot("pci", 2), Rot("pco", 2)

    def emit(self, n):
        P = self.P
        for _ in range(n):
            if not self.jobs:
                return
            src, dst = self.jobs.pop(0)
            ki, ni = self.rin.next()
            ko, no = self.rout.next()
            P.dma('pool', ni, self.tin[:, ki, :], src, writes=[ni])
            P.op('pool', lambda en: en.tensor_copy(out=self.tout[:, ko, :], in_=self.tin[:, ki, :]), reads=[ni],
                 writes=[no])
            P.dma('pool', no, dst, self.tout[:, ko, :], reads=[no], writes=[])


def phase_wcast(P, g, layers=(0, 1)):
    with ExitStack() as es:
        tin = P.sbuf(es, "wc_in", [128, 3, 2048], F32)
        tout = P.sbuf(es, "wc_out", [128, 3, 2048], BF16)
        rin, rout = Rot("wci", 3), Rot("wco", 3)
        engs = ['pool', 'act', 'dve']
        jobs = []
        for l in layers:
            for c in range(NFM):
                jobs.append((g.wfm[l, c, :, :], g.wfm_b[l, c, :, :]))
            for c in range(NTM):
                for q in range(4):
                    jobs.append((g.wtm[l, c, :, q * 2048:(q + 1) * 2048], g.wtm_b[l, c, :, q * 2048:(q + 1) * 2048]))
            for c in range(4):
                for q in range(4):
                    jobs.append((g.wo[l, c, :, q * 2048:(q + 1) * 2048], g.wo_b[l, c, :, q * 2048:(q + 1) * 2048]))
        for i, (src, dst) in enumerate(jobs):
            ki, ni = rin.next()
            ko, no = rout.next()
            P.dma('sp', ni, tin[:, ki, :], src, writes=[ni])
            e = engs[i % 3]
            if e == 'act':
                P.op(e, lambda en: en.activation(out=tout[:, ko, :], in_=tin[:, ki, :], func=AF.Copy),
                     reads=[ni], writes=[no])
            else:
                P.op(e, lambda en: en.tensor_copy(out=tout[:, ko, :], in_=tin[:, ki, :]), reads=[ni], writes=[no])
            P.dma('pool', no, dst, tout[:, ko, :], reads=[no], writes=[])
        P.barrier()


def phase_mod(P, g, K):
    with ExitStack() as es:
        cT = P.sbuf(es, "m_cT", [128, KC, 2], F32)
        sc = P.sbuf(es, "m_sc", [128, KC, 2], F32)
        scb = P.sbuf(es, "m_scb", [128, KC, 2, 128], F32)
        wm = P.sbuf(es, "m_wm", [128, 2, KC, 512], F32)
        bmf = P.sbuf(es, "m_bmf", [128, DEPTH, 32], F32)
        gpr = P.sbuf(es, "m_gpr", [128, DEPTH, KC], F32)
        modfm = P.sbuf(es, "m_modfm", [128, 32, 2], F32)
        gssh = P.sbuf(es, "m_gssh", [128, 2, KC, 2], F32)
        bmg = P.sbuf(es, "m_bmg", [128, 2, 512], F32)
        gpo = P.sbuf(es, "m_gpo", [128, 2, 512], F32)
        ggt = P.sbuf(es, "m_ggt", [128, 2, 512], F32)
        psfm = P.psum(es, "m_psfm", [128, 64], F32)
        psg = P.psum(es, "m_psg", [128, 2, 512], F32)
        P.dma('sp', 'm_cT', cT[:], g.cT, writes=['m_cT'])
        P.dma('sp', 'm_bmf', bmf[:], g.bm_fm, writes=['m_bmf'])
        P.dma('sp', 'm_gpr', gpr[:], g.gpre, writes=['m_gpr'])
        P.op('act', lambda e: e.activation(out=sc[:], in_=cT[:], func=AF.Silu), reads=['m_cT'], writes=['m_sc'])
        P.op('dve', lambda e: e.tensor_copy(out=scb[:], in_=sc[:].unsqueeze(3).broadcast_to([128, KC, 2, 128])),
             reads=['m_sc'], writes=['m_scb'])
        rw = Rot("m_wm", 2)
        rg = Rot("m_psg", 2)
        rb = Rot("m_bg", 2)
        rt = Rot("m_ggt", 2)
        for l in range(DEPTH):
            for j in range(12):
                kw, nw = rw.next()
                P.dma('sp', nw, wm[:, kw], g.wmod[l, j], writes=[nw])
                if j < 8:
                    for fc in range(4):
                        col = (j * 4 + fc) * 2
                        for kc in range(KC):
                            P.op('pe', lambda e: e.matmul(psfm[:, col:col + 2], lhsT=wm[:, kw, kc, fc * 128:(fc + 1) * 128],
                                                          rhs=sc[:, kc, :], start=(kc == 0), stop=(kc == KC - 1)),
                                 reads=[nw, 'm_sc'], writes=['m_psfm'])
                    if j == 7:
                        P.op('dve', lambda e: e.tensor_tensor(
                            out=modfm[:], in0=psfm[:].rearrange("p (c j) -> p c j", j=2),
                            in1=bmf[:, l, :].unsqueeze(2).broadcast_to([128, 32, 2]), op=ALU.add),
                            reads=['m_psfm', 'm_bmf'], writes=['m_modfm'])
                        P.op('dve', lambda e: e.tensor_scalar(out=gssh[:, 0], in0=modfm[:, 16:32, :], scalar1=1.0,
                                                               scalar2=None, op0=ALU.add),
                             reads=['m_modfm'], writes=['m_gssh'])
                        P.op('dve', lambda e: e.tensor_tensor(
                            out=gssh[:, 0], in0=gssh[:, 0],
                            in1=gpr[:, l, :].unsqueeze(2).broadcast_to([128, KC, 2]), op=ALU.mult),
                            reads=['m_gssh', 'm_gpr'], writes=['m_gssh'])
                        P.op('dve', lambda e: e.tensor_copy(out=gssh[:, 1], in_=modfm[:, 0:16, :]),
                             reads=['m_modfm'], writes=['m_gssh'])
                        P.dma('pool', 'm_gssh', g.gsd[l].rearrange("a p k j -> p a k j"), gssh[:],
                              reads=['m_gssh'], writes=[])
                else:
                    jg = j - 8
                    kb, nb = rb.next()
                    P.dma('sp', nb, bmg[:, kb], g.bm_g[:, l, jg * 512:(jg + 1) * 512], writes=[nb])
                    P.dma('sp', nb, gpo[:, kb], g.gpost[:, l, jg * 512:(jg + 1) * 512], writes=[nb])
                    for jj in range(2):
                        kg, ng = rg.next()
                        for kc in range(KC):
                            P.op('pe', lambda e: e.matmul(psg[:, kg, :], lhsT=scb[:, kc, jj, :], rhs=wm[:, kw, kc, :],
                                                          start=(kc == 0), stop=(kc == KC - 1)),
                                 reads=[nw, 'm_scb'], writes=[ng])
                        kt, ntt = rt.next()
                        P.op('dve', lambda e: e.tensor_tensor(out=ggt[:, kt], in0=psg[:, kg, :], in1=bmg[:, kb],
                                                               op=ALU.add), reads=[ng, nb], writes=[ntt])
                        P.op('dve', lambda e: e.tensor_tensor(out=ggt[:, kt], in0=ggt[:, kt], in1=gpo[:, kb],
                                                               op=ALU.mult), reads=[ntt, nb], writes=[ntt])
                        P.dma('pool', ntt, g.ggd[l, jj, :, jg * 512:(jg + 1) * 512], ggt[:, kt], reads=[ntt], writes=[])
        P.barrier()


def phase_inproj(P, g, K, l, src):
    with ExitStack() as es:
        gssh = P.sbuf(es, "i_gssh", [128, 2, KC, 2], F32)
        xt = P.sbuf(es, "i_xt", [128, 2, D], F32)
        junk = P.sbuf(es, "i_junk", [128, D], BF16)
        xb = P.sbuf(es, "i_xb", [128, 2, D], BF16)
        st = P.sbuf(es, "i_st", [128, 2, 4], F32)
        xnT = P.sbuf(es, "i_xnT", [128, 2, KC, 1024], BF16)
        wf = P.sbuf(es, "i_wf", [128, 3, KC, 128], BF16)
        wt = P.sbuf(es, "i_wt", [128, 2, KC, 512], BF16)
        sfm = P.sbuf(es, "i_sfm", [128, 2, 1024], BF16)
        sg = P.sbuf(es, "i_sg", [4, 4, 1024], F32)
        stm = P.sbuf(es, "i_stm", [128, 4, 512], BF16)
        psT = P.psum(es, "i_psT", [128, 2, 8, 128], BF16)
        psF = P.psum(es, "i_psF", [128, 4, 512], F32)
        psM = P.psum(es, "i_psM", [128, 2, 512], F32)
        P.dma('sp', 'i_gssh', gssh[:], g.gsd[l].rearrange("a p k j -> p a k j"), writes=['i_gssh'])
        rx, rT, rwf, rwt = Rot("i_xt", 2), Rot("i_psT", 2), Rot("i_wf", 3), Rot("i_wt", 2)
        rF, rM, rsf, rsm = Rot("i_psF", 2), Rot("i_psM", 2), Rot("i_sfm", 2), Rot("i_stm", 4)
        evi = [0]

        def evac(out, in_, reads, writes, par=None):
            evi[0] += 1
            if (evi[0] if par is None else par) % 2 == 0:
                P.op('act', lambda e: e.activation(out=out, in_=in_, func=AF.Copy), reads=reads, writes=writes)
            else:
                P.op('dve', lambda e: e.tensor_copy(out=out, in_=in_), reads=reads, writes=writes)

        import os
        NG = int(os.environ.get("NGRP", "5"))

        def prep_tile(gi, ti):
            t0, gn = GROUPS[gi]
            jj = 1 if t0 < TC else 0
            xbuf = gi % 2
            kx, nx = rx.next()
            r0 = t0 + ti * 128
            P.dma('sp', nx, xt[:, kx, :], src[r0:r0 + 128, :], writes=[nx])
            nst = "i_st%d" % kx
            P.op('act', lambda e: e.activation(out=junk[:], in_=xt[:, kx, :], func=AF.Square,
                                               accum_out=st[:, kx, 0:1]), reads=[nx], writes=['i_junk', nst])
            P.op('dve', lambda e: e.tensor_scalar(out=st[:, kx, 1:2], in0=st[:, kx, 0:1], scalar1=1.0 / D,
                                                   scalar2=EPS, op0=ALU.mult, op1=ALU.add),
                 reads=[nst], writes=[nst])
            P.op('act', lambda e: e.activation(out=st[:, kx, 2:3], in_=st[:, kx, 1:2], func=AF.Sqrt),
                 reads=[nst], writes=[nst])
            P.op('dve', lambda e: e.reciprocal(out=st[:, kx, 3:4], in_=st[:, kx, 2:3]), reads=[nst], writes=[nst])
            nxb = "i_xb%d" % kx
            P.op('act', lambda e: e.activation(out=xb[:, kx, :], in_=xt[:, kx, :], func=AF.Copy,
                                               scale=st[:, kx, 3:4]), reads=[nx, nst], writes=[nxb])
            for q4 in range(2):
                kT, nT = rT.next()
                for i in range(8):
                    kc = q4 * 8 + i
                    P.op('pe', lambda e: e.transpose(out=psT[:, kT, i, :], in_=xb[:, kx, kc * 128:(kc + 1) * 128],
                                                     identity=K.identb[:]), reads=[nxb, 'identb'], writes=[nT])
                for i in range(8):
                    kc = q4 * 8 + i
                    dst = xnT[:, xbuf, kc, ti * 128:(ti + 1) * 128]
                    if os.environ.get('DSTJ'):
                        dst = junk[:, kc * 128:(kc + 1) * 128]
                    s1 = gssh[:, 0, kc, jj:jj + 1] if not os.environ.get('CSC') else 1.5
                    s2 = gssh[:, 1, kc, jj:jj + 1] if not os.environ.get('CSC') else 0.5
                    EV = os.environ.get('EVMODE', 'x')
                    if (q4 % 2 == 0 and EV == 'x') or EV == 'd':
                        P.op('dve', lambda e: e.tensor_scalar(out=dst, in0=psT[:, kT, i, :],
                                                               scalar1=s1,
                                                               scalar2=s2,
                                                               op0=ALU.mult, op1=ALU.add),
                             reads=[nT, 'i_gssh'], writes=['i_xn%d_%d_0' % (xbuf, ti)])
                    else:
                        P.op('act', lambda e: e.activation(out=dst, in_=psT[:, kT, i, :], func=AF.Identity,
                                                           scale=s1,
                                                           bias=s2),
                             reads=[nT, 'i_gssh'], writes=['i_xn%d_%d_1' % (xbuf, ti)])

        def prep_list(gi):
            return [(lambda gi=gi, ti=ti: prep_tile(gi, ti)) for ti in range(GROUPS[gi][1] // 128)]

        for f_ in prep_list(0):
            f_()
        for gi, (t0, gn) in enumerate(GROUPS[:NG]):
            ntile = gn // 128
            jj = 1 if t0 < TC else 0
            xbuf = gi % 2
            nxt_prep = prep_list(gi + 1) if gi + 1 < NG else []
            xkeys = ['i_xn%d_%d_%d' % (xbuf, a, b) for a in range(ntile) for b in range(2)]
            nsub = (gn + 511) // 512
            subw = min(gn, 512)
            PARTS = os.environ.get('IP_PARTS', 'fgt')
            for c in range(NFM):
                if (c < 30 and 'f' not in PARTS) or (c == 30 and 'g' not in PARTS):
                    continue
                if c % 3 == 2 and nxt_prep:
                    nxt_prep.pop(0)()
                kw, nw = rwf.next()
                P.dma('sp', nw, wf[:, kw].rearrange("p k c -> p (k c)"), g.wfm_b[l, c, :, :], writes=[nw])
                if c < 30:
                    kF, nF = rF.next()
                    for s in range(nsub):
                        for kc in range(KC):
                            P.op('pe', lambda e: e.matmul(psF[:, kF * 2 + s, 0:subw], lhsT=wf[:, kw, kc, :],
                                                          rhs=xnT[:, xbuf, kc, s * 512:s * 512 + subw],
                                                          start=(kc == 0), stop=(kc == KC - 1)),
                                 reads=[nw] + xkeys, writes=[nF])
                    ks, ns = rsf.next()
                    for s in range(nsub):
                        evac(sfm[:, ks, s * 512:s * 512 + subw], psF[:, kF * 2 + s, 0:subw], [nF], [ns], par=c)
                    P.dma('pool', ns, g.PT[c * 128:(c + 1) * 128, t0:t0 + gn], sfm[:, ks, 0:gn], reads=[ns],
                          writes=[])
                else:
                    for j4 in range(4):
                        kF, nF = rF.next()
                        for s in range(nsub):
                            for kc in range(KC):
                                P.op('pe', lambda e: e.matmul(psF[0:4, kF * 2 + s, 0:subw],
                                                              lhsT=wf[:, kw, kc, j4 * 4:(j4 + 1) * 4],
                                                              rhs=xnT[:, xbuf, kc, s * 512:s * 512 + subw],
                                                              start=(kc == 0), stop=(kc == KC - 1)),
                                     reads=[nw] + xkeys, writes=[nF])
                        for s in range(nsub):
                            evac(sg[0:4, j4, s * 512:s * 512 + subw], psF[0:4, kF * 2 + s, 0:subw], [nF],
                                 ['i_sg%d' % j4], par=j4)
                        P.dma('pool', 'i_sg%d' % j4, g.PG[j4, :, t0:t0 + gn], sg[0:4, j4, 0:gn],
                              reads=['i_sg%d' % j4], writes=[])
            for f_ in nxt_prep:
                f_()
            for c in range(NTM):
                if 't' not in PARTS:
                    continue
                kw, nw = rwt.next()
                P.dma('sp', nw, wt[:, kw].rearrange("p k c -> p (k c)"), g.wtm_b[l, c, :, :], writes=[nw])
                cw = 512 if c < NTM - 1 else 256
                for ti in range(ntile):
                    kM, nM = rM.next()
                    for kc in range(KC):
                        P.op('pe', lambda e: e.matmul(psM[:, kM, 0:cw], lhsT=xnT[:, xbuf, kc, ti * 128:(ti + 1) * 128],
                                                      rhs=wt[:, kw, kc, 0:cw], start=(kc == 0), stop=(kc == KC - 1)),
                             reads=[nw, 'i_xn%d_%d_0' % (xbuf, ti), 'i_xn%d_%d_1' % (xbuf, ti)], writes=[nM])
                    ks, ns = rsm.next()
                    evac(stm[:, ks, 0:cw], psM[:, kM, 0:cw], [nM], [ns])
                    r0 = t0 + ti * 128
                    P.dma('pool', ns, g.PM[r0:r0 + 128, c * 512:c * 512 + cw], stm[:, ks, 0:cw], reads=[ns], writes=[])
        P.barrier()


def setup_consts(P, g, es):
    K = Ctx()
    tmp = P.sbuf(es, "k_tmp", [128, 128], F32)
    K.identb = P.sbuf(es, "k_identb", [128, 128], BF16)
    K.identf = P.sbuf(es, "k_identf", [128, 128], F32)
    P.dma('sp', 'identf', K.identf[:], g.ident, writes=['identf'])
    P.op('dve', lambda e: e.tensor_copy(out=K.identb[:], in_=K.identf[:]), reads=['identf'], writes=['identb'])
    return K


SCALE = 128 ** -0.5


def phase_attn(P, g, K, l, with_ctx, cast_layers=()):
    with ExitStack() as es:
        caster = PoolCaster(P, g, es, cast_layers) if cast_layers else None

        def wc_jobs(n):
            if caster is not None:
                caster.emit(n)
        ropeC = P.sbuf(es, "c_ropeC", [128, TL], F32)
        ropeS = P.sbuf(es, "c_ropeS", [128, TL], F32)
        permb = P.sbuf(es, "c_permb", [128, 128], BF16)
        bandb = P.sbuf(es, "c_bandb", [128, 384], BF16)
        ctmp = P.sbuf(es, "c_ctmp", [128, 384], F32)
        sink = P.sbuf(es, "c_sink", [128, DEPTH, 8], F32)
        nsink = P.sbuf(es, "c_nsink", [128, 8], F32)
        kraw = P.sbuf(es, "c_kraw", [128, T], BF16)
        kr = P.sbuf(es, "c_kr", [128, NKV, TL], BF16)
        kctx = P.sbuf(es, "c_kctx", [128, NKV, TC], BF16)
        vv = P.sbuf(es, "c_v", [128, NKV, NT, 128], BF16)
        qraw = P.sbuf(es, "c_qraw", [128, T], BF16)
        qr = P.sbuf(es, "c_qr", [128, TL], BF16)
        czr = P.sbuf(es, "c_czr", [128, NT, 128], BF16)
        szh = P.sbuf(es, "c_szh", [128, NT, 128], BF16)
        t1 = P.sbuf(es, "c_t1", [128, 2, 512], F32)
        t2 = P.sbuf(es, "c_t2", [128, 2, 512], F32)
        Pt = P.sbuf(es, "c_Pt", [128, 2, 640], BF16)
        PTs = P.sbuf(es, "c_PTs", [128, 2, 5, 128], BF16)
        stt = P.sbuf(es, "c_st", [128, 4, 8], F32)
        ot = P.sbuf(es, "c_ot", [128, 4, 128], BF16)
        psR = P.psum(es, "c_psR", [128, 2, 512], F32)
        psS = P.psum(es, "c_psS", [128, 2, 2, 512], F32)
        psPT = P.psum(es, "c_psPT", [128, 2, 8, 128], BF16)
        P.dma('sp', 'c_ropeC', ropeC[:], g.ropec, writes=['c_ropeC'])
        P.dma('sp', 'c_ropeS', ropeS[:], g.ropes, writes=['c_ropeS'])
        P.dma('sp', 'c_ctmp', ctmp[:, 0:128], g.perm, writes=['c_ctmp'])
        P.op('dve', lambda e: e.tensor_copy(out=permb[:], in_=ctmp[:, 0:128]), reads=['c_ctmp'], writes=['c_permb'])
        P.dma('sp', 'c_ctmp', ctmp[:], g.band, writes=['c_ctmp'])
        P.op('dve', lambda e: e.tensor_copy(out=bandb[:], in_=ctmp[:]), reads=['c_ctmp'], writes=['c_bandb'])
        P.dma('sp', 'c_sink', sink[:], g.sink, writes=['c_sink'])
        P.op('dve', lambda e: e.tensor_scalar(out=nsink[:], in0=sink[:, l, :], scalar1=-1.0, scalar2=None,
                                               op0=ALU.mult), reads=['c_sink'], writes=['c_nsink'])
        rR, rt = Rot("c_psR", 2), Rot("c_t", 2)
        PMv = g.PM.rearrange("(n p) c -> p n c", p=128)

        def rope(dst, src, n512, skey, dkey):
            for i in range(n512):
                kR, nR = rR.next()
                kt, nt = rt.next()
                sl = slice(i * 512, (i + 1) * 512)
                P.op('pe', lambda e: e.matmul(psR[:, kR, :], lhsT=permb[:], rhs=src[:, sl], start=True, stop=True),
                     reads=[skey, 'c_permb'], writes=[nR])
                P.op('dve', lambda e: e.tensor_tensor(out=t1[:, kt, :], in0=psR[:, kR, :], in1=ropeS[:, sl],
                                                       op=ALU.mult), reads=['c_ropeS'], writes=[nR, nt + 'a'])
                P.op('pool', lambda e: e.tensor_tensor(out=t2[:, kt, :], in0=src[:, sl], in1=ropeC[:, sl],
                                                        op=ALU.mult), reads=[skey, 'c_ropeC'], writes=[nt + 'b'])
                P.op('dve', lambda e: e.tensor_tensor(out=dst[:, sl], in0=t1[:, kt, :], in1=t2[:, kt, :],
                                                       op=ALU.add), reads=[nt + 'a', nt + 'b'], writes=[dkey])

        for kv in range(NKV):
            P.dma('sp', 'c_kraw', kraw[:], g.PT[(C_CK + kv) * 128:(C_CK + kv + 1) * 128, :], writes=['c_kraw'])
            P.op('pool', lambda e: e.tensor_copy(out=kctx[:, kv, :], in_=kraw[:, 0:TC]), reads=['c_kraw'],
                 writes=['c_kctx'])
            rope(kr[:, kv, :], kraw[:, TC:], 8, 'c_kraw', 'c_kr')
            P.dma('sp', 'c_v', vv[:, kv], PMv[:, :, O_CV + kv * 128:O_CV + (kv + 1) * 128], writes=['c_v'])
        rS, rP, rPT, rst, rot = Rot("c_psS", 2), Rot("c_Pt", 2), Rot("c_PTs", 2), Rot("c_st", 4), Rot("c_ot", 4)

        def block(qT, qkey, kv, hq, kloc, nk, b0, vtiles, sz, dst):
            kS, nS = rS.next()
            kP, nP = rP.next()
            kst, nst = rst.next()
            kT, nT = rPT.next()
            kO, nO = rR.next()
            ko, no = rot.next()
            st = stt[:, kst, :]
            W = nk * 128
            npt = 'c_psPT%d' % kT

            def s1():
                if nk > 0:
                    P.op('pe', lambda e: e.matmul(psS[:, kS, 0, 0:W], lhsT=qT, rhs=kloc, start=True, stop=False),
                         reads=[qkey, 'c_kr'], writes=[nS])
                    P.op('pe', lambda e: e.matmul(psS[:, kS, 0, 0:W], lhsT=K.identb[:], rhs=bandb[:, b0:b0 + W],
                                                  start=False, stop=True), reads=['identb', 'c_bandb'], writes=[nS])
                P.op('pe', lambda e: e.matmul(psS[:, kS, 1, 0:TC], lhsT=qT, rhs=kctx[:, kv, :], start=True, stop=True),
                     reads=[qkey, 'c_kctx'], writes=[nS])
                P.op('dve', lambda e: e.tensor_reduce(out=st[:, 1:2], in_=psS[:, kS, 1, 0:TC], axis=AX.X, op=ALU.max),
                     writes=[nS, nst])
                P.op('dve', lambda e: e.tensor_scalar(out=st[:, 2:3], in0=st[:, 1:2], scalar1=-SCALE,
                                                       scalar2=nsink[:, hq:hq + 1], op0=ALU.mult, op1=ALU.min),
                     reads=['c_nsink'], writes=[nst])
                if nk > 0:
                    P.op('dve', lambda e: e.tensor_reduce(out=st[:, 0:1], in_=psS[:, kS, 0, 0:W], axis=AX.X, op=ALU.max),
                         writes=[nS, nst])
                    P.op('dve', lambda e: e.tensor_scalar(out=st[:, 2:3], in0=st[:, 0:1], scalar1=-SCALE,
                                                           scalar2=st[:, 2:3], op0=ALU.mult, op1=ALU.min),
                         writes=[nst])
                else:
                    P.op('dve', lambda e: e.memset(st[:, 3:4], 0.0), writes=[nst])

            def s2():
                if nk > 0:
                    P.op('act', lambda e: e.activation(out=Pt[:, kP, 0:W], in_=psS[:, kS, 0, 0:W], func=AF.Exp,
                                                       scale=SCALE, bias=st[:, 2:3], accum_out=st[:, 3:4]),
                         reads=[], writes=[nS, nP, nst])
                P.op('act', lambda e: e.activation(out=Pt[:, kP, 384:640], in_=psS[:, kS, 1, 0:TC], func=AF.Exp,
                                                   scale=SCALE, bias=st[:, 2:3], accum_out=st[:, 4:5]),
                     reads=[], writes=[nS, nP, nst])
                P.op('act', lambda e: e.activation(out=st[:, 5:6], in_=st[:, 2:3], func=AF.Exp, scale=1.0,
                                                   bias=sink[:, l, hq:hq + 1]), reads=['c_sink'], writes=[nst])
                P.op('dve', lambda e: e.tensor_reduce(out=st[:, 6:7], in_=st[:, 3:6], axis=AX.X, op=ALU.add),
                     writes=[nst])
                P.op('dve', lambda e: e.reciprocal(out=st[:, 7:8], in_=st[:, 6:7]), writes=[nst])
                for j in range(nk):
                    P.op('pe', lambda e: e.transpose(out=psPT[:, kT, j, :], in_=Pt[:, kP, j * 128:(j + 1) * 128],
                                                     identity=K.identb[:]), reads=[nP, 'identb'], writes=[npt])
                for j in range(2):
                    P.op('pe', lambda e: e.transpose(out=psPT[:, kT, 3 + j, :],
                                                     in_=Pt[:, kP, 384 + j * 128:384 + (j + 1) * 128],
                                                     identity=K.identb[:]), reads=[nP, 'identb'], writes=[npt])

            def s3():
                if nk == 3:
                    P.op('act', lambda e: e.activation(out=PTs[:, kT], in_=psPT[:, kT, 0:5, :], func=AF.Copy),
                         writes=[npt, nT])
                else:
                    if nk > 0:
                        P.op('act', lambda e: e.activation(out=PTs[:, kT, 0:nk], in_=psPT[:, kT, 0:nk, :], func=AF.Copy),
                             writes=[npt, nT])
                    P.op('act', lambda e: e.activation(out=PTs[:, kT, 3:5], in_=psPT[:, kT, 3:5, :], func=AF.Copy),
                         writes=[npt, nT])
                seq = [(j, vtiles[j]) for j in range(nk)] + [(3, 0), (4, 1)]
                for i, (j, vt) in enumerate(seq):
                    P.op('pe', lambda e: e.matmul(psR[:, kO, 0:128], lhsT=PTs[:, kT, j, :], rhs=vv[:, kv, vt, :],
                                                  start=(i == 0), stop=(i == len(seq) - 1)),
                         reads=[nT, 'c_v'], writes=[nO])
                P.op('dve', lambda e: e.scalar_tensor_tensor(out=ot[:, ko, :], in0=psR[:, kO, 0:128], scalar=st[:, 7:8],
                                                              in1=sz, op0=ALU.mult, op1=ALU.mult),
                     reads=[nst, 'c_szh'], writes=[nO, no])
                P.dma('sp' if caster is not None else 'pool', no, dst, ot[:, ko, :], reads=[no], writes=[])
            return (s1, s2, s3)

        def run_pipe(blocks):
            n = len(blocks)
            for i in range(n + 2):
                if i < n:
                    blocks[i][0]()
                if 0 <= i - 1 < n:
                    blocks[i - 1][1]()
                if 0 <= i - 2 < n:
                    blocks[i - 2][2]()

        for hq in range(NQ):
            kv = hq // (NQ // NKV)
            P.dma('sp', 'c_qraw', qraw[:], g.PT[(C_CQ + hq) * 128:(C_CQ + hq + 1) * 128, :], writes=['c_qraw'])
            P.dma('sp', 'c_czr', czr[:], PMv[:, :, O_CZ + hq * 128:O_CZ + (hq + 1) * 128], writes=['c_czr'])
            P.op('act', lambda e: e.activation(out=szh[:], in_=czr[:], func=AF.Silu), reads=['c_czr'], writes=['c_szh'])
            rope(qr, qraw[:, TC:], 8, 'c_qraw', 'c_qr')
            blks = []
            if with_ctx:
                for ct in range(2):
                    blks.append(block(qraw[:, ct * 128:(ct + 1) * 128], 'c_qraw', kv, hq, None, 0, 0, [], szh[:, ct, :],
                                      g.mix[ct * 128:(ct + 1) * 128, 1024 + hq * 128:1024 + (hq + 1) * 128]))
            for n in range(TL // 128):
                lo, hi = max(n - 1, 0), min(n + 1, TL // 128 - 1)
                nk = hi - lo + 1
                b0 = (lo - (n - 1)) * 128
                blks.append(block(qr[:, n * 128:(n + 1) * 128], 'c_qr', kv, hq, kr[:, kv, lo * 128:(hi + 1) * 128], nk, b0,
                                  [2 + lo + j for j in range(nk)], szh[:, 2 + n, :],
                                  g.mix[TC + n * 128:TC + (n + 1) * 128, 1024 + hq * 128:1024 + (hq + 1) * 128]))
            run_pipe(blks)
            wc_jobs(10)
        wc_jobs(1000)
        P.barrier()


def order_chunks(d, with_ctx_first=True):
    if d == 0:
        return list(range(NT))
    return [1, 0] + list(range(NT - 1, 1, -1))


def phase_mlstm(P, g, K, l):
    with ExitStack() as es:
        gcol = P.sbuf(es, "b_gcol", [128, NT, 16], F32)
        cb = P.sbuf(es, "b_cb", [128, 512], F32)
        convw = P.sbuf(es, "b_convw", [128, DEPTH, 3, 8], F32)
        maskf = P.sbuf(es, "b_maskf", [128, 2, 128], F32)
        es2 = ExitStack()
        gb = P.sbuf(es2, "b_gb", [4, DEPTH, 4], F32)
        rmask = P.sbuf(es2, "b_rmask", [4, T], F32)
        G1 = P.sbuf(es2, "b_G1", [4, T], F32)
        G2 = P.sbuf(es2, "b_G2", [4, T], F32)
        G3 = P.sbuf(es2, "b_G3", [4, T], F32)
        G4 = P.sbuf(es2, "b_G4", [4, T], F32)
        sm = P.sbuf(es2, "b_sm", [4, 6, NT + 1], F32)
        comb = P.sbuf(es2, "b_comb", [16, T], F32)
        psA = P.psum(es, "b_psA", [128, 2, 512], F32)
        psK = P.psum(es, "b_psK", [128, 2, 1024], BF16)
        psN = P.psum(es, "b_psN", [128, 2, 512], F32)
        psU = P.psum(es, "b_psU", [128, 2, 512], F32)
        P.dma('sp', 'b_gb', gb[:], g.gbias, writes=['b_gb'])
        P.dma('sp', 'b_rmask', rmask[:], g.rmask, writes=['b_rmask'])
        P.dma('sp', 'b_convw', convw[:], g.convw, writes=['b_convw'])
        P.dma('sp', 'b_maskf', maskf[:, 0, :], g.maskf, writes=['b_maskf'])
        P.dma('sp', 'b_maskf', maskf[:, 1, :], g.maskb, writes=['b_maskf'])
        v3 = lambda t: t[:].rearrange("p (c s) -> p c s", s=128)
        for d in range(2):
            ordr = order_chunks(d)
            P.dma('sp', 'b_G1', G1[:], g.PG[2 * d], writes=['b_G1'])
            P.dma('sp', 'b_G2', G2[:], g.PG[2 * d + 1], writes=['b_G2'])
            P.op('dve', lambda e: e.tensor_scalar(out=G1[:], in0=G1[:], scalar1=gb[:, l, 2 * d:2 * d + 1], scalar2=None,
                                                   op0=ALU.add), reads=['b_gb'], writes=['b_G1'])
            P.op('act', lambda e: e.activation(out=G2[:], in_=G2[:], func=AF.Sigmoid,
                                               bias=gb[:, l, 2 * d + 1:2 * d + 2], scale=1.0),
                 reads=['b_gb'], writes=['b_G2'])
            P.op('act', lambda e: e.activation(out=G2[:], in_=G2[:], func=AF.Ln), writes=['b_G2'])
            P.op('dve', lambda e: e.tensor_tensor_scan(out=G3[:], data0=rmask[:], data1=G2[:], initial=0.0,
                                                        op0=ALU.mult, op1=ALU.add),
                 reads=['b_rmask', 'b_G2'], writes=['b_G3'])
            if d == 1:
                P.op('dve', lambda e: e.tensor_tensor(out=G2[:], in0=G2[:], in1=G3[:], op=ALU.subtract),
                     reads=['b_G3'], writes=['b_G2'])
                P.op('dve', lambda e: e.tensor_tensor(out=v3(G3), in0=v3(G2),
                                                       in1=v3(G3)[:, :, 127:128].broadcast_to([4, NT, 128]),
                                                       op=ALU.add), reads=['b_G2'], writes=['b_G3'])
            P.op('dve', lambda e: e.tensor_tensor(out=G1[:], in0=G1[:], in1=G3[:], op=ALU.subtract),
                 reads=['b_G3'], writes=['b_G1'])
            P.op('dve', lambda e: e.tensor_reduce(out=sm[:, 0, 0:NT], in_=v3(G1), axis=AX.X, op=ALU.max),
                 reads=['b_G1'], writes=['b_sm'])
            ex = 127 if d == 0 else 0
            P.op('dve', lambda e: e.tensor_copy(out=sm[:, 1, 0:NT], in_=v3(G3)[:, :, ex]), reads=['b_G3'],
                 writes=['b_sm'])
            P.op('dve', lambda e: e.memset(sm[:, 2, :], 0.0), writes=['b_sm'])
            for i, c in enumerate(ordr):
                P.op('dve', lambda e: e.tensor_tensor(out=sm[:, 3, c:c + 1], in0=sm[:, 2, c:c + 1],
                                                       in1=sm[:, 0, c:c + 1], op=ALU.max), writes=['b_sm'])
                nx = ordr[i + 1] if i + 1 < NT else NT
                P.op('dve', lambda e: e.tensor_tensor(out=sm[:, 2, nx:nx + 1], in0=sm[:, 1, c:c + 1],
                                                       in1=sm[:, 3, c:c + 1], op=ALU.add), writes=['b_sm'])
            P.op('dve', lambda e: e.tensor_tensor(out=sm[:, 4, 0:NT], in0=sm[:, 2, 0:NT], in1=sm[:, 3, 0:NT],
                                                   op=ALU.subtract), writes=['b_sm'])
            P.op('act', lambda e: e.activation(out=sm[:, 4, 0:NT], in_=sm[:, 4, 0:NT], func=AF.Exp), writes=['b_sm'])
            P.dma('sp', 'b_sm', g.carry[0, d * 256:d * 256 + 4 * 64].rearrange("(h c) -> h c", c=64)[:, 0:NT],
                  sm[:, 4, 0:NT], reads=['b_sm'], writes=['carry'])
            Gb = sm[:, 3, 0:NT].unsqueeze(2).broadcast_to([4, NT, 128])
            P.op('dve', lambda e: e.tensor_tensor(out=v3(G4), in0=v3(G1), in1=Gb, op=ALU.subtract),
                 reads=['b_G1', 'b_sm'], writes=['b_G4'])
            P.op('act', lambda e: e.activation(out=G4[:], in_=G4[:], func=AF.Exp), writes=['b_G4'])
            P.op('dve', lambda e: e.tensor_tensor(out=v3(G2), in0=v3(G3), in1=Gb, op=ALU.add),
                 reads=['b_G3', 'b_sm'], writes=['b_G2'])
            P.op('act', lambda e: e.activation(out=G2[:], in_=G2[:], func=AF.Exp, scale=-1.0), writes=['b_G2'])
            P.dma('sp', 'b_comb%da' % d, comb[d * 8:d * 8 + 4, :], G4[:], reads=['b_G4'], writes=['b_comb%da' % d])
            P.dma('sp', 'b_comb%db' % d, comb[d * 8 + 4:d * 8 + 8, :], G2[:], reads=['b_G2'], writes=['b_comb%db' % d])
        for half in range(2):
            for cc in range(17):
                c = half * 17 + cc
                P.op('pe', lambda e: e.matmul(psA[:, half, cc * 16:(cc + 1) * 16], lhsT=comb[:, c * 128:(c + 1) * 128],
                                              rhs=K.identf[0:16, 0:16], start=True, stop=True),
                     reads=['b_comb0a', 'b_comb0b', 'b_comb1a', 'b_comb1b', 'identf'], writes=['b_psA%d' % half])
            P.op('dve', lambda e: e.tensor_copy(out=gcol[:, half * 17:(half + 1) * 17, :].rearrange("p c r -> p (c r)"),
                                                 in_=psA[:, half, 0:272]), writes=['b_psA%d' % half, 'b_gcol'])
        P.dma('sp', 'b_cb', cb[:], g.carry[0].partition_broadcast(128), reads=['carry'], writes=['b_cb'])
        P.barrier()
        es2.close()
        raw = P.sbuf(es, "b_raw", [128, T], BF16)
        acc = P.sbuf(es, "b_acc", [128, T], F32)
        qT = P.sbuf(es, "b_qT", [128, NB, T], BF16)
        kT = P.sbuf(es, "b_kT", [128, NB, T], BF16)
        vaug = P.sbuf(es, "b_vaug", [128, NB, NT, 129], BF16)
        Cst = P.sbuf(es, "b_C", [128, 2 * NB, 129], F32)
        Cbf = P.sbuf(es, "b_Cbf", [128, 2 * NB, 129], BF16)
        Wt = P.sbuf(es, "b_Wt", [128, 4, 128], BF16)
        kw = P.sbuf(es, "b_kw", [128, 4, 128], BF16)
        dm = P.sbuf(es, "b_dm", [128, 4, 2], F32)
        ho = P.sbuf(es, "b_ho", [128, 4, 128], F32)
        PMv = g.PM.rearrange("(n p) c -> p n c", p=128)
        for h in range(NB):
            for qk in range(2):
                ch = (C_BQ if qk == 0 else C_BK) + h
                P.dma('sp', 'b_raw', raw[:], g.PT[ch * 128:(ch + 1) * 128, :], writes=['b_raw'])
                w = lambda j: convw[:, l, j, qk * 4 + h:qk * 4 + h + 1]
                P.op('dve', lambda e: e.tensor_scalar(out=acc[:], in0=raw[:], scalar1=w(1), scalar2=None, op0=ALU.mult),
                     reads=['b_raw', 'b_convw'], writes=['b_acc'])
                for (s0, e0) in ((0, TC), (TC, T)):
                    P.op('dve', lambda e: e.scalar_tensor_tensor(out=acc[:, s0 + 1:e0], in0=raw[:, s0:e0 - 1], scalar=w(0),
                                                                  in1=acc[:, s0 + 1:e0], op0=ALU.mult, op1=ALU.add),
                         reads=['b_raw', 'b_convw'], writes=['b_acc'])
                    P.op('dve', lambda e: e.scalar_tensor_tensor(out=acc[:, s0:e0 - 1], in0=raw[:, s0 + 1:e0], scalar=w(2),
                                                                  in1=acc[:, s0:e0 - 1], op0=ALU.mult, op1=ALU.add),
                         reads=['b_raw', 'b_convw'], writes=['b_acc'])
                if qk == 0:
                    P.op('act', lambda e: e.activation(out=acc[:], in_=acc[:], func=AF.Silu), writes=['b_acc'])
                    P.op('pool', lambda e: e.tensor_scalar(out=qT[:, h, :], in0=acc[:], scalar1=SCALE, scalar2=None,
                                                            op0=ALU.mult), reads=['b_acc'], writes=['b_qT%d' % h])
                else:
                    P.op('act', lambda e: e.activation(out=kT[:, h, :], in_=acc[:], func=AF.Silu), reads=['b_acc'],
                         writes=['b_kT%d' % h])
            P.op('pool', lambda e: e.memset(vaug[:, h, :, 128:129], 1.0), writes=['b_v%d' % h])
            P.dma('sp', 'b_v%d' % h, vaug[:, h, :, 0:128], PMv[:, :, O_BV + h * 128:O_BV + (h + 1) * 128],
                  writes=['b_v%d' % h])
        P.op('pool', lambda e: e.memset(Cst[:], 0.0), writes=['b_C%d' % i for i in range(2 * NB)])
        rA, rK, rN, rU = Rot("b_psAr", 2), Rot("b_psK", 2), Rot("b_psN", 2), Rot("b_psU", 2)
        rW, rkw, rdm, rho = Rot("b_Wt", 4), Rot("b_kw", 4), Rot("b_dm", 4), Rot("b_ho", 4)
        mpend = [None]

        def mstage2(h, d, c, ci, ts, thr, car, kN, nN, kU, nU, kW, nW, kk, nkw, kd, ndm, kh, nho, qk_):
            P.op('pe', lambda e: e.matmul(psN[:, kN, 0:129], lhsT=qT[:, h, ts], rhs=Cbf[:, ci, :], start=True,
                                          stop=False), reads=[qk_, 'b_Cbf%d' % ci], writes=[nN])
            P.op('pe', lambda e: e.matmul(psN[:, kN, 0:129], lhsT=Wt[:, kW, :], rhs=vaug[:, h, c, :], start=False,
                                          stop=True), reads=[nW, 'b_v%d' % h], writes=[nN])
            P.op('pe', lambda e: e.matmul(psU[:, kU, 0:129], lhsT=kw[:, kk, :], rhs=vaug[:, h, c, :], start=True,
                                          stop=True), reads=[nkw, 'b_v%d' % h], writes=[nU])
            P.op('dve', lambda e: e.scalar_tensor_tensor(out=Cst[:, ci, :], in0=Cst[:, ci, :], scalar=car,
                                                          in1=psU[:, kU, 0:129], op0=ALU.mult, op1=ALU.add),
                 reads=['b_cb'], writes=[nU, 'b_C%d' % ci])
            P.op('dve', lambda e: e.tensor_scalar(out=dm[:, kd, 1:2], in0=psN[:, kN, 128:129], scalar1=-1.0,
                                                   scalar2=None, op0=ALU.mult), writes=[nN, ndm])
            P.op('dve', lambda e: e.scalar_tensor_tensor(out=dm[:, kd, 0:1], in0=psN[:, kN, 128:129], scalar=thr,
                                                          in1=dm[:, kd, 1:2], op0=ALU.max, op1=ALU.max),
                 reads=['b_gcol'], writes=[nN, ndm])
            P.op('dve', lambda e: e.reciprocal(out=dm[:, kd, 1:2], in_=dm[:, kd, 0:1]), writes=[ndm])
            P.op('act', lambda e: e.activation(out=ho[:, kh, :], in_=psN[:, kN, 0:128], func=AF.Copy,
                                               scale=dm[:, kd, 1:2]), reads=[ndm], writes=[nN, nho])
            P.dma('sp', nho, g.yB[d, c * 128:(c + 1) * 128, h * 128:(h + 1) * 128], ho[:, kh, :], reads=[nho],
                  writes=[])

        orders = [order_chunks(0), order_chunks(1)]
        for i in range(NT):
            for h in range(NB):
                for d in range(2):
                    c = orders[d][i]
                    ci = d * NB + h
                    ts = slice(c * 128, (c + 1) * 128)
                    ea = gcol[:, c, d * 8 + h:d * 8 + h + 1]
                    thr = gcol[:, c, d * 8 + 4 + h:d * 8 + 4 + h + 1]
                    car = cb[:, d * 256 + h * 64 + c:d * 256 + h * 64 + c + 1]
                    kA, nA = rA.next()
                    kK, nK = rK.next()
                    kN, nN = rN.next()
                    kU, nU = rU.next()
                    kW, nW = rW.next()
                    kk, nkw = rkw.next()
                    kd, ndm = rdm.next()
                    kh, nho = rho.next()
                    qk_ = 'b_qT%d' % h
                    kk_ = 'b_kT%d' % h
                    P.op('pe', lambda e: e.matmul(psA[:, kA, 0:128], lhsT=kT[:, h, ts], rhs=qT[:, h, ts], start=True,
                                                  stop=True), reads=[qk_, kk_], writes=[nA])
                    P.op('dve', lambda e: e.scalar_tensor_tensor(out=Wt[:, kW, :], in0=psA[:, kA, 0:128], scalar=ea,
                                                                  in1=maskf[:, d, :], op0=ALU.mult, op1=ALU.mult),
                         reads=['b_gcol', 'b_maskf'], writes=[nA, nW])
                    P.op('pe', lambda e: e.transpose(out=psK[:, kK, 0:128], in_=kT[:, h, ts], identity=K.identb[:]),
                         reads=[kk_, 'identb'], writes=[nK])
                    P.op('act', lambda e: e.activation(out=kw[:, kk, :], in_=psK[:, kK, 0:128], func=AF.Copy, scale=ea),
                         reads=['b_gcol'], writes=[nK, nkw])
                    P.op('act', lambda e: e.activation(out=Cbf[:, ci, :], in_=Cst[:, ci, :], func=AF.Copy, scale=car),
                         reads=['b_cb', 'b_C%d' % ci], writes=['b_Cbf%d' % ci])
                    nxt = (lambda h=h, d=d, c=c, ci=ci, ts=ts, thr=thr, car=car, kN=kN, nN=nN, kU=kU, nU=nU, kW=kW, nW=nW,
                           kk=kk, nkw=nkw, kd=kd, ndm=ndm, kh=kh, nho=nho, qk_=qk_:
                           mstage2(h, d, c, ci, ts, thr, car, kN, nN, kU, nU, kW, nW, kk, nkw, kd, ndm, kh, nho, qk_))
                    if mpend[0] is not None:
                        mpend[0]()
                    mpend[0] = nxt
        mpend[0]()
        P.barrier()


def phase_readout(P, g, K, which, with_ctx):
    with ExitStack() as es:
        y = g.yA if which == 'a' else g.yB
        ocol = O_AG if which == 'a' else O_BZ
        mcol = 0 if which == 'a' else 512
        yfb = P.sbuf(es, "r_yfb", [128, 3, 2, 512], F32)
        yf = yfb[:, :, 0, :]
        yb = yfb[:, :, 1, :]
        goz = P.sbuf(es, "r_goz", [128, 3, 2, 512], BF16)
        go = goz[:, :, 0, :]
        gz = goz[:, :, 1, :]
        sz = P.sbuf(es, "r_sz", [128, 3, 512], F32)
        so = P.sbuf(es, "r_so", [128, 3, 512], F32)
        junk = P.sbuf(es, "r_junk", [128, 128], F32)
        nhalf = P.sbuf(es, "r_nhalf", [128, 4], F32)
        P.op('pool', lambda e: e.memset(nhalf[:], -0.5), writes=['r_nhalf'])
        st = P.sbuf(es, "r_st", [128, 3, 16], F32)
        om = P.sbuf(es, "r_om", [128, 3, 512], BF16)
        rr = Rot("r", 3)
        rpend = [None]
        for tt in range(0 if with_ctx else 2, NT):
            k, n = rr.next()
            rows = slice(tt * 128, (tt + 1) * 128)
            def rs2(k=k, n=n, rows=rows):
                P.op('dve', lambda e: e.tensor_scalar(out=st[:, k, 4:8], in0=st[:, k, 0:4], scalar1=1.0 / 128, scalar2=EPS,
                                                       op0=ALU.mult, op1=ALU.add), writes=[n + 'st'])
                P.op('pool', lambda e: e.tensor_tensor(out=st[:, k, 12:16], in0=st[:, k, 4:8], in1=nhalf[:, 0:4],
                                                        op=ALU.pow), reads=['r_nhalf'], writes=[n + 'st'])
                if which == 'b':
                    P.op('act', lambda e: e.activation(out=sz[:, k, :], in_=gz[:, k, :], func=AF.Sigmoid),
                         reads=[n + 'gz'], writes=[n + 'sz'])
                    P.op('pool', lambda e: e.tensor_tensor(out=sz[:, k, :], in0=sz[:, k, :], in1=gz[:, k, :],
                                                            op=ALU.mult), reads=[n + 'gz'], writes=[n + 'sz'])
                else:
                    P.op('act', lambda e: e.activation(out=sz[:, k, :], in_=gz[:, k, :], func=AF.Silu),
                         reads=[n + 'gz'], writes=[n + 'sz'])
                for h in range(4):
                    hs = slice(h * 128, (h + 1) * 128)
                    P.op('dve', lambda e: e.scalar_tensor_tensor(out=om[:, k, hs], in0=yf[:, k, hs],
                                                                  scalar=st[:, k, 12 + h:13 + h], in1=sz[:, k, hs],
                                                                  op0=ALU.mult, op1=ALU.mult),
                         reads=[n + 'yf', n + 'st', n + 'sz'], writes=[n + 'om'])
                P.dma('pool', n + 'om', g.mix[rows, mcol:mcol + 512], om[:, k, :], reads=[n + 'om'], writes=[])
            P.dma('sp', n + 'yf', yfb[:, k], y[:, rows, :].rearrange("d p c -> p d c"), writes=[n + 'yf', n + 'yb'])
            if which == 'b':
                P.dma('sp', n + 'gz', goz[:, k].rearrange("p a c -> p (a c)"), g.PM[rows, O_BO:O_BO + 1024],
                      writes=[n + 'gz', n + 'go'])
            else:
                P.dma('sp', n + 'gz', gz[:, k, :], g.PM[rows, ocol:ocol + 512], writes=[n + 'gz'])
            P.op('dve', lambda e: e.tensor_tensor(out=yf[:, k, :], in0=yf[:, k, :], in1=yb[:, k, :], op=ALU.add),
                 reads=[n + 'yb'], writes=[n + 'yf'])
            if which == 'b':
                P.op('act', lambda e: e.activation(out=so[:, k, :], in_=go[:, k, :], func=AF.Sigmoid), reads=[n + 'go'],
                     writes=[n + 'so'])
                P.op('dve', lambda e: e.tensor_tensor(out=yf[:, k, :], in0=yf[:, k, :], in1=so[:, k, :], op=ALU.mult),
                     reads=[n + 'so'], writes=[n + 'yf'])
            for h in range(4):
                P.op('dve', lambda e: e.scalar_tensor_tensor(out=junk[:], in0=yf[:, k, h * 128:(h + 1) * 128], scalar=1.0,
                                                              in1=yf[:, k, h * 128:(h + 1) * 128], op0=ALU.mult,
                                                              op1=ALU.mult, accum_out=st[:, k, h:h + 1]),
                     reads=[n + 'yf'], writes=['r_junk', n + 'st'])
            if rpend[0] is not None:
                rpend[0]()
            rpend[0] = rs2
        rpend[0]()
        P.barrier()


NBLK = T // 32


def order_blocks(d):
    if d == 0:
        return list(range(NBLK))
    return list(range(7, -1, -1)) + list(range(NBLK - 1, 7, -1))


def phase_hgrn(P, g, K, l):
    with ExitStack() as es:
        lbl = P.sbuf(es, "a_lbl", [128, DEPTH, 8], F32)
        lb = P.sbuf(es, "a_lb", [128, 3, 8], F32)
        msk = P.sbuf(es, "a_msk", [128, 2, 128], F32)
        ones = P.sbuf(es, "a_ones", [128, T], BF16)
        zq = P.sbuf(es, "a_zq", [128, T], BF16)
        qs = P.sbuf(es, "a_qs", [128, T], BF16)
        zf = P.sbuf(es, "a_zf", [128, T], BF16)
        Bt = P.sbuf(es, "a_B", [128, T], F32)
        Ct = P.sbuf(es, "a_C", [128, T], F32)
        Dt = P.sbuf(es, "a_D", [128, T], F32)
        kk = P.sbuf(es, "a_k", [128, T], BF16)
        qt = P.sbuf(es, "a_qt", [128, 2, T], BF16)
        kt = P.sbuf(es, "a_kt", [128, 2, T], BF16)
        EH = P.sbuf(es, "a_EH", [128, 2, NBLK], F32)
        gin = P.sbuf(es, "a_gin", [128, 2, NBLK], F32)
        V = P.sbuf(es, "a_V", [32, NBLK, 128], BF16)
        R = P.sbuf(es, "a_R", [128, 2, 2, 128], F32)
        Rbf = P.sbuf(es, "a_Rbf", [128, 2, 128], BF16)
        KTM = P.sbuf(es, "a_KTM", [32, 2, 2, 4, 128], BF16)
        Wt = P.sbuf(es, "a_Wt", [32, 2, 2, 4, 32], BF16)
        osb = P.sbuf(es, "a_osb", [32, 2, 2, 4, 128], F32)
        psKT = P.psum(es, "a_psKT", [128, 2, 1024], BF16)
        psU = P.psum(es, "a_psU", [128, 2, 512], F32)
        psS = P.psum(es, "a_psS", [128, 2, 512], F32)
        psO = P.psum(es, "a_psO", [128, 2, 512], F32)
        P.dma('sp', 'a_lbl', lbl[:], g.lbl, writes=['a_lbl'])
        P.dma('sp', 'a_msk', msk[:, 0, :], g.maskf, writes=['a_msk'])
        P.dma('sp', 'a_msk', msk[:, 1, :], g.maskb, writes=['a_msk'])
        P.op('pool', lambda e: e.memset(ones[:], 1.0), writes=['a_ones'])
        if l == 0:
            P.op('dve', lambda e: e.memset(lb[:, 0, :], 0.0), writes=['a_lb'])
        else:
            P.op('dve', lambda e: e.tensor_tensor(out=lb[:, 0, :], in0=lbl[:, 1, :], in1=lbl[:, 0, :], op=ALU.subtract),
                 reads=['a_lbl'], writes=['a_lb'])
            P.op('act', lambda e: e.activation(out=lb[:, 0, :], in_=lb[:, 0, :], func=AF.Sigmoid), writes=['a_lb'])
        P.op('dve', lambda e: e.tensor_scalar(out=lb[:, 1, :], in0=lb[:, 0, :], scalar1=-1.0, scalar2=1.0, op0=ALU.mult,
                                               op1=ALU.add), writes=['a_lb'])
        P.op('dve', lambda e: e.tensor_scalar(out=lb[:, 2, :], in0=lb[:, 1, :], scalar1=-1.0, scalar2=None,
                                               op0=ALU.mult), writes=['a_lb'])
        b3 = lambda t: t.rearrange("p (c s) -> p c s", s=32)
        PMb = g.PM.rearrange("(c s) w -> s c w", s=32)
        for h in range(NA):
            P.dma('sp', 'a_zq', zq[:], g.PT[(C_AQ + h) * 128:(C_AQ + h + 1) * 128, :], writes=['a_zq'])
            P.op('act', lambda e: e.activation(out=qs[:], in_=zq[:], func=AF.Silu), reads=['a_zq'], writes=['a_qs'])
            P.dma('sp', 'a_V', V[:], PMb[:, :, O_AI + h * 128:O_AI + (h + 1) * 128], writes=['a_V'])
            for d in range(2):
                ch = (C_AFF if d == 0 else C_AFB) + h
                col = d * 4 + h
                lbc, om, nom = lb[:, 0, col:col + 1], lb[:, 1, col:col + 1], lb[:, 2, col:col + 1]
                P.dma('sp', 'a_zf', zf[:], g.PT[ch * 128:(ch + 1) * 128, :], writes=['a_zf'])
                P.op('act', lambda e: e.activation(out=Bt[:], in_=zf[:], func=AF.Sigmoid), reads=['a_zf'], writes=['a_B'])
                P.op('dve', lambda e: e.tensor_scalar(out=Ct[:], in0=Bt[:], scalar1=om, scalar2=lbc, op0=ALU.mult,
                                                       op1=ALU.add), reads=['a_B', 'a_lb'], writes=['a_C'])
                P.op('act', lambda e: e.activation(out=Ct[:], in_=Ct[:], func=AF.Ln), writes=['a_C'])
                P.op('dve', lambda e: e.tensor_scalar(out=kk[:], in0=Bt[:], scalar1=nom, scalar2=om, op0=ALU.mult,
                                                       op1=ALU.add), reads=['a_B', 'a_lb'], writes=['a_k'])
                P.op('dve', lambda e: e.tensor_tensor_scan(out=Dt[:], data0=ones[:], data1=Ct[:], initial=0.0,
                                                            op0=ALU.mult, op1=ALU.add),
                     reads=['a_ones', 'a_C'], writes=['a_D'])
                P.op('dve', lambda e: e.tensor_tensor(out=Ct[:], in0=Dt[:], in1=Ct[:], op=ALU.subtract), reads=['a_D'],
                     writes=['a_C'])
                D3, X3, B3 = b3(Dt[:]), b3(Ct[:]), b3(Bt[:])
                if d == 0:
                    P.op('dve', lambda e: e.tensor_tensor(out=B3, in0=D3, in1=D3[:, :, 15:16].broadcast_to([128, NBLK, 32]),
                                                           op=ALU.subtract), reads=['a_D'], writes=['a_B'])
                    P.op('dve', lambda e: e.tensor_copy(out=EH[:, 0, :], in_=B3[:, :, 31]), reads=['a_B'], writes=['a_EH'])
                    P.op('dve', lambda e: e.tensor_tensor(out=EH[:, 1, :], in0=D3[:, :, 15], in1=X3[:, :, 0],
                                                           op=ALU.subtract), reads=['a_D', 'a_C'], writes=['a_EH'])
                    P.op('dve', lambda e: e.tensor_tensor(out=gin[:, d, 1:NBLK], in0=EH[:, 0, 0:NBLK - 1],
                                                           in1=EH[:, 1, 1:NBLK], op=ALU.add), writes=['a_EH', 'a_gin%d' % d])
                    P.op('dve', lambda e: e.memset(gin[:, d, 0:1], 0.0), writes=['a_gin%d' % d])
                else:
                    P.op('dve', lambda e: e.tensor_tensor(out=B3, in0=X3[:, :, 16:17].broadcast_to([128, NBLK, 32]),
                                                           in1=X3, op=ALU.subtract), reads=['a_C'], writes=['a_B'])
                    P.op('dve', lambda e: e.tensor_copy(out=EH[:, 0, :], in_=B3[:, :, 0]), reads=['a_B'], writes=['a_EH'])
                    P.op('dve', lambda e: e.tensor_tensor(out=EH[:, 1, :], in0=D3[:, :, 31], in1=X3[:, :, 16],
                                                           op=ALU.subtract), reads=['a_D', 'a_C'], writes=['a_EH'])
                    P.op('dve', lambda e: e.memset(gin[:, d, 7:8], 0.0), writes=['a_gin%d' % d])
                    P.op('dve', lambda e: e.tensor_tensor(out=gin[:, d, 0:7], in0=EH[:, 0, 1:8], in1=EH[:, 1, 0:7],
                                                           op=ALU.add), writes=['a_EH', 'a_gin%d' % d])
                    P.op('dve', lambda e: e.tensor_tensor(out=gin[:, d, 8:NBLK - 1], in0=EH[:, 0, 9:NBLK],
                                                           in1=EH[:, 1, 8:NBLK - 1], op=ALU.add),
                         writes=['a_EH', 'a_gin%d' % d])
                    P.op('dve', lambda e: e.tensor_tensor(out=gin[:, d, NBLK - 1:NBLK], in0=EH[:, 0, 0:1],
                                                           in1=EH[:, 1, NBLK - 1:NBLK], op=ALU.add),
                         writes=['a_EH', 'a_gin%d' % d])
                P.op('act', lambda e: e.activation(out=gin[:, d, :], in_=gin[:, d, :], func=AF.Exp),
                     writes=['a_gin%d' % d])
                P.op('act', lambda e: e.activation(out=Dt[:], in_=Bt[:], func=AF.Exp), reads=['a_B'], writes=['a_D'])
                P.op('act', lambda e: e.activation(out=Ct[:], in_=Bt[:], func=AF.Exp, scale=-1.0), reads=['a_B'],
                     writes=['a_C'])
                P.op('dve', lambda e: e.tensor_tensor(out=qt[:, d, :], in0=qs[:], in1=Dt[:], op=ALU.mult),
                     reads=['a_qs', 'a_D'], writes=['a_qt%d' % d])
                P.op('dve', lambda e: e.tensor_tensor(out=kt[:, d, :], in0=kk[:], in1=Ct[:], op=ALU.mult),
                     reads=['a_k', 'a_C'], writes=['a_kt%d' % d])
            P.op('pool', lambda e: e.memset(R[:], 0.0), writes=['a_R0_0', 'a_R0_1', 'a_R1_0', 'a_R1_1'])
            orders = [order_blocks(0), order_blocks(1)]
            rK = [Rot("a_KTM0_", 2), Rot("a_KTM1_", 2)]
            rW = [Rot("a_Wt0_", 2), Rot("a_Wt1_", 2)]
            ro = [Rot("a_osb0_", 2), Rot("a_osb1_", 2)]
            cur = [None, None]
            curo = [None, None]
            apend = [None]
            for i in range(NBLK):
                pp = i % 2
                for d in range(2):
                    c = orders[d][i]
                    tile_, j = c // 4, c % 4
                    bs = slice(c * 32, (c + 1) * 32)
                    nqt, nkt = 'a_qt%d' % d, 'a_kt%d' % d
                    if cur[d] is None or cur[d][0] != tile_:
                        kK, nK = rK[d].next()
                        for jj in range(4):
                            cs = slice(tile_ * 128 + jj * 32, tile_ * 128 + (jj + 1) * 32)
                            P.op('pe', lambda e: e.transpose(out=psKT[0:32, d, jj * 128:(jj + 1) * 128], in_=kt[:, d, cs],
                                                             identity=K.identb[:]), reads=[nkt, 'identb'],
                                 writes=['a_psKT%d' % d])
                        P.op('act', lambda e: e.activation(out=KTM[:, d, kK].rearrange("p j c -> p (j c)"),
                                                           in_=psKT[0:32, d, 0:512], func=AF.Copy),
                             writes=['a_psKT%d' % d, nK])
                        for jj in range(4):
                            cs = slice(tile_ * 128 + jj * 32, tile_ * 128 + (jj + 1) * 32)
                            P.op('pe', lambda e: e.matmul(psS[0:32, d, jj * 32:(jj + 1) * 32], lhsT=kt[:, d, cs],
                                                          rhs=qt[:, d, cs], start=True, stop=True),
                                 reads=[nqt, nkt], writes=['a_psS%d' % d])
                        kW, nW = rW[d].next()
                        P.op('dve', lambda e: e.tensor_tensor(
                            out=Wt[:, d, kW], in0=psS[0:32, d, 0:128].rearrange("p (j t) -> p j t", t=32),
                            in1=msk[0:32, d, 0:32].unsqueeze(1).broadcast_to([32, 4, 32]), op=ALU.mult),
                            reads=['a_msk'], writes=['a_psS%d' % d, nW])
                        ko, no = ro[d].next()
                        cur[d] = (tile_, kK, nK, kW, nW)
                        curo[d] = (ko, no, 0)
                    _, kK, nK, kW, nW = cur[d]
                    ko, no, cnt = curo[d]
                    P.op('pe', lambda e: e.matmul(psU[:, d, 0:128], lhsT=KTM[:, d, kK, j, :], rhs=V[:, c, :], start=True,
                                                  stop=True), reads=[nK, 'a_V'], writes=['a_psU%d' % d])
                    P.op('act', lambda e: e.activation(out=Rbf[:, d, :], in_=R[:, d, pp, :], func=AF.Copy,
                                                       scale=gin[:, d, c:c + 1]),
                         reads=['a_R%d_%d' % (d, pp), 'a_gin%d' % d], writes=['a_Rbf%d' % d])
                    cnt += 1
                    curo[d] = (ko, no, cnt)

                    def st2(d=d, c=c, bs=bs, nqt=nqt, kW=kW, nW=nW, ko=ko, no=no, j=j, cnt=cnt, tile_=tile_, h=h, pp=pp):
                        P.op('pe', lambda e: e.matmul(psO[0:32, d, j * 128:(j + 1) * 128], lhsT=qt[:, d, bs], rhs=Rbf[:, d, :],
                                                      start=True, stop=False),
                             reads=[nqt, 'a_Rbf%d' % d], writes=['a_psO%d' % d])
                        P.op('pe', lambda e: e.matmul(psO[0:32, d, j * 128:(j + 1) * 128], lhsT=Wt[:, d, kW, j, :],
                                                      rhs=V[:, c, :], start=False, stop=True),
                             reads=[nW, 'a_V'], writes=['a_psO%d' % d])
                        P.op('dve', lambda e: e.scalar_tensor_tensor(out=R[:, d, 1 - pp, :], in0=R[:, d, pp, :],
                                                                      scalar=gin[:, d, c:c + 1], in1=psU[:, d, 0:128],
                                                                      op0=ALU.mult, op1=ALU.add),
                             reads=['a_gin%d' % d, 'a_R%d_%d' % (d, pp)],
                             writes=['a_psU%d' % d, 'a_R%d_%d' % (d, 1 - pp)])
                        if cnt == 4:
                            P.op('dve', lambda e: e.tensor_copy(out=osb[:, d, ko].rearrange("p j v -> p (j v)"),
                                                                 in_=psO[0:32, d, 0:512]),
                                 writes=['a_psO%d' % d, no])
                            P.dma('sp', no, g.yA[d, tile_ * 128:(tile_ + 1) * 128, h * 128:(h + 1) * 128].rearrange(
                                "(j s) v -> s j v", s=32), osb[:, d, ko], reads=[no], writes=[])
                    if apend[0] is not None:
                        apend[0]()
                    apend[0] = st2
            apend[0]()
            apend[0] = None
        P.barrier()


def phase_outproj(P, g, K, l, src, with_ctx, final):
    with ExitStack() as es:
        wo = P.sbuf(es, "o_wo", [128, 4, KC, 512], BF16)
        gg = P.sbuf(es, "o_gg", [128, 2, D], F32)
        mt = P.sbuf(es, "o_mt", [128, 2, D], BF16)
        mixT = P.sbuf(es, "o_mixT", [128, 2, KC, 128], BF16)
        xt = P.sbuf(es, "o_xt", [128, 2, D], F32)
        tmp = P.sbuf(es, "o_tmp", [128, 2, D], F32)
        ysb = P.sbuf(es, "o_ysb", [128, 2, D], F32)
        junk = P.sbuf(es, "o_junk", [128, 512], BF16)
        st = P.sbuf(es, "o_st", [128, 2, 8], F32)
        psT = P.psum(es, "o_psT", [128, 2, 8, 128], BF16)
        psY = P.psum(es, "o_psY", [128, 4, 512], F32)
        for jc in range(4):
            P.dma('sp', 'o_wo%d' % jc, wo[:, jc].rearrange("p k c -> p (k c)"), g.wo_b[l, jc, :, :], writes=['o_wo%d' % jc])
        P.dma('sp', 'o_gg0', gg[:, 0, :], g.ggd[l, 0], writes=['o_gg0'])
        P.dma('sp', 'o_gg1', gg[:, 1, :], g.ggd[l, 1], writes=['o_gg1'])
        rr = Rot("o", 2)
        opend = [None]
        for tt in range(0 if with_ctx else 2, NT):
            k, n = rr.next()
            jj = 1 if tt < 2 else 0
            rows = slice(tt * 128, (tt + 1) * 128)
            P.dma('sp', n + 'mt', mt[:, k, :], g.mix[rows, :], writes=[n + 'mt'])
            P.dma('sp', n + 'xt', xt[:, k, :], src[rows, :], writes=[n + 'xt'])
            for q2 in range(2):
                for i in range(8):
                    kc = q2 * 8 + i
                    P.op('pe', lambda e: e.transpose(out=psT[:, q2, i, :], in_=mt[:, k, kc * 128:(kc + 1) * 128],
                                                     identity=K.identb[:]), reads=[n + 'mt', 'identb'],
                         writes=['o_psT%d' % q2])
                if q2 == 0:
                    P.op('dve', lambda e: e.tensor_copy(out=mixT[:, k, 0:8, :], in_=psT[:, 0]),
                         writes=['o_psT0', n + 'mixTa'])
                else:
                    P.op('act', lambda e: e.activation(out=mixT[:, k, 8:16, :], in_=psT[:, 1], func=AF.Copy),
                         writes=['o_psT1', n + 'mixTb'])
            def oB(k=k, n=n, jj=jj, rows=rows, tt=tt):
                for jc in range(4):
                    for kc in range(KC):
                        P.op('pe', lambda e: e.matmul(psY[:, jc, :], lhsT=mixT[:, k, kc, :], rhs=wo[:, jc, kc, :],
                                                      start=(kc == 0), stop=(kc == KC - 1)),
                             reads=[n + 'mixTa', n + 'mixTb', 'o_wo%d' % jc], writes=['o_psY%d' % jc])
                for jc in range(4):
                    P.op('act', lambda e: e.activation(out=junk[:], in_=psY[:, jc, :], func=AF.Square,
                                                       accum_out=st[:, k, jc:jc + 1]),
                         writes=['o_psY%d' % jc, 'o_junk', n + 'st'])
                    P.op('dve', lambda e: e.tensor_copy(out=ysb[:, k, jc * 512:(jc + 1) * 512], in_=psY[:, jc, :]),
                         writes=['o_psY%d' % jc, n + 'ysb%d' % jc])
                P.op('dve', lambda e: e.tensor_reduce(out=st[:, k, 4:5], in_=st[:, k, 0:4], axis=AX.X, op=ALU.add),
                     writes=[n + 'st'])
                P.op('dve', lambda e: e.tensor_scalar(out=st[:, k, 5:6], in0=st[:, k, 4:5], scalar1=1.0 / D, scalar2=EPS,
                                                       op0=ALU.mult, op1=ALU.add), writes=[n + 'st'])
                P.op('act', lambda e: e.activation(out=st[:, k, 6:7], in_=st[:, k, 5:6], func=AF.Sqrt), writes=[n + 'st'])
                P.op('dve', lambda e: e.reciprocal(out=st[:, k, 7:8], in_=st[:, k, 6:7]), writes=[n + 'st'])
                for jc in range(4):
                    cs = slice(jc * 512, (jc + 1) * 512)
                    P.op('dve', lambda e: e.scalar_tensor_tensor(out=tmp[:, k, cs], in0=ysb[:, k, cs], scalar=st[:, k, 7:8],
                                                                  in1=gg[:, jj, cs], op0=ALU.mult, op1=ALU.mult),
                         reads=[n + 'st', 'o_gg%d' % jj, n + 'ysb%d' % jc], writes=[n + 'tmp'])
                P.op('pool', lambda e: e.tensor_tensor(out=tmp[:, k, :], in0=tmp[:, k, :], in1=xt[:, k, :], op=ALU.add),
                     reads=[n + 'xt'], writes=[n + 'tmp'])
                if final:
                    dst = g.out[(tt - 2) * 128:(tt - 1) * 128, :]
                else:
                    dst = g.xs[rows, :]
                P.dma('pool', n + 'tmp', dst, tmp[:, k, :], reads=[n + 'tmp'], writes=[])
            if opend[0] is not None:
                opend[0]()
            opend[0] = oB
        opend[0]()
        P.barrier()


def build(dbg=False, phases=None):
    nc = bass.Bass("TRN2", target_bir_lowering=False)
    g = declare(nc, dbg)
    if phases is None:
        phases = ("w", "m", "i0", "a0", "b0", "c0", "o0", "i1", "a1", "b1", "c1", "o1")
    with ExitStack() as es:
        P = Prog(nc, es)
        K = setup_consts(P, g, es)
        if "w" in phases:
            phase_wcast(P, g, layers=(0,) if "c0" in phases else (0, 1))
        if "m" in phases:
            phase_mod(P, g, K)
        for l in range(DEPTH):
            wc = (l < DEPTH - 1)
            src = g.xin if l == 0 else g.xs
            if "i%d" % l in phases:
                phase_inproj(P, g, K, l, src)
            if "a%d" % l in phases:
                phase_hgrn(P, g, K, l)
                phase_readout(P, g, K, 'a', wc)
            if "b%d" % l in phases:
                phase_mlstm(P, g, K, l)
                phase_readout(P, g, K, 'b', wc)
            if "c%d" % l in phases:
                phase_attn(P, g, K, l, wc, cast_layers=(1,) if (l == 0 and "w" in phases) else ())
            if "o%d" % l in phases:
                phase_outproj(P, g, K, l, src, wc, l == DEPTH - 1)
        P.finish()
        print("instructions", P.ninst, "waits", P.nwait, P.cnt, "dma sems", len(P.dsem), flush=True)
    return nc


_NC_CACHE = {}


def kernel(**inputs):
    inputs = {k: np.asarray(v) for k, v in inputs.items()}
    per = host_prep(**inputs)
    if 'nc' not in _NC_CACHE:
        _NC_CACHE['nc'] = build()
    nc = _NC_CACHE['nc']
    n = len(per)
    res = run_bass_kernel_spmd(nc, per, core_ids=list(range(n)))
    out = np.stack([np.asarray(res.results[i]['out']) for i in range(n)], axis=0)
    return out.astype(np.float32, copy=False)
```
